# Optimizing a Trainium2 kernel written in Bass

```python
import math
import jax, jax.numpy as jnp
from jax import lax
import numpy as np

D_MODEL = 1024
BATCH = 8
SEQ = 4096
DEPTH = 2

CTX_LEN = 256
GRID_W = 64
D_MIX = D_MODEL
W_BR = D_MIX // 4
EPS = 1e-6

RW_HEAD = 64
RW_HEADS = W_BR // RW_HEAD
RW_DECAY_RANK = 64
RW_A_RANK = 64
RW_SHIFT = 3 * W_BR + 2 * RW_DECAY_RANK + 2 * RW_A_RANK
RW_COLS = RW_SHIFT + W_BR
RW_GN_EPS = 64e-5

S5_CH = 16
S5_GROUPS = W_BR // S5_CH
S5_P = 64
S5_COLS = 2 * W_BR

SSD_HEADDIM = 64
SSD_HEADS = W_BR // SSD_HEADDIM
SSD_NGROUPS = 2
SSD_N = 64
SSD_CONV = 5
SSD_CHUNK = 128
SSD_XBC = W_BR + 2 * SSD_NGROUPS * SSD_N
SSD_COLS = SSD_XBC + 2 * SSD_HEADS + W_BR

GLA_HEADS = 4
GLA_DK = (W_BR // 2) // GLA_HEADS
GLA_DV = W_BR // GLA_HEADS
GLA_RANK = 16
GLA_TAU = 16.0
GLA_CHUNK = 64
GLA_COLS = 2 * GLA_HEADS * GLA_DK + W_BR + 2 * GLA_RANK + W_BR

N_IN = RW_COLS + S5_COLS + SSD_COLS + GLA_COLS

kernel_name = 'hybrid_parallel_heads_rwkv7_s5_ssd_gla_dit'


def split_last(t, sizes):
    idx = np.cumsum(sizes)[:-1].tolist()
    return jnp.split(t, idx, axis=-1)


def flip(t):
    return jnp.flip(t, axis=1)


def rmsnorm(t, g):
    tf = t.astype(jnp.float32)
    tf = tf * lax.rsqrt(jnp.mean(tf * tf, axis=-1, keepdims=True) + EPS)
    return (tf * g.astype(jnp.float32)).astype(t.dtype)


def token_shift(f, grid):
    b, L, C = f.shape
    if grid:
        rows = L // GRID_W
        q = C // 4
        g = f.reshape(b, rows, GRID_W, C)
        left = jnp.pad(g[:, :, :-1, :q], ((0, 0), (0, 0), (1, 0), (0, 0)))
        right = jnp.pad(g[:, :, 1:, q:2 * q], ((0, 0), (0, 0), (0, 1), (0, 0)))
        up = jnp.pad(g[:, :-1, :, 2 * q:3 * q], ((0, 0), (1, 0), (0, 0), (0, 0)))
        down = jnp.pad(g[:, 1:, :, 3 * q:], ((0, 0), (0, 1), (0, 0), (0, 0)))
        return jnp.concatenate([left, right, up, down], axis=-1).reshape(b, L, C)
    half = C // 2
    prev = jnp.pad(f[:, :-1, :half], ((0, 0), (1, 0), (0, 0)))
    nxt = jnp.pad(f[:, 1:, half:], ((0, 0), (0, 1), (0, 0)))
    return jnp.concatenate([prev, nxt], axis=-1)


def dwconv_centred(t, w, bias):
    kw = w.shape[0]
    y = lax.conv_general_dilated(t, w.astype(t.dtype)[:, None, :], window_strides=(1,),
                                 padding=[(kw // 2, kw // 2)], dimension_numbers=('NWC', 'WIO', 'NWC'),
                                 feature_group_count=t.shape[-1])
    return y + bias


def rwkv7_scan(r, w, k, v, kk, a, s0, reverse, need_out):
    def step(S, inp):
        r_t, w_t, k_t, v_t, kk_t, a_t = inp
        sa = jnp.einsum('bhvk,bhk->bhv', S, -kk_t)
        S = S * w_t[:, :, None, :] + sa[..., None] * (kk_t * a_t)[:, :, None, :] + v_t[..., None] * k_t[:, :, None, :]
        y = jnp.einsum('bhvk,bhk->bhv', S, r_t) if need_out else None
        return S, y
    xs = tuple(jnp.moveaxis(t, 1, 0) for t in (r, w, k, v, kk, a))
    s_fin, ys = lax.scan(step, s0, xs, reverse=reverse)
    return (jnp.moveaxis(ys, 0, 1) if need_out else None), s_fin


def rwkv7_mixer(f, p, grid, init, need_out):
    b, L, _ = f.shape
    f = f.astype(jnp.float32)
    z, gate = f[..., :RW_SHIFT], f[..., RW_SHIFT:]
    z = z + p['rw_mu'] * (token_shift(z, grid) - z)
    r, k, v, wl, al = split_last(z, [W_BR, W_BR, W_BR, 2 * RW_DECAY_RANK, 2 * RW_A_RANK])
    heads = lambda t: t.reshape(b, L, RW_HEADS, RW_HEAD)
    r, k, v = heads(r), heads(k), heads(v)
    wl = wl.reshape(b, L, 2, RW_DECAY_RANK)
    al = al.reshape(b, L, 2, RW_A_RANK)
    w_pre = p['rw_w0'] + jnp.einsum('bldr,drc->bldc', jnp.tanh(wl), p['rw_w2'])
    decay = jnp.exp(-jnp.exp(-jax.nn.softplus(-w_pre) - 0.5))
    a = jax.nn.sigmoid(p['rw_a0'] + jnp.einsum('bldr,drc->bldc', al, p['rw_a2']))
    kk = k * p['rw_kk'].reshape(RW_HEADS, RW_HEAD)
    kk = kk * lax.rsqrt(jnp.sum(kk * kk, axis=-1, keepdims=True) + EPS)
    if init is None:
        zero = jnp.zeros((b, RW_HEADS, RW_HEAD, RW_HEAD), jnp.float32)
        init = (zero, zero)
    ys, bonus, finals = [], [], []
    for d, rev in enumerate((False, True)):
        w_d = heads(decay[:, :, d])
        a_d = heads(a[:, :, d])
        k_d = k * (1.0 + (a_d - 1.0) * p['rw_ka'].reshape(RW_HEADS, RW_HEAD))
        y_d, s_d = rwkv7_scan(r, w_d, k_d, v, kk, a_d, init[d], rev, need_out)
        finals.append(s_d)
        if need_out:
            ys.append(y_d)
            bonus.append(jnp.sum(r * k_d * p['rw_rk'], axis=-1, keepdims=True) * v)
    if not need_out:
        return None, (finals[0], finals[1])
    y = ys[0] + ys[1]
    mu = jnp.mean(y, axis=-1, keepdims=True)
    var = jnp.mean(jnp.square(y - mu), axis=-1, keepdims=True)
    y = ((y - mu) * lax.rsqrt(var + RW_GN_EPS)).reshape(b, L, W_BR) * p['rw_ln_w'] + p['rw_ln_b']
    y = y + (bonus[0] + bonus[1]).reshape(b, L, W_BR)
    return y * jax.nn.silu(gate), (finals[0], finals[1])


def s5_combine(e1, e2):
    a1, b1 = e1
    a2, b2 = e2
    return a1 * a2, a2 * b1 + b2


def s5_mixer(f, p, init, need_out):
    b, L, _ = f.shape
    u, gate = split_last(f.astype(jnp.float32), [W_BR, W_BR])
    ug = u.reshape(b, L, S5_GROUPS, S5_CH).astype(jnp.complex64)
    if init is None:
        zero = jnp.zeros((b, S5_GROUPS, S5_P), jnp.complex64)
        init = (zero, zero)
    f32 = jnp.float32
    outs, finals = [], []
    for d, rev in enumerate((False, True)):
        lam = lax.complex(p['s5_a_re'][d].astype(f32), p['s5_a_im'][d].astype(f32))
        dt = jnp.exp(p['s5_log_dt'][d].astype(f32))[:, None]
        lam_bar = jnp.exp(lam * dt)
        b_bar = ((lam_bar - 1.0) / lam)[..., None] * lax.complex(p['s5_b_re'][d].astype(f32), p['s5_b_im'][d].astype(f32))
        bu = jnp.einsum('gpc,blgc->blgp', b_bar, ug)
        edge = L - 1 if rev else 0
        bu = bu.at[:, edge].add(lam_bar * init[d])
        _, xs = lax.associative_scan(s5_combine, (jnp.broadcast_to(lam_bar, bu.shape), bu), reverse=rev, axis=1)
        finals.append(xs[:, 0] if rev else xs[:, L - 1])
        if need_out:
            c_c = lax.complex(p['s5_c_re'][d].astype(f32), p['s5_c_im'][d].astype(f32))
            outs.append(jnp.real(jnp.einsum('gcp,blgp->blgc', c_c, xs)))
    if not need_out:
        return None, (finals[0], finals[1])
    y = (outs[0] + outs[1]).reshape(b, L, W_BR) + p['s5_d'] * u
    y = jax.nn.gelu(y)
    y = y * jax.nn.sigmoid(y @ p['s5_glu_w'] + p['s5_glu_b'])
    return y * jax.nn.silu(gate), (finals[0], finals[1])


def chunk_state_pass(decay, contrib, s0):
    def step(S, inp):
        dec, ctb = inp
        return dec * S + ctb, S
    s_fin, prev = lax.scan(step, s0, (jnp.moveaxis(decay, 1, 0), jnp.moveaxis(contrib, 1, 0)))
    return jnp.moveaxis(prev, 0, 1), s_fin


def ssd_chunked(q, k, v, log_a, s0, need_out):
    b, L, H, _ = q.shape
    nc = L // SSD_CHUNK
    ch = lambda t: t.reshape((b, nc, SSD_CHUNK) + t.shape[2:])
    q, k, v, log_a = ch(q), ch(k), ch(v), ch(log_a)
    cs = jnp.cumsum(log_a, axis=2)
    cs_last = cs[:, :, -1:]
    contrib = jnp.einsum('bcqhn,bcqhp->bchnp', k * jnp.exp(cs_last - cs)[..., None], v)
    prev, s_fin = chunk_state_pass(jnp.exp(cs_last[:, :, 0])[..., None, None], contrib, s0)
    if not need_out:
        return None, s_fin
    cs_h = jnp.moveaxis(cs, 2, 3)
    within = jnp.tril(jnp.ones((SSD_CHUNK, SSD_CHUNK), bool))
    seg = jnp.exp(jnp.where(within, cs_h[..., :, None] - cs_h[..., None, :], -jnp.inf))
    scores = jnp.einsum('bcihn,bcjhn->bchij', q, k) * seg
    y = jnp.einsum('bchij,bcjhp->bcihp', scores, v) + jnp.einsum('bcihn,bchnp->bcihp', q * jnp.exp(cs)[..., None], prev)
    return y.reshape(b, L, H, -1), s_fin


def gla_chunked(q, k, v, log_a, s0, need_out):
    b, L, H, _ = q.shape
    nc = L // GLA_CHUNK
    ch = lambda t: t.reshape((b, nc, GLA_CHUNK) + t.shape[2:])
    q, k, v, log_a = ch(q), ch(k), ch(v), ch(log_a)
    bc = jnp.cumsum(log_a, axis=2)
    b_last = bc[:, :, -1:]
    contrib = jnp.einsum('bcqhk,bcqhv->bchkv', k * jnp.exp(b_last - bc), v)
    prev, s_fin = chunk_state_pass(jnp.exp(b_last[:, :, 0])[..., None], contrib, s0)
    if not need_out:
        return None, s_fin
    b_mid = bc[:, :, GLA_CHUNK // 2:GLA_CHUNK // 2 + 1]
    scores = jnp.einsum('bcihk,bcjhk->bchij', q * jnp.exp(bc - b_mid), k * jnp.exp(b_mid - bc))
    within = jnp.tril(jnp.ones((GLA_CHUNK, GLA_CHUNK), bool))
    scores = jnp.where(within, scores, 0.0)
    y = jnp.einsum('bchij,bcjhv->bcihv', scores, v) + jnp.einsum('bcihk,bchkv->bcihv', q * jnp.exp(bc), prev)
    return y.reshape(b, L, H, -1), s_fin


def ssd_mixer(f, p, init, need_out):
    b, L, _ = f.shape
    xbc, dt_raw, z = split_last(f.astype(jnp.float32), [SSD_XBC, 2 * SSD_HEADS, W_BR])
    xbc = jax.nn.silu(dwconv_centred(xbc, p['ssd_conv_w'], p['ssd_conv_b']))
    xs, bm, cm = split_last(xbc, [W_BR, SSD_NGROUPS * SSD_N, SSD_NGROUPS * SSD_N])
    xh = xs.reshape(b, L, SSD_HEADS, SSD_HEADDIM)
    rep = SSD_HEADS // SSD_NGROUPS
    bh = jnp.repeat(bm.reshape(b, L, SSD_NGROUPS, SSD_N), rep, axis=2)
    chh = jnp.repeat(cm.reshape(b, L, SSD_NGROUPS, SSD_N), rep, axis=2)
    dt = jax.nn.softplus(dt_raw.reshape(b, L, 2, SSD_HEADS) + p['ssd_dt_bias'])
    a_neg = -jnp.exp(p['ssd_a_log'].astype(jnp.float32))
    if init is None:
        zero = jnp.zeros((b, SSD_HEADS, SSD_N, SSD_HEADDIM), jnp.float32)
        init = (zero, zero)
    ys, finals = [], []
    for d, rev in enumerate((False, True)):
        dt_d = dt[:, :, d]
        args = (chh, bh, xh * dt_d[..., None], dt_d * a_neg[d])
        if rev:
            args = tuple(flip(t) for t in args)
        y_d, s_d = ssd_chunked(args[0], args[1], args[2], args[3], init[d], need_out)
        finals.append(s_d)
        if need_out:
            ys.append(flip(y_d) if rev else y_d)
    if not need_out:
        return None, (finals[0], finals[1])
    y = ys[0] + ys[1] + p['ssd_d'][:, None] * xh
    y = rmsnorm(y.reshape(b, L, W_BR) * jax.nn.silu(z), p['ssd_norm'])
    return y, (finals[0], finals[1])


def gla_mixer(f, p, init, need_out):
    b, L, _ = f.shape
    q, k, v, gl, gate = split_last(f.astype(jnp.float32), [GLA_HEADS * GLA_DK, GLA_HEADS * GLA_DK, W_BR, 2 * GLA_RANK, W_BR])
    q = q.reshape(b, L, GLA_HEADS, GLA_DK) * GLA_DK ** -0.5
    k = k.reshape(b, L, GLA_HEADS, GLA_DK)
    v = v.reshape(b, L, GLA_HEADS, GLA_DV)
    log_a = jax.nn.log_sigmoid(jnp.einsum('bldr,drc->bldc', gl.reshape(b, L, 2, GLA_RANK), p['gla_g2']) + p['gla_gb']) / GLA_TAU
    if init is None:
        zero = jnp.zeros((b, GLA_HEADS, GLA_DK, GLA_DV), jnp.float32)
        init = (zero, zero)
    ys, finals = [], []
    for d, rev in enumerate((False, True)):
        la = log_a[:, :, d].reshape(b, L, GLA_HEADS, GLA_DK)
        args = (q, k, v, la)
        if rev:
            args = tuple(flip(t) for t in args)
        y_d, s_d = gla_chunked(args[0], args[1], args[2], args[3], init[d], need_out)
        finals.append(s_d)
        if need_out:
            ys.append(flip(y_d) if rev else y_d)
    if not need_out:
        return None, (finals[0], finals[1])
    y = ys[0] + ys[1]
    y = y * lax.rsqrt(jnp.mean(y * y, axis=-1, keepdims=True) + EPS)
    y = y.reshape(b, L, W_BR) * p['gla_norm']
    return y * jax.nn.silu(gate), (finals[0], finals[1])


def mixer_layer(h, mod, p, grid, init, need_out):
    shift, scale, gate = jnp.split(mod, 3, axis=-1)
    hn = rmsnorm(h, p['norm_pre']) * (1.0 + scale) + shift
    proj = hn @ p['w_in']
    f_rw, f_s5, f_ssd, f_gla = split_last(proj, [RW_COLS, S5_COLS, SSD_COLS, GLA_COLS])
    ini = init if init is not None else (None, None, None, None)
    y_rw, st_rw = rwkv7_mixer(f_rw, p, grid, ini[0], need_out)
    y_s5, st_s5 = s5_mixer(f_s5, p, ini[1], need_out)
    y_ssd, st_ssd = ssd_mixer(f_ssd, p, ini[2], need_out)
    y_gla, st_gla = gla_mixer(f_gla, p, ini[3], need_out)
    states = (st_rw, st_s5, st_ssd, st_gla)
    if not need_out:
        return None, states
    y = jnp.concatenate([y_rw, y_s5, y_ssd, y_gla], axis=-1).astype(h.dtype) @ p['w_out']
    return h + gate * rmsnorm(y, p['norm_post']), states


def setup_inputs(seed: int = 0) -> dict:
    key = jax.random.key(seed)
    ks = iter(jax.random.split(key, 48))
    f32 = jnp.float32

    def nrm(shape, s):
        return jax.random.normal(next(ks), shape, f32) * s

    def unif(shape, lo, hi):
        return jax.random.uniform(next(ks), shape, f32, lo, hi)

    Ld = DEPTH
    x = nrm((BATCH, SEQ, D_MODEL), 1.0)
    c = nrm((BATCH, D_MODEL), 1.0)
    ctx = nrm((BATCH, CTX_LEN, D_MODEL), 1.0)
    c_ctx = nrm((D_MODEL,), 1.0)
    ada_w = nrm((Ld, D_MODEL, 3 * D_MODEL), 0.5 * D_MODEL ** -0.5)
    ada_b = nrm((Ld, 3 * D_MODEL), 0.01)
    norm_pre = 1.0 + nrm((Ld, D_MODEL), 0.05)
    norm_post = 1.0 + nrm((Ld, D_MODEL), 0.05)
    w_in = nrm((Ld, D_MODEL, N_IN), D_MODEL ** -0.5)
    w_out = nrm((Ld, D_MIX, D_MODEL), D_MIX ** -0.5)
    rw_mu = unif((Ld, RW_SHIFT), 0.0, 1.0)
    rw_w0 = unif((Ld, 2, W_BR), -6.0, -1.0)
    rw_w2 = nrm((Ld, 2, RW_DECAY_RANK, W_BR), 0.5 * RW_DECAY_RANK ** -0.5)
    rw_a0 = nrm((Ld, 2, W_BR), 0.1)
    rw_a2 = nrm((Ld, 2, RW_A_RANK, W_BR), 0.5 * RW_A_RANK ** -0.5)
    rw_kk = 0.85 + nrm((Ld, W_BR), 0.05)
    rw_ka = 1.0 + nrm((Ld, W_BR), 0.05)
    rw_rk = nrm((Ld, RW_HEADS, RW_HEAD), 0.1)
    rw_ln_w = 1.0 + nrm((Ld, W_BR), 0.05)
    rw_ln_b = nrm((Ld, W_BR), 0.01)
    s5_a_re = -0.5 + nrm((Ld, 2, S5_GROUPS, S5_P), 0.01)
    s5_a_im = math.pi * jnp.arange(S5_P, dtype=f32) + nrm((Ld, 2, S5_GROUPS, S5_P), 0.01)
    s5_log_dt = unif((Ld, 2, S5_GROUPS), math.log(1e-3), math.log(1e-1))
    s5_b_re = nrm((Ld, 2, S5_GROUPS, S5_P, S5_CH), (2 * S5_CH) ** -0.5)
    s5_b_im = nrm((Ld, 2, S5_GROUPS, S5_P, S5_CH), (2 * S5_CH) ** -0.5)
    s5_c_re = nrm((Ld, 2, S5_GROUPS, S5_CH, S5_P), (2 * S5_P) ** -0.5)
    s5_c_im = nrm((Ld, 2, S5_GROUPS, S5_CH, S5_P), (2 * S5_P) ** -0.5)
    s5_d = nrm((Ld, W_BR), 1.0)
    s5_glu_w = nrm((Ld, W_BR, W_BR), W_BR ** -0.5)
    s5_glu_b = nrm((Ld, W_BR), 0.01)
    ssd_conv_w = nrm((Ld, SSD_CONV, SSD_XBC), SSD_CONV ** -0.5)
    ssd_conv_b = nrm((Ld, SSD_XBC), 0.01)
    dt0 = jnp.exp(unif((Ld, 2, SSD_HEADS), math.log(1e-3), math.log(1e-1)))
    ssd_dt_bias = dt0 + jnp.log(-jnp.expm1(-dt0))
    ssd_a_log = jnp.log(unif((Ld, 2, SSD_HEADS), 1.0, 16.0))
    ssd_d = 1.0 + nrm((Ld, SSD_HEADS), 0.05)
    ssd_norm = 1.0 + nrm((Ld, W_BR), 0.05)
    gla_g2 = nrm((Ld, 2, GLA_RANK, GLA_HEADS * GLA_DK), GLA_RANK ** -0.5)
    gla_gb = nrm((Ld, 2, GLA_HEADS * GLA_DK), 0.5)
    gla_norm = 1.0 + nrm((Ld, W_BR), 0.05)
    return {'x': x, 'c': c, 'ctx': ctx, 'c_ctx': c_ctx, 'ada_w': ada_w, 'ada_b': ada_b,
            'norm_pre': norm_pre, 'norm_post': norm_post, 'w_in': w_in, 'w_out': w_out,
            'rw_mu': rw_mu, 'rw_w0': rw_w0, 'rw_w2': rw_w2, 'rw_a0': rw_a0, 'rw_a2': rw_a2,
            'rw_kk': rw_kk, 'rw_ka': rw_ka, 'rw_rk': rw_rk, 'rw_ln_w': rw_ln_w, 'rw_ln_b': rw_ln_b,
            's5_a_re': s5_a_re, 's5_a_im': s5_a_im, 's5_log_dt': s5_log_dt, 's5_b_re': s5_b_re,
            's5_b_im': s5_b_im, 's5_c_re': s5_c_re, 's5_c_im': s5_c_im, 's5_d': s5_d,
            's5_glu_w': s5_glu_w, 's5_glu_b': s5_glu_b, 'ssd_conv_w': ssd_conv_w, 'ssd_conv_b': ssd_conv_b,
            'ssd_dt_bias': ssd_dt_bias, 'ssd_a_log': ssd_a_log, 'ssd_d': ssd_d, 'ssd_norm': ssd_norm,
            'gla_g2': gla_g2, 'gla_gb': gla_gb, 'gla_norm': gla_norm}


def reference(x, c, ctx, c_ctx, ada_w, ada_b, norm_pre, norm_post, w_in, w_out,
              rw_mu, rw_w0, rw_w2, rw_a0, rw_a2, rw_kk, rw_ka, rw_rk, rw_ln_w, rw_ln_b,
              s5_a_re, s5_a_im, s5_log_dt, s5_b_re, s5_b_im, s5_c_re, s5_c_im, s5_d, s5_glu_w, s5_glu_b,
              ssd_conv_w, ssd_conv_b, ssd_dt_bias, ssd_a_log, ssd_d, ssd_norm,
              gla_g2, gla_gb, gla_norm):
    h, hc = x, ctx
    silu_c = jax.nn.silu(c)
    silu_cc = jax.nn.silu(c_ctx)
    for l in range(DEPTH):
        p = dict(norm_pre=norm_pre[l], norm_post=norm_post[l], w_in=w_in[l], w_out=w_out[l],
                 rw_mu=rw_mu[l], rw_w0=rw_w0[l], rw_w2=rw_w2[l], rw_a0=rw_a0[l], rw_a2=rw_a2[l],
                 rw_kk=rw_kk[l], rw_ka=rw_ka[l], rw_rk=rw_rk[l], rw_ln_w=rw_ln_w[l], rw_ln_b=rw_ln_b[l],
                 s5_a_re=s5_a_re[l], s5_a_im=s5_a_im[l], s5_log_dt=s5_log_dt[l], s5_b_re=s5_b_re[l],
                 s5_b_im=s5_b_im[l], s5_c_re=s5_c_re[l], s5_c_im=s5_c_im[l], s5_d=s5_d[l],
                 s5_glu_w=s5_glu_w[l], s5_glu_b=s5_glu_b[l], ssd_conv_w=ssd_conv_w[l], ssd_conv_b=ssd_conv_b[l],
                 ssd_dt_bias=ssd_dt_bias[l], ssd_a_log=ssd_a_log[l], ssd_d=ssd_d[l], ssd_norm=ssd_norm[l],
                 gla_g2=gla_g2[l], gla_gb=gla_gb[l], gla_norm=gla_norm[l])
        mod = (silu_c @ ada_w[l] + ada_b[l])[:, None, :]
        mod_c = (silu_cc @ ada_w[l] + ada_b[l])[None, None, :]
        last = l == DEPTH - 1
        hc_next, ctx_states = mixer_layer(hc, mod_c, p, False, None, not last)
        h, _ = mixer_layer(h, mod, p, True, ctx_states, True)
        hc = hc_next
    return h
```

```python
import os
import numpy as np
import concourse.bass as bass
import concourse.mybir as mybir

F32, BF16 = mybir.dt.float32, mybir.dt.bfloat16
AF = mybir.ActivationFunctionType
ALU = mybir.AluOpType
AX = mybir.AxisListType
NDS = 24


class Tok:
    __slots__ = ('w', 'r')

    def __init__(self):
        self.w = None
        self.r = {}


class A:
    __slots__ = ('ap', 'tok')

    def __init__(self, ap, tok):
        self.ap = ap
        self.tok = tok


class Buf:
    def __init__(self, t):
        self.t = t
        self.tok = Tok()
        self.sub = {}

    def __getitem__(self, idx):
        return A(self.t[idx], self.tok)

    def at(self, key, idx):
        return A(self.t[idx], self.sub.setdefault(key, Tok()))

    def wrap(self, ap, key=None):
        return A(ap, self.tok if key is None else self.sub.setdefault(key, Tok()))


def _ap(x):
    return x.ap if isinstance(x, A) else x


class KB:
    def __init__(self, nc, es):
        self.nc, self.es = nc, es
        self.E = {'pe': nc.tensor, 'act': nc.scalar, 'dve': nc.vector, 'pool': nc.gpsimd, 'sp': nc.sync}
        self.sems, self.cnt = {}, {}
        self.seen = {e: {} for e in self.E}
        for e in self.E:
            self._mksem(e)
        self.dq = {'hw': [], 'sw': []}
        for i in range(NDS):
            self._mksem('d%d' % i)
            self.dq['hw'].append('d%d' % i)
        for i in range(8):
            self._mksem('q%d' % i)
            self.dq['sw'].append('q%d' % i)
        self.dq_i = {'hw': 0, 'sw': 0}
        self.nid = 0
        self.ninst = 0

    def _mksem(self, key):
        self.sems[key] = self.es.enter_context(self.nc.semaphore('s_' + key))
        self.cnt[key] = 0

    def sb(self, shape, dt=F32, name=None):
        self.nid += 1
        t = self.es.enter_context(self.nc.sbuf_tensor('%s_%d' % (name or 'sb', self.nid), list(shape), dt))
        return Buf(t)

    def ps(self, shape, dt=F32, name=None):
        self.nid += 1
        t = self.es.enter_context(self.nc.psum_tensor('%s_%d' % (name or 'ps', self.nid), list(shape), dt))
        return Buf(t)

    def dram(self, name, shape, dt=F32, kind="Internal"):
        t = self.nc.dram_tensor(name, list(shape), dt, kind=kind)
        b = Buf(t.ap())
        return b

    def _wait(self, eng, key, val):
        if self.seen[eng].get(key, 0) >= val:
            return
        self.E[eng].wait_ge(self.sems[key], val)
        self.seen[eng][key] = val
        self.ninst += 1

    def _deps(self, R, W):
        evs = {}
        for t in R:
            if t.w is not None and evs.get(t.w[0], 0) < t.w[1]:
                evs[t.w[0]] = t.w[1]
        for t in W:
            if t.w is not None and evs.get(t.w[0], 0) < t.w[1]:
                evs[t.w[0]] = t.w[1]
            for k, v in t.r.items():
                if evs.get(k, 0) < v:
                    evs[k] = v
        return evs

    def _toks(self, xs):
        out = []
        for x in xs:
            if isinstance(x, A):
                if x.tok not in out:
                    out.append(x.tok)
            elif isinstance(x, Tok):
                if x not in out:
                    out.append(x)
        return out

    def op(self, eng, fn, R=(), W=()):
        R = self._toks(R)
        W = self._toks(W)
        evs = self._deps(R, W)
        for k, v in evs.items():
            if k == 'pe' and eng == 'pe':
                continue
            self._wait(eng, k, v)
        ins = fn(self.E[eng])
        self.cnt[eng] += 1
        ins.then_inc(self.sems[eng], 1)
        self.ninst += 1
        n = self.cnt[eng]
        for t in R:
            t.r[eng] = n
        for t in W:
            t.w = (eng, n)
            t.r = {}
        return ins

    def dma(self, q, out, in_, **kw):
        R = self._toks([in_])
        W = self._toks([out])
        cls = 'sw' if q == 'pool' else 'hw'
        key = self.dq[cls][self.dq_i[cls]]
        self.dq_i[cls] = (self.dq_i[cls] + 1) % len(self.dq[cls])
        evs = self._deps(R, W)
        if self.cnt[key] > 0:
            evs[key] = max(evs.get(key, 0), self.cnt[key])
        for k, v in evs.items():
            self._wait(q, k, v)
        ins = self.E[q].dma_start(out=_ap(out), in_=_ap(in_), **kw)
        self.cnt[key] += 16
        ins.then_inc(self.sems[key], 16)
        self.ninst += 1
        n = self.cnt[key]
        for t in R:
            t.r[key] = n
        for t in W:
            t.w = (key, n)
            t.r = {}
        return ins

    def finish(self, toks):
        for t in self._toks(toks):
            if t.w is not None:
                self._wait('sp', t.w[0], t.w[1])

    def barrier_all(self):
        for e in self.E:
            for k, v in self.cnt.items():
                if v > 0 and not (k == e):
                    self._wait(e, k, v)

    def tt(self, eng, out, in0, in1, op):
        return self.op(eng, lambda e: e.tensor_tensor(out=_ap(out), in0=_ap(in0), in1=_ap(in1), op=op),
                       R=[in0, in1], W=[out])

    def ts(self, eng, out, in0, s1, op0, s2=None, op1=None, accum_out=None):
        kw = {}
        if op1 is not None:
            kw['op1'] = op1
        if accum_out is not None:
            kw['accum_out'] = _ap(accum_out)
        return self.op(eng, lambda e: e.tensor_scalar(out=_ap(out), in0=_ap(in0), scalar1=_ap(s1), scalar2=_ap(s2),
                                                      op0=op0, **kw),
                       R=[in0, s1, s2], W=[out, accum_out])

    def stt(self, eng, out, in0, scalar, in1, op0, op1):
        return self.op(eng, lambda e: e.scalar_tensor_tensor(out=_ap(out), in0=_ap(in0), scalar=_ap(scalar),
                                                             in1=_ap(in1), op0=op0, op1=op1),
                       R=[in0, scalar, in1], W=[out])

    def act(self, out, in_, func, bias=None, scale=None, accum_out=None, eng='act'):
        kw = {}
        if bias is not None:
            kw['bias'] = _ap(bias)
        if scale is not None:
            kw['scale'] = _ap(scale)
        if accum_out is not None:
            kw['accum_out'] = _ap(accum_out)
        return self.op(eng, lambda e: e.activation(out=_ap(out), in_=_ap(in_), func=func, **kw),
                       R=[in_, bias, scale], W=[out, accum_out])

    def amul(self, out, in_, scale):
        return self.act(out, in_, AF.Copy, scale=scale)

    def mm(self, out, lhsT, rhs, start=True, stop=True):
        return self.op('pe', lambda e: e.matmul(_ap(out), _ap(lhsT), _ap(rhs), start=start, stop=stop),
                       R=[lhsT, rhs], W=[out])

    def tr(self, out, in_, ident):
        return self.op('pe', lambda e: e.transpose(_ap(out), _ap(in_), _ap(ident)),
                       R=[in_, ident], W=[out])

    def cp(self, eng, out, in_):
        if eng == 'act':
            return self.op(eng, lambda e: e.copy(out=_ap(out), in_=_ap(in_)), R=[in_], W=[out])
        return self.op(eng, lambda e: e.tensor_copy(out=_ap(out), in_=_ap(in_)), R=[in_], W=[out])

    def memset(self, eng, out, val):
        return self.op(eng, lambda e: e.memset(_ap(out), val), R=[], W=[out])

    def recip(self, out, in_, eng='dve'):
        return self.op(eng, lambda e: e.reciprocal(out=_ap(out), in_=_ap(in_)), R=[in_], W=[out])

    def reduce(self, eng, out, in_, op, axis=AX.X):
        return self.op(eng, lambda e: e.tensor_reduce(out=_ap(out), in_=_ap(in_), op=op, axis=axis),
                       R=[in_], W=[out])

    def scan(self, out, d0, d1, initial, op0=ALU.mult, op1=ALU.add):
        return self.op('dve', lambda e: e.tensor_tensor_scan(out=_ap(out), data0=_ap(d0), data1=_ap(d1),
                                                              initial=_ap(initial), op0=op0, op1=op1),
                       R=[d0, d1, initial], W=[out])

from contextlib import ExitStack
from concourse.bass_utils import run_bass_kernel_spmd

D = 1024; SEQ = 4096; CTXL = 256; T = SEQ + CTXL; NT = T // 128; NTC = CTXL // 128
NIN = 3368; DEPTH = 2; EPS = 1e-6
FMBLK = [(1280, 128), (1408, 128), (1536, 128), (1664, 128),
         (1792, 128), (1920, 128), (2048, 128), (2176, 128),
         (2568, 128), (2696, 128), (3080, 32)]
FMROW = {}
_r = 0
for _c, _n in FMBLK:
    FMROW[_c] = _r
    _r += _n
NFM = _r

WNAMES = ['ada_w', 'ada_b', 'norm_pre', 'norm_post', 'w_in', 'w_out', 'rw_mu', 'rw_w0', 'rw_w2', 'rw_a0', 'rw_a2',
          'rw_kk', 'rw_ka', 'rw_rk', 'rw_ln_w', 'rw_ln_b', 's5_a_re', 's5_a_im', 's5_log_dt', 's5_b_re', 's5_b_im',
          's5_c_re', 's5_c_im', 's5_d', 's5_glu_w', 's5_glu_b', 'ssd_conv_w', 'ssd_conv_b', 'ssd_dt_bias',
          'ssd_a_log', 'ssd_d', 'ssd_norm', 'gla_g2', 'gla_gb', 'gla_norm']
WSHAPES = {'ada_w': (2, 1024, 3072), 'ada_b': (2, 3072), 'norm_pre': (2, 1024), 'norm_post': (2, 1024),
           'w_in': (2, 1024, 3368), 'w_out': (2, 1024, 1024), 'rw_mu': (2, 1024), 'rw_w0': (2, 2, 256),
           'rw_w2': (2, 2, 64, 256), 'rw_a0': (2, 2, 256), 'rw_a2': (2, 2, 64, 256), 'rw_kk': (2, 256),
           'rw_ka': (2, 256), 'rw_rk': (2, 4, 64), 'rw_ln_w': (2, 256), 'rw_ln_b': (2, 256),
           's5_a_re': (2, 2, 16, 64), 's5_a_im': (2, 2, 16, 64), 's5_log_dt': (2, 2, 16),
           's5_b_re': (2, 2, 16, 64, 16), 's5_b_im': (2, 2, 16, 64, 16), 's5_c_re': (2, 2, 16, 16, 64),
           's5_c_im': (2, 2, 16, 16, 64), 's5_d': (2, 256), 's5_glu_w': (2, 256, 256), 's5_glu_b': (2, 256),
           'ssd_conv_w': (2, 5, 512), 'ssd_conv_b': (2, 512), 'ssd_dt_bias': (2, 2, 4), 'ssd_a_log': (2, 2, 4),
           'ssd_d': (2, 4), 'ssd_norm': (2, 256), 'gla_g2': (2, 2, 16, 128), 'gla_gb': (2, 2, 128),
           'gla_norm': (2, 256)}


def make_consts():
    p = np.arange(128)[:, None]
    f = np.arange(128)[None, :]
    c = {}
    c['ident'] = np.eye(128, dtype=np.float32)
    c['LE'] = (p <= f).astype(np.float32)
    c['GE'] = (p >= f).astype(np.float32)
    c['LT'] = (p < f).astype(np.float32)
    c['GT'] = (p > f).astype(np.float32)
    c['hm4'] = (p // 32 == np.arange(4)[None, :]).astype(np.float32)
    c['hm2'] = (p // 64 == np.arange(2)[None, :]).astype(np.float32)
    c['bm_gla'] = (p // 32 == np.arange(256)[None, :] // 64).astype(np.float32)
    c['ones'] = np.ones((128, 128), np.float32)
    c['bm2'] = (p // 64 == f // 64).astype(np.float32)
    c['mL'] = np.concatenate([(p % 64 != 0), (p % 64 != 63)], axis=1).astype(np.float32)
    c['BMf'] = (f // 16 >= p // 16).astype(np.float32)
    c['BMr'] = (p // 16 >= f // 16).astype(np.float32)
    c['NLE'] = -(p <= f).astype(np.float32)
    c['NGE'] = -(p >= f).astype(np.float32)
    c['MLE'] = np.where(p <= f, 0.0, -30000.0).astype(np.float32)
    c['MGE'] = np.where(p >= f, 0.0, -30000.0).astype(np.float32)
    c['GM0'] = ((f // 64 == 0) * np.ones((128, 1))).astype(np.float32)
    c['GM1'] = ((f // 64 == 1) * np.ones((128, 1))).astype(np.float32)
    off = {}
    cols = []
    o = 0
    for k_, v_ in c.items():
        off[k_] = (o, v_.shape[1])
        o += v_.shape[1]
        cols.append(v_)
    return np.concatenate(cols, axis=1), off


def make_sel():
    p = np.arange(128)[:, None]
    x = np.arange(256)[None, :]
    mats = []
    for g in range(8):
        mats.append(((p // 16 == g) & (x == 128 + (p - 16 * g))).astype(np.float32))
    for l in range(8):
        mats.append(((p // 16 == l) & (x == 128 + (p % 16))).astype(np.float32))
    return np.concatenate(mats, axis=1)


SELC = make_sel()
CONSTS, COFF = make_consts()
NCONST = CONSTS.shape[1]


class Prog:
    def __init__(self, nc, es, dbg=()):
        self.nc, self.es = nc, es
        k = self.k = KB(nc, es)
        self.dbg = set(dbg)
        self.x = k.dram('x', [SEQ, D], kind="ExternalInput")
        self.ctx = k.dram('ctx', [CTXL, D], kind="ExternalInput")
        self.c2 = k.dram('c2', [2, D], kind="ExternalInput")
        self.w = {n: k.dram(n, list(WSHAPES[n]), kind="ExternalInput") for n in WNAMES}
        self.cdram = k.dram('consts', [128, NCONST], kind="ExternalInput")
        self.seldram = k.dram('selc', [128, 16 * 256], kind="ExternalInput")
        self.out = k.dram('out', [SEQ, D], kind="ExternalOutput")
        def scr(name, shape):
            return k.dram(name, shape, kind="ExternalOutput" if name in self.dbg else "Internal")
        self.hs = [scr('h_a', [T, D]), scr('h_b', [T, D])]
        self.projTM = scr('projTM', [T, NIN])
        self.projFM = scr('projFM', [NFM, T])
        self.yfD = k.dram('yfD', [NT, 128, 256], BF16)
        self.csb = k.sb([128, NCONST], F32, 'consts')
        k.dma('sp', self.csb[:, :], self.cdram[:, :])
        self.ident = self.C('ident')
        self.identb = k.sb([128, 128], BF16, 'identb')
        k.cp('dve', self.identb[:, :], self.ident)
        self.ycatD = k.dram('ycatD', [128, 8, T], BF16, kind="ExternalOutput" if 'ycatD' in self.dbg else "Internal")
        self.g1T = k.sb([128, 8, 2], F32, 'g1T')
        self.shT = k.sb([128, 8, 2], F32, 'shT')
        self.Gbc = [k.sb([128, D], F32, 'Gbc') for _ in range(2)]

    def C(self, name, c0=0, c1=None):
        o, n = COFF[name]
        c1 = n if c1 is None else c1
        return self.csb[:, o + c0:o + c1]

    def h_src(self, l, t):
        if l == 0:
            if t < NTC:
                return self.ctx[t * 128:(t + 1) * 128, :]
            return self.x[(t - NTC) * 128:(t - NTC + 1) * 128, :]
        return self.hs[(l - 1) % 2][t * 128:(t + 1) * 128, :]

    def phase_mod(self, l):
        k, nc = self.k, self.nc
        es = ExitStack()
        kk = KBScope(k, es)
        cT = kk.sb([128, 8, 2], F32, 'cT')
        with nc.allow_non_contiguous_dma(reason="small param transposes"):
            for kc in range(8):
                k.dma('sp', cT[:, kc, :], self.c2.wrap(self.c2.t[:, kc * 128:(kc + 1) * 128].rearrange("s p -> p s")))
        scT = kk.sb([128, 8, 2], F32, 'scT')
        k.act(scT[:, :, :], cT[:, :, :], AF.Silu)
        abT = kk.sb([128, 24], F32, 'abT')
        npT = kk.sb([128, 8], F32, 'npT')
        with nc.allow_non_contiguous_dma(reason="small param transposes"):
            k.dma('sp', abT[:, :], self.w['ada_b'].wrap(self.w['ada_b'].t[l, :].rearrange("(j p) -> p j", p=128)))
            k.dma('sp', npT[:, :], self.w['norm_pre'].wrap(self.w['norm_pre'].t[l, :].rearrange("(j p) -> p j", p=128)))
        modT = kk.sb([128, 24, 2], F32, 'modT')
        aw = [kk.sb([128, 8, 128], F32, 'aw') for _ in range(2)]
        pm = [kk.ps([128, 512], F32, 'pm') for _ in range(2)]
        for j in range(24):
            a = aw[j % 2]
            k.dma('sp', a[:, :, :], self.w['ada_w'].wrap(
                self.w['ada_w'].t[l, :, j * 128:(j + 1) * 128].rearrange("(k p) n -> p k n", p=128)))
            p = pm[j % 2]
            for kc in range(8):
                k.mm(p[:, 0:2], a[:, kc, :], scT[:, kc, :], start=(kc == 0), stop=(kc == 7))
            k.ts('dve', modT[:, j, :], p[:, 0:2], abT[:, j:j + 1], ALU.add)
        for s in range(2):
            for kc in range(8):
                k.ts('dve', self.g1T[:, kc, s:s + 1], modT[:, 8 + kc, s:s + 1], 1.0, ALU.add, npT[:, kc:kc + 1], ALU.mult)
        k.cp('dve', self.shT[:, :, :], modT[:, 0:8, :])
        rep = kk.sb([128, 8, 2, 128], F32, 'rep')
        ones = kk.sb([128, 128], F32, 'ones')
        k.memset('dve', ones[:, :], 1.0)
        for kc in range(8):
            for s in range(2):
                k.ts('dve', rep[:, kc, s, :], ones[:, :], scT[:, kc, s:s + 1], ALU.mult)
        awg = kk.sb([128, 8, D], F32, 'awg')
        for kc in range(8):
            k.dma('sp', awg[:, kc, :], self.w['ada_w'][l, kc * 128:(kc + 1) * 128, 2048:3072])
        bbc = kk.sb([128, D], F32, 'bbc')
        npbc = kk.sb([128, D], F32, 'npbc')
        k.dma('sp', bbc[:, :], self.w['ada_b'].wrap(self.w['ada_b'].t[l, 2048:3072].partition_broadcast(128)))
        k.dma('sp', npbc[:, :], self.w['norm_post'].wrap(self.w['norm_post'].t[l, :].partition_broadcast(128)))
        pg = [kk.ps([128, 512], F32, 'pg') for _ in range(2)]
        i = 0
        for s in range(2):
            for nb in range(2):
                p = pg[i % 2]; i += 1
                for kc in range(8):
                    k.mm(p[:, :], rep[:, kc, s, :], awg[:, kc, nb * 512:(nb + 1) * 512], start=(kc == 0), stop=(kc == 7))
                g = self.Gbc[s]
                k.tt('dve', g[:, nb * 512:(nb + 1) * 512], p[:, :], bbc[:, nb * 512:(nb + 1) * 512], ALU.add)
                k.tt('dve', g[:, nb * 512:(nb + 1) * 512], g[:, nb * 512:(nb + 1) * 512], npbc[:, nb * 512:(nb + 1) * 512], ALU.mult)
        k.barrier_all()
        es.close()

    def phase_proj(self, l, tiles=None):
        k, nc = self.k, self.nc
        es = ExitStack()
        kk = KBScope(k, es)
        tiles = list(range(NT)) if tiles is None else tiles
        wbf = kk.sb([128, 8, NIN], BF16, 'wbf')
        half = NIN // 2
        for kc in range(8):
            for hh in range(2):
                k.dma('pool', wbf[:, kc, hh * half:(hh + 1) * half],
                      self.w['w_in'][l, kc * 128:(kc + 1) * 128, hh * half:(hh + 1) * half])
        NB = 2
        hb = [kk.sb([128, D], F32, 'hb') for _ in range(NB)]
        junk = kk.sb([128, D], BF16, 'junk')
        ssq = [kk.sb([128, 1], F32, 'ssq') for _ in range(NB)]
        rstd = [kk.sb([128, 1], F32, 'rstd') for _ in range(NB)]
        hsb = [kk.sb([128, D], BF16, 'hsb') for _ in range(NB)]
        hnT = [kk.sb([128, 8, 512], BF16, 'hnT') for _ in range(2)]
        pjs = [kk.sb([128, NIN], F32, 'pjs') for _ in range(NB)]
        fms = [kk.sb([128, len(FMBLK), 512], F32, 'fms') for _ in range(2)]
        tp = [kk.ps([128, 8, 128], BF16, 'tp') for _ in range(2)]
        pj = [kk.ps([128, 512], F32, 'pj') for _ in range(3)]
        pf = [kk.ps([128, 512], F32, 'pf') for _ in range(2)]
        epsb = kk.sb([128, 1], F32, 'epsb')
        k.memset('dve', epsb[:, :], EPS)
        nblk = [(0, 512), (512, 512), (1024, 256), (2304, 264), (2696, 384), (3112, 256)]
        groups = []
        ct = [t for t in tiles if t < NTC]
        lt = [t for t in tiles if t >= NTC]
        if ct:
            groups.append(ct)
        for i in range(0, len(lt), 4):
            groups.append(lt[i:i + 4])
        ipj = 0; ipf = 0; it = 0
        for gi, grp in enumerate(groups):
            hg = hnT[gi % 2]
            fg = fms[gi % 2]
            for ti, t in enumerate(grp):
                b = it % NB
                s = 1 if t < NTC else 0
                k.dma('sp', hb[b][:, :], self.h_src(l, t))
                k.act(junk[:, :], hb[b][:, :], AF.Square, accum_out=ssq[b][:, :])
                k.act(rstd[b][:, :], ssq[b][:, :], AF.Sqrt, bias=epsb[:, :], scale=1.0 / D)
                k.recip(rstd[b][:, :], rstd[b][:, :])
                k.act(hsb[b][:, :], hb[b][:, :], AF.Copy, scale=rstd[b][:, :])
                tpp = tp[it % 2]
                it += 1
                for kc in range(8):
                    k.tr(tpp[:, kc, :], hsb[b][:, kc * 128:(kc + 1) * 128], self.identb[:, :])
                for kc in range(8):
                    k.ts('dve', hg[:, kc, ti * 128:(ti + 1) * 128], tpp[:, kc, :], self.g1T[:, kc, s:s + 1], ALU.mult,
                         self.shT[:, kc, s:s + 1], ALU.add)
                for bi, (n0, nn) in enumerate(nblk):
                    p = pj[ipj % 3]; ipj += 1
                    for kc in range(8):
                        k.mm(p[:, 0:nn], hg[:, kc, ti * 128:(ti + 1) * 128], wbf[:, kc, n0:n0 + nn], start=(kc == 0), stop=(kc == 7))
                    if bi % 2 == 0:
                        k.cp('act', pjs[b][:, n0:n0 + nn], p[:, 0:nn])
                    else:
                        k.cp('dve', pjs[b][:, n0:n0 + nn], p[:, 0:nn])
                k.dma('pool', self.projTM[t * 128:(t + 1) * 128, 0:1280], pjs[b][:, 0:1280])
                k.dma('pool', self.projTM[t * 128:(t + 1) * 128, 2304:2568], pjs[b][:, 2304:2568])
                k.dma('pool', self.projTM[t * 128:(t + 1) * 128, 2696:3080], pjs[b][:, 2696:3080])
                k.dma('pool', self.projTM[t * 128:(t + 1) * 128, 3112:3368], pjs[b][:, 3112:3368])
            ng = len(grp) * 128
            t0_ = grp[0] * 128
            for fi, (c0, ncol) in enumerate(FMBLK):
                p = pf[ipf % 2]; ipf += 1
                for kc in range(8):
                    k.mm(p[0:ncol, 0:ng], wbf[:, kc, c0:c0 + ncol], hg[:, kc, 0:ng], start=(kc == 0), stop=(kc == 7))
                if fi % 2 == 0:
                    k.cp('act', fg[0:ncol, fi, 0:ng], p[0:ncol, 0:ng])
                else:
                    k.cp('dve', fg[0:ncol, fi, 0:ng], p[0:ncol, 0:ng])
            k.dma('pool', self.projFM.wrap(self.projFM.t[0:1280, t0_:t0_ + ng].rearrange("(f p) n -> p f n", p=128)),
                  fg[:, 0:10, 0:ng])
            k.dma('pool', self.projFM[1280:1312, t0_:t0_ + ng], fg[0:32, 10, 0:ng])
        k.barrier_all()
        es.close()


class KBScope:
    def __init__(self, k, es):
        self.k, self.es = k, es

    def sb(self, shape, dt=F32, name=None):
        self.k.nid += 1
        t = self.es.enter_context(self.k.nc.sbuf_tensor('%s_%d' % (name or 'sb', self.k.nid), list(shape), dt))
        try:
            self.k.min_rem = min(getattr(self.k, 'min_rem', 1 << 30), self.k.nc.sbuf_bytes_remaining)
        except Exception:
            pass
        return Buf(t)

    def ps(self, shape, dt=F32, name=None):
        self.k.nid += 1
        nb = int(np.prod(shape[1:])) * (4 if dt == F32 else 2)
        assert nb == 2048, ("psum tiles must be exactly one bank", shape, dt)
        t = self.es.enter_context(self.k.nc.psum_tensor('%s_%d' % (name or 'ps', self.k.nid), list(shape), dt))
        return Buf(t)


def run_interleaved(gens):
    live = list(gens)
    waiting = [False] * len(live)
    alive = [True] * len(live)
    while any(a and not w for a, w in zip(alive, waiting)):
        for i, g in enumerate(live):
            if not alive[i] or waiting[i]:
                continue
            try:
                r = next(g)
                if r == 'STATE':
                    waiting[i] = True
            except StopIteration:
                alive[i] = False
    for i, g in enumerate(live):
        if alive[i]:
            for _ in g:
                pass


def _chain_order(d):
    if d == 0:
        return list(range(NT))
    return list(range(NTC - 1, -1, -1)) + list(range(NT - 1, NTC - 1, -1))


def gen_gla(self, l, ntile=None, NBX=2):
    k, nc = self.k, self.nc
    es = ExitStack()
    kk = KBScope(k, es)
    W = self.w
    g2aug = [kk.sb([17, 128], F32, 'g2aug') for _ in range(2)]
    for d in range(2):
        k.dma('sp', g2aug[d][0:16, :], W['gla_g2'][l, d, :, :])
        k.dma('sp', g2aug[d][16:17, :], W['gla_gb'][l, d:d + 1, :])
    gn_bc = kk.sb([128, 256], F32, 'gn_bc')
    k.dma('sp', gn_bc[:, :], W['gla_norm'].wrap(W['gla_norm'].t[l, :].partition_broadcast(128)))
    yfwd = kk.sb([128, NT, 256], F32, 'yfwd')
    S = kk.sb([128, 256], F32, 'S')
    Sb = kk.sb([128, 256], BF16, 'Sb')
    bm = self.C('bm_gla')
    NB = NBX
    def mk(shape, dt=F32, name='b'):
        return [kk.sb(shape, dt, name) for _ in range(NB)]
    glT = mk([17, 128]); qT = mk([128, 128]); kT = mk([128, 128]); kv = mk([128, 384]); gate = mk([128, 256])
    e1 = mk([128, 128]); la16 = mk([128, 128]); eq = mk([128, 128]); ek = mk([128, 128])
    qtT = mk([128, 128], BF16); ktT = mk([128, 128], F32); kth = mk([128, 4, 128], BF16)
    kdec = mk([128, 128]); khat = mk([128, 128], BF16); vbf = mk([128, 256], BF16)
    sc = mk([128, 4, 128], BF16); ysum = mk([128, 256]); ssq = mk([128, 4]); rstd = mk([128, 4])
    yn = mk([128, 256]); sg = mk([128, 256]); yo = mk([128, 256], BF16); tmpc = mk([128, 256])
    jk = kk.sb([128, 64], F32, 'jk')
    epsb = kk.sb([128, 1], F32, 'epsb')
    k.memset('dve', epsb[:, :], EPS)
    psA = [kk.ps([128, 4, 128], F32, 'psA') for _ in range(NB)]
    psS = [kk.ps([128, 4, 128], F32, 'psS') for _ in range(NB)]
    psY = [kk.ps([128, 2, 256], F32, 'psY') for _ in range(NB)]
    psT = kk.ps([128, 8, 128], BF16, 'psT')
    ystg = mk([128, 2, 128], BF16, 'ystg')
    for b in range(NB):
        k.memset('dve', glT[b][:, :], 1.0)
    rq = FMROW[2568]; rk = FMROW[2696]; rg = FMROW[3080]
    qscale = 32 ** -0.5
    it = 0
    yield
    for d in range(2):
        order = _chain_order(d)
        if ntile is not None:
            order = [t for t in order if t < ntile]
        TRI = self.C('LE') if d == 0 else self.C('GE')
        STR = self.C('GT') if d == 0 else self.C('LT')
        last = 127 if d == 0 else 0
        k.memset('dve', S[:, :], 0.0)
        k.memset('dve', Sb[:, :], 0.0)
        def body(t, b):
            tok = slice(t * 128, (t + 1) * 128)
            pa = psA[b]; pS = psS[b]; pY = psY[b]
            k.dma('sp', glT[b][0:16, :], self.projFM[rg + 16 * d:rg + 16 * d + 16, tok])
            k.dma('sp', qT[b][:, :], self.projFM[rq:rq + 128, tok])
            k.dma('sp', kT[b][:, :], self.projFM[rk:rk + 128, tok])
            k.dma('sp', kv[b][:, :], self.projTM[tok, 2696:3080])
            if d == 1:
                k.dma('sp', gate[b][:, :], self.projTM[tok, 3112:3368])
            k.mm(pa[:, 0, :], glT[b][:, :], g2aug[d][:, :])
            k.act(e1[b][:, :], pa[:, 0, :], AF.Exp, scale=-1.0)
            k.act(e1[b][:, :], e1[b][:, :], AF.Ln, bias=1.0)
            k.ts('dve', la16[b][:, :], e1[b][:, :], -1.0 / 16.0, ALU.mult)
            yield
            k.mm(pa[:, 1, :], la16[b][:, :], TRI)
            k.act(eq[b][:, :], pa[:, 1, :], AF.Exp)
            k.act(ek[b][:, :], pa[:, 1, :], AF.Exp, scale=-1.0)
            k.stt('dve', qtT[b][:, :], qT[b][:, :], qscale, eq[b][:, :], ALU.mult, ALU.mult)
            k.tt('dve', ktT[b][:, :], kT[b][:, :], ek[b][:, :], ALU.mult)
            k.mm(pa[:, 2, :], STR, la16[b][:, :])
            k.act(kdec[b][:, :], pa[:, 2, :], AF.Exp)
            k.tt('dve', khat[b][:, :], kv[b][:, 0:128], kdec[b][:, :], ALU.mult)
            k.cp('pool', vbf[b][:, :], kv[b][:, 128:384])
            yield
            for h in range(4):
                k.amul(kth[b][:, h, :], ktT[b][:, :], self.C('hm4', h, h + 1)) if h % 2 else k.ts('dve', kth[b][:, h, :], ktT[b][:, :], self.C('hm4', h, h + 1), ALU.mult)
            for h in range(4):
                k.mm(pS[:, h, :], kth[b][:, h, :], qtT[b][:, :])
            for h in range(4):
                k.tt('dve', sc[b][:, h, :], pS[:, h, :], TRI, ALU.mult)
            yield 'STATE'
            for h in range(4):
                hs = slice(h * 64, (h + 1) * 64)
                k.mm(pY[:, 0, hs], qtT[b][:, :], Sb[:, hs], start=True, stop=False)
                k.mm(pY[:, 0, hs], sc[b][:, h, :], vbf[b][:, hs], start=False, stop=True)
            if d == 0:
                k.cp('act', yfwd[:, t, :], pY[:, 0, :])
            else:
                k.tt('dve', ysum[b][:, :], pY[:, 0, :], yfwd[:, t, :], ALU.add)
                for h in range(4):
                    hs = slice(h * 64, (h + 1) * 64)
                    k.act(jk[:, :], ysum[b][:, hs], AF.Square, accum_out=ssq[b][:, h:h + 1])
                k.act(rstd[b][:, :], ssq[b][:, :], AF.Sqrt, bias=epsb[:, :], scale=1.0 / 64)
                k.recip(rstd[b][:, :], rstd[b][:, :])
                for h in range(4):
                    hs = slice(h * 64, (h + 1) * 64)
                    k.stt('dve', yn[b][:, hs], ysum[b][:, hs], rstd[b][:, h:h + 1], gn_bc[:, hs], ALU.mult, ALU.mult)
                k.act(sg[b][:, :], gate[b][:, :], AF.Silu)
                k.tt('dve', yo[b][:, :], yn[b][:, :], sg[b][:, :], ALU.mult)
                for c in range(2):
                    k.tr(psT[:, c, :], yo[b][:, c * 128:(c + 1) * 128], self.identb[:, :])
                k.cp('act', ystg[b][:, :, :], psT[:, 0:2, :])
                k.dma('act', self.ycatD.at('gla', (slice(None), slice(6, 8), tok)), ystg[b][:, :, :])
            yield
            k.mm(pY[:, 1, :], khat[b][:, :], vbf[b][:, :])
            k.tt('dve', tmpc[b][:, :], pY[:, 1, :], bm, ALU.mult)
            k.stt('dve', S[:, :], S[:, :], eq[b][:, last:last + 1], tmpc[b][:, :], ALU.mult, ALU.add)
            k.cp('act', Sb[:, :], S[:, :])
        if NB == 1:
            for t_ in order:
                for _r in body(t_, 0):
                    yield
        else:
            for i_ in range(0, len(order), 2):
                run_interleaved([body(t_, b_) for b_, t_ in enumerate(order[i_:i_ + 2])])
                yield
    yield ('DONE', es)


def phase_gla(self, l, ntile=None):
    for r in gen_gla(self, l, ntile):
        if isinstance(r, tuple):
            self.k.barrier_all()
            r[1].close()


def phase_gla_s5(self, l, ntile=None):
    ra = int(os.environ.get('RA', '6'))
    rb = int(os.environ.get('RB', '6'))
    g1 = gen_gla(self, l, None, NBX=1)
    next(g1)
    g2 = gen_s5(self, l)
    d1 = d2 = None
    while d1 is None or d2 is None:
        if d1 is None:
            for _ in range(ra):
                r = next(g1)
                if isinstance(r, tuple):
                    d1 = r[1]
                    break
        if d2 is None:
            for _ in range(rb):
                r = next(g2)
                if isinstance(r, tuple):
                    d2 = r[1]
                    break
    self.k.barrier_all()
    d2.close()
    d1.close()


Prog.phase_gla_s5 = phase_gla_s5
Prog.phase_gla = phase_gla


def phase_ssd(self, l, ntile=None):
    k, nc = self.k, self.nc
    es = ExitStack()
    kk = KBScope(k, es)
    W = self.w
    ntile = NT if ntile is None else ntile
    r0 = FMROW[1792]
    cw = kk.sb([128, 4, 5], F32, 'cw')
    cb = kk.sb([128, 4], F32, 'cb')
    with nc.allow_non_contiguous_dma(reason="small param transposes"):
        for kt in range(5):
            k.dma('sp', cw[:, :, kt], W['ssd_conv_w'].wrap(W['ssd_conv_w'].t[l, kt, :].rearrange("(c p) -> p c", p=128)))
        k.dma('sp', cb[:, :], W['ssd_conv_b'].wrap(W['ssd_conv_b'].t[l, :].rearrange("(c p) -> p c", p=128)))
    dtb = kk.sb([128, 8], F32, 'dtb')
    aneg = kk.sb([128, 8], F32, 'aneg')
    Dbc = kk.sb([128, 4], F32, 'Dbc')
    nbc = kk.sb([128, 256], F32, 'nbc')
    k.dma('sp', dtb[:, :], W['ssd_dt_bias'].wrap(W['ssd_dt_bias'].t[l].rearrange("a b -> (a b)").partition_broadcast(128)))
    k.dma('sp', aneg[:, :], W['ssd_a_log'].wrap(W['ssd_a_log'].t[l].rearrange("a b -> (a b)").partition_broadcast(128)))
    k.dma('sp', Dbc[:, :], W['ssd_d'].wrap(W['ssd_d'].t[l, :].partition_broadcast(128)))
    k.dma('sp', nbc[:, :], W['ssd_norm'].wrap(W['ssd_norm'].t[l, :].partition_broadcast(128)))
    k.act(aneg[:, :], aneg[:, :], AF.Exp)
    k.ts('dve', aneg[:, :], aneg[:, :], -1.0, ALU.mult)
    xc = kk.sb([128, 4, T], BF16, 'xc')
    xin = [kk.sb([128, 516], F32, 'xin') for _ in range(2)]
    acc = [kk.sb([128, 512], F32, 'acc') for _ in range(2)]
    pieces = [(0, min(CTXL, ntile * 128), 0, CTXL)]
    for p0 in range(CTXL, min(T, ntile * 128), 512):
        pieces.append((p0, min(512, ntile * 128 - p0), CTXL, T))
    i = 0
    for (p0, n, lo, hi) in pieces:
        for blk in range(4):
            b = i % 2
            i += 1
            a0 = max(lo, p0 - 2)
            a1 = min(hi, p0 + n + 2, ntile * 128)
            if a0 > p0 - 2 or a1 < p0 + n + 2:
                k.memset('pool', xin[b][:, :], 0.0)
            k.dma('sp', xin[b][:, a0 - (p0 - 2):a1 - (p0 - 2)], self.projFM[r0 + blk * 128:r0 + (blk + 1) * 128, a0:a1])
            k.ts('dve', acc[b][:, 0:n], xin[b][:, 0:n], cw[:, blk, 0:1], ALU.mult, cb[:, blk:blk + 1], ALU.add)
            for kt in range(1, 5):
                k.stt('dve', acc[b][:, 0:n], xin[b][:, kt:kt + n], cw[:, blk, kt:kt + 1], acc[b][:, 0:n], ALU.mult, ALU.add)
            k.act(xc[:, blk, p0:p0 + n], acc[b][:, 0:n], AF.Silu)
    yfwd = kk.sb([128, NT, 256], F32, 'yfwd')
    S = kk.sb([128, 2, 64], F32, 'S')
    Sb = kk.sb([128, 2, 2, 64], BF16, 'Sb')
    NB = 2
    def mk(shape, dt=F32, name='b'):
        return [kk.sb(shape, dt, name) for _ in range(NB)]
    dtr = mk([128, 8]); zt = mk([128, 256]); dt_ = mk([128, 4]); la = mk([128, 4]); ecs = mk([128, 4]); dec = mk([128, 4])
    tot = mk([128, 4]); sdec = mk([128, 2]); rep = mk([128, 4, 128]); seg = mk([128, 4, 128], F32)
    scm = mk([128, 4, 128], BF16); xtm = mk([128, 256], BF16); btm = mk([128, 128], BF16); vv = mk([128, 256], BF16)
    bz = mk([128, 4, 128], BF16); btg = mk([128, 2, 128], BF16); yst = mk([128, 256]); ysum = mk([128, 256]); gz = mk([128, 256]); yo = mk([128, 256], BF16)
    ssq = mk([128, 1]); rstd = mk([128, 1]); sz = mk([128, 256]); ystg = mk([128, 2, 128], BF16, 'ystg')
    jk = kk.sb([128, 256], F32, 'jk')
    epsb = kk.sb([128, 1], F32, 'epsb')
    k.memset('dve', epsb[:, :], EPS)
    psA = kk.ps([128, 4, 128], F32, 'psA')
    psG = [kk.ps([128, 4, 128], F32, 'psG') for _ in range(2)]
    psC = [kk.ps([128, 4, 128], F32, 'psC') for _ in range(2)]
    psY = [kk.ps([128, 2, 256], F32, 'psY') for _ in range(2)]
    psT = kk.ps([128, 8, 128], BF16, 'psT')
    ones = self.C('ones')
    MSK4 = [kk.sb([128, 4, 128], F32, 'MSK4') for _ in range(2)]
    for d_ in range(2):
        for h_ in range(4):
            k.cp('pool', MSK4[d_][:, h_, :], self.C('MLE') if d_ == 0 else self.C('MGE'))
    it = 0
    for d in range(2):
        order = [t for t in _chain_order(d) if t < ntile]
        TRI = self.C('LE') if d == 0 else self.C('GE')
        NTRI = self.C('NLE') if d == 0 else self.C('NGE')
        MSK = self.C('MLE') if d == 0 else self.C('MGE')
        STR = self.C('GT') if d == 0 else self.C('LT')
        k.memset('dve', S[:, :, :], 0.0)
        k.memset('dve', Sb[:, :, :, :], 0.0)
        def body(t, b):
            tok = slice(t * 128, (t + 1) * 128)
            pG = psG[b]; pC = psC[b]; pY = psY[b]
            k.dma('sp', dtr[b][:, :], self.projTM[tok, 2304:2312])
            if d == 1:
                k.dma('sp', zt[b][:, :], self.projTM[tok, 2312:2568])
            k.tt('dve', dt_[b][:, :], dtr[b][:, 4 * d:4 * d + 4], dtb[:, 4 * d:4 * d + 4], ALU.add)
            k.act(dt_[b][:, :], dt_[b][:, :], AF.Exp)
            k.act(dt_[b][:, :], dt_[b][:, :], AF.Ln, bias=1.0)
            k.tt('dve', la[b][:, :], dt_[b][:, :], aneg[:, 4 * d:4 * d + 4], ALU.mult)
            k.mm(psA[:, 0, 0:4], TRI, la[b][:, :])
            k.mm(psA[:, 1, 0:4], STR, la[b][:, :])
            k.mm(psA[:, 2, 0:4], ones, la[b][:, :])
            k.act(ecs[b][:, :], psA[:, 0, 0:4], AF.Exp)
            k.act(dec[b][:, :], psA[:, 1, 0:4], AF.Exp)
            k.act(tot[b][:, :], psA[:, 2, 0:4], AF.Exp)
            k.ts('dve', sdec[b][:, :], tot[b][:, 0:2], self.C('hm2', 0, 1), ALU.mult)
            k.stt('dve', sdec[b][:, :], tot[b][:, 2:4], self.C('hm2', 1, 2), sdec[b][:, :], ALU.mult, ALU.add)
            yield
            for h in range(4):
                k.amul(rep[b][:, h, :], ones, la[b][:, h:h + 1]) if h % 2 else k.ts('dve', rep[b][:, h, :], ones, la[b][:, h:h + 1], ALU.mult)
            pG4 = pG.wrap(pG.t[:, :, :].rearrange("p a b -> p (a b)"))
            k.mm(pG4, NTRI, rep[b].wrap(rep[b].t[:, :, :].rearrange("p a b -> p (a b)")), start=True, stop=False)
            for h in range(4):
                k.mm(pG[:, h, :], rep[b][:, h, :], TRI, start=False, stop=False)
            k.mm(pG4, self.ident, MSK4[d].wrap(MSK4[d].t[:, :, :].rearrange("p a b -> p (a b)")), start=False, stop=True)
            k.act(seg[b][:, :, :], pG[:, :, :], AF.Exp)
            yield
            for g in range(2):
                k.amul(btg[b][:, g, :], xc[:, 2, tok], self.C('hm2', g, g + 1))
                k.mm(pC[:, g, :], btg[b][:, g, :], xc[:, 3, tok])
            for h in range(4):
                k.tt('dve', scm[b][:, h, :], pC[:, h // 2, :], seg[b][:, h, :], ALU.mult)
            yield
            for c in range(2):
                k.tr(psT[:, c, :], xc[:, c, tok], self.identb[:, :])
            k.tr(psT[:, 2, :], xc[:, 2, tok], self.identb[:, :])
            k.cp('act', xtm[b][:, :], psT[:, 0:2, :])
            k.cp('act', btm[b][:, :], psT[:, 2, :])
            for h in range(4):
                hs = slice(h * 64, (h + 1) * 64)
                k.act(vv[b][:, hs], xtm[b][:, hs], AF.Copy, scale=dt_[b][:, h:h + 1])
            yield 'STATE'
            for h in range(4):
                hs = slice(h * 64, (h + 1) * 64)
                k.mm(pY[:, 0, hs], scm[b][:, h, :], vv[b][:, hs])
                k.mm(pY[:, 1, hs], xc[:, 3, tok], Sb[:, h // 2, h % 2, :])
            for h in range(4):
                hs = slice(h * 64, (h + 1) * 64)
                k.act(yst[b][:, hs], pY[:, 1, hs], AF.Copy, scale=ecs[b][:, h:h + 1])
            if d == 0:
                k.tt('dve', yfwd[:, t, :], pY[:, 0, :], yst[b][:, :], ALU.add)
            else:
                k.tt('dve', ysum[b][:, :], pY[:, 0, :], yst[b][:, :], ALU.add)
                k.tt('pool', ysum[b][:, :], ysum[b][:, :], yfwd[:, t, :], ALU.add)
                for h in range(4):
                    hs = slice(h * 64, (h + 1) * 64)
                    k.stt('dve', ysum[b][:, hs], xtm[b][:, hs], Dbc[:, h:h + 1], ysum[b][:, hs], ALU.mult, ALU.add)
                k.act(sz[b][:, :], zt[b][:, :], AF.Silu)
                k.tt('dve', gz[b][:, :], ysum[b][:, :], sz[b][:, :], ALU.mult)
                k.act(jk[:, :], gz[b][:, :], AF.Square, accum_out=ssq[b][:, :])
                k.act(rstd[b][:, :], ssq[b][:, :], AF.Sqrt, bias=epsb[:, :], scale=1.0 / 256)
                k.recip(rstd[b][:, :], rstd[b][:, :])
                k.stt('dve', yo[b][:, :], gz[b][:, :], rstd[b][:, 0:1], nbc[:, :], ALU.mult, ALU.mult)
                for c in range(2):
                    k.tr(psT[:, 4 + c, :], yo[b][:, c * 128:(c + 1) * 128], self.identb[:, :])
                k.cp('act', ystg[b][:, :, :], psT[:, 4:6, :])
                k.dma('act', self.ycatD.at('ssd', (slice(None), slice(4, 6), tok)), ystg[b][:, :, :])
            yield
            for h in range(4):
                k.stt('dve', bz[b][:, h, :], btm[b][:, :], dec[b][:, h:h + 1],
                      self.C('GM%d' % (h // 2)),
                      ALU.mult, ALU.mult)
            for hh in range(2):
                k.mm(psA[:, 3, hh * 64:(hh + 1) * 64], bz[b][:, hh, :], vv[b][:, hh * 64:(hh + 1) * 64], start=True, stop=False)
                k.mm(psA[:, 3, hh * 64:(hh + 1) * 64], bz[b][:, 2 + hh, :], vv[b][:, (2 + hh) * 64:(3 + hh) * 64], start=False, stop=True)
            for hh in range(2):
                k.stt('dve', S[:, hh, :], S[:, hh, :], sdec[b][:, hh:hh + 1], psA[:, 3, hh * 64:(hh + 1) * 64], ALU.mult, ALU.add)
            for g in range(2):
                k.amul(Sb[:, g, :, :], S[:, :, :], self.C('hm2', g, g + 1))
        for i_ in range(0, len(order), 2):
            run_interleaved([body(t_, b_) for b_, t_ in enumerate(order[i_:i_ + 2])])
    k.barrier_all()
    es.close()


Prog.phase_ssd = phase_ssd


def gen_s5(self, l, ntile=None):
    k, nc = self.k, self.nc
    es = ExitStack()
    kk = KBScope(k, es)
    W = self.w
    I32 = mybir.dt.int32
    NCH = T // 8
    NCC = CTXL // 8
    TWO_PI = 6.283185307179586
    Tsum = kk.sb([128, 16, 128], BF16, 'Tsum')
    Wall = kk.sb([128, 16, 2, 128], BF16, 'Wall')
    Vall = kk.sb([128, 16, 2, 128], BF16, 'Vall')
    MUr = kk.sb([128, 16, 10], F32, 'MUr'); MUi = kk.sb([128, 16, 10], F32, 'MUi'); NMi = kk.sb([128, 16, 10], F32, 'NMi')
    BK = [kk.ps([128, 512], F32, 'bk5') for _ in range(4)]
    psP = [Buf(BK[i].t[:, :].rearrange("p (a b) -> p a b", a=4)) for i in range(2)]
    for i in range(2):
        psP[i].tok = BK[i].tok
    es2 = ExitStack()
    kp = KBScope(k, es2)
    def sm(shape=(128, 16), dt=F32, name='sm'):
        return kp.sb(list(shape), dt, name)
    are = sm(); aim = sm(); ldt = sm()
    with nc.allow_non_contiguous_dma(reason="small param transposes"):
        for d in range(2):
            k.dma('sp', are[64 * d:64 * d + 64, :], W['s5_a_re'].wrap(W['s5_a_re'].t[l, d].rearrange("g p -> p g")))
            k.dma('sp', aim[64 * d:64 * d + 64, :], W['s5_a_im'].wrap(W['s5_a_im'].t[l, d].rearrange("g p -> p g")))
            k.dma('sp', ldt[64 * d:64 * d + 64, :], W['s5_log_dt'].wrap(W['s5_log_dt'].t[l, d, :].partition_broadcast(64)))
    bre = kp.sb([128, 16, 16], F32, 'bre'); bim = kp.sb([128, 16, 16], F32, 'bim')
    for d in range(2):
        k.dma('sp', bre[64 * d:64 * d + 64, :, :], W['s5_b_re'].wrap(W['s5_b_re'].t[l, d].rearrange("g p c -> p g c")))
        k.dma('sp', bim[64 * d:64 * d + 64, :, :], W['s5_b_im'].wrap(W['s5_b_im'].t[l, d].rearrange("g p c -> p g c")))
    cre = kp.sb([128, 16, 16], F32, 'cre'); cim = kp.sb([128, 16, 16], F32, 'cim')
    craw = kp.sb([128, 2, 2, 2, 128], F32, 'craw')
    k.memset('pool', craw[:, :, :, :, :], 0.0)
    for ri, nm in enumerate(['s5_c_re', 's5_c_im']):
        for d in range(2):
            k.dma('sp', craw[:, ri, d, :, 64 * d:64 * d + 64],
                  W[nm].wrap(W[nm].t[l, d].rearrange("g c p -> (g c) p").rearrange("(q r) p -> r q p", r=128)))
    for ri, dst in enumerate([cre, cim]):
        pp = psP[ri]
        for q in range(2):
            for d in range(2):
                k.mm(pp[:, q, :], craw[:, ri, d, q, :], self.ident, start=(d == 0), stop=(d == 1))
        k.cp('dve', dst[:, :, :], pp[:, 0:2, :])
    dt_ = sm(); ar = sm(); ai = sm(); mag = sm(); tt_ = sm(); ki = sm(dt=I32); kf = sm(); rr = sm(); half = sm()
    sh = sm(); ah = sm(); ch = sm(); cc = sm(); ss = sm(); lr = sm(); li = sm(); t1 = sm(); t2 = sm()
    k.act(dt_[:, :], ldt[:, :], AF.Exp)
    k.tt('dve', ar[:, :], are[:, :], dt_[:, :], ALU.mult)
    k.tt('dve', ai[:, :], aim[:, :], dt_[:, :], ALU.mult)
    k.act(mag[:, :], ar[:, :], AF.Exp)
    k.ts('dve', tt_[:, :], ai[:, :], 1.0 / TWO_PI, ALU.mult)
    k.cp('dve', ki[:, :], tt_[:, :])
    k.cp('dve', kf[:, :], ki[:, :])
    k.stt('dve', rr[:, :], kf[:, :], -TWO_PI, ai[:, :], ALU.mult, ALU.add)
    k.ts('dve', half[:, :], rr[:, :], 0.5, ALU.mult)
    k.act(sh[:, :], half[:, :], AF.Sin)
    k.act(ah[:, :], half[:, :], AF.Abs)
    k.act(ch[:, :], ah[:, :], AF.Sin, scale=-1.0, bias=1.5707963267948966)
    k.tt('dve', ss[:, :], sh[:, :], ch[:, :], ALU.mult)
    k.ts('dve', ss[:, :], ss[:, :], 2.0, ALU.mult)
    k.tt('dve', t1[:, :], ch[:, :], ch[:, :], ALU.mult)
    k.tt('dve', t2[:, :], sh[:, :], sh[:, :], ALU.mult)
    k.tt('dve', cc[:, :], t1[:, :], t2[:, :], ALU.subtract)
    k.tt('dve', lr[:, :], mag[:, :], cc[:, :], ALU.mult)
    k.tt('dve', li[:, :], mag[:, :], ss[:, :], ALU.mult)
    yield
    PPr = kp.sb([128, 16, 9], F32, 'PPr'); PPi = kp.sb([128, 16, 9], F32, 'PPi')
    PNr = kp.sb([128, 16, 9], F32, 'PNr'); PNi = kp.sb([128, 16, 9], F32, 'PNi')
    def cmul(or_, oi_, ar_, ai_, br_, bi_):
        k.tt('dve', t1[:, :], ar_, br_, ALU.mult)
        k.tt('dve', t2[:, :], ai_, bi_, ALU.mult)
        k.tt('dve', or_, t1[:, :], t2[:, :], ALU.subtract)
        k.tt('dve', t1[:, :], ar_, bi_, ALU.mult)
        k.tt('dve', t2[:, :], ai_, br_, ALU.mult)
        k.tt('dve', oi_, t1[:, :], t2[:, :], ALU.add)
    im2 = sm(); nr = sm(); ni = sm()
    k.act(im2[:, :], ar[:, :], AF.Exp, scale=-2.0)
    k.tt('dve', nr[:, :], lr[:, :], im2[:, :], ALU.mult)
    k.tt('dve', ni[:, :], li[:, :], im2[:, :], ALU.mult)
    k.ts('dve', ni[:, :], ni[:, :], -1.0, ALU.mult)
    for (Pr, Pi, br_, bi_, n) in ((PPr, PPi, lr, li, 9), (PNr, PNi, nr, ni, 8)):
        k.memset('dve', Pr[:, :, 0], 1.0)
        k.memset('dve', Pi[:, :, 0], 0.0)
        for s_ in range(1, n):
            cmul(Pr[:, :, s_], Pi[:, :, s_], Pr[:, :, s_ - 1], Pi[:, :, s_ - 1], br_[:, :], bi_[:, :])
    k.cp('dve', MUr[:, :, 0], PPr[:, :, 8])
    k.cp('dve', MUi[:, :, 0], PPi[:, :, 8])
    for q in range(1, 10):
        cmul(MUr[:, :, q], MUi[:, :, q], MUr[:, :, q - 1], MUi[:, :, q - 1], MUr[:, :, q - 1], MUi[:, :, q - 1])
    k.ts('dve', NMi[:, :, :], MUi[:, :, :], -1.0, ALU.mult)
    yield
    den = sm(); cfr = sm(); cfi = sm(); n1 = sm()
    k.tt('dve', t1[:, :], are[:, :], are[:, :], ALU.mult)
    k.tt('dve', t2[:, :], aim[:, :], aim[:, :], ALU.mult)
    k.tt('dve', den[:, :], t1[:, :], t2[:, :], ALU.add)
    k.recip(den[:, :], den[:, :])
    k.ts('dve', n1[:, :], lr[:, :], -1.0, ALU.add)
    k.tt('dve', t1[:, :], n1[:, :], are[:, :], ALU.mult)
    k.tt('dve', t2[:, :], li[:, :], aim[:, :], ALU.mult)
    k.tt('dve', cfr[:, :], t1[:, :], t2[:, :], ALU.add)
    k.tt('dve', cfr[:, :], cfr[:, :], den[:, :], ALU.mult)
    k.tt('dve', t1[:, :], li[:, :], are[:, :], ALU.mult)
    k.tt('dve', t2[:, :], n1[:, :], aim[:, :], ALU.mult)
    k.tt('dve', cfi[:, :], t1[:, :], t2[:, :], ALU.subtract)
    k.tt('dve', cfi[:, :], cfi[:, :], den[:, :], ALU.mult)
    bbr = kp.sb([128, 16, 16], F32, 'bbr'); bbi = kp.sb([128, 16, 16], F32, 'bbi')
    u1 = kp.sb([128, 16, 16], F32, 'u1'); u2 = kp.sb([128, 16, 16], F32, 'u2')
    def bc3(a):
        return A(a.ap.unsqueeze(2).to_broadcast([128, 16, 16]), a.tok)
    k.tt('dve', u1[:, :, :], bre[:, :, :], bc3(cfr[:, :]), ALU.mult)
    k.tt('dve', u2[:, :, :], bim[:, :, :], bc3(cfi[:, :]), ALU.mult)
    k.tt('dve', bbr[:, :, :], u1[:, :, :], u2[:, :, :], ALU.subtract)
    k.tt('dve', u1[:, :, :], bim[:, :, :], bc3(cfr[:, :]), ALU.mult)
    k.tt('dve', u2[:, :, :], bre[:, :, :], bc3(cfi[:, :]), ALU.mult)
    k.tt('dve', bbi[:, :, :], u1[:, :, :], u2[:, :, :], ALU.add)
    GB = 4
    def tb(name):
        return kp.sb([128, GB, 8, 16], F32, name)
    Xr = tb('Xr'); Xi = tb('Xi'); v1 = tb('v1'); v2 = tb('v2')
    Wtr = tb('Wtr'); Wti = tb('Wti'); Vtr = tb('Vtr'); NVti = tb('NVti')
    Wz = [[tb('Wz') for _ in range(2)] for _ in range(2)]
    tmpT = kp.sb([128, 128], F32, 'tmpT')
    def pw(P, half, gs, sl):
        a = P.t[64 * half:64 * half + 64, gs, sl]
        return A(a.unsqueeze(3).to_broadcast([64, GB, 8, 16]), P.tok)
    def bb(Bt, half, gs):
        a = Bt.t[64 * half:64 * half + 64, gs, :]
        return A(a.unsqueeze(2).to_broadcast([64, GB, 8, 16]), Bt.tok)
    def ctab(outr, outi, Pr, Pi, slf, slr, Br, Bi, gs, neg_i=False):
        for half, sl in ((0, slf), (1, slr)):
            hs = slice(64 * half, 64 * half + 64)
            k.tt('dve', v1[hs, :, :, :], pw(Pr, half, gs, sl), bb(Br, half, gs), ALU.mult)
            k.tt('dve', v2[hs, :, :, :], pw(Pi, half, gs, sl), bb(Bi, half, gs), ALU.mult)
            k.tt('dve', outr[hs, :, :, :], v1[hs, :, :, :], v2[hs, :, :, :], ALU.subtract)
            k.tt('dve', v1[hs, :, :, :], pw(Pr, half, gs, sl), bb(Bi, half, gs), ALU.mult)
            k.tt('dve', v2[hs, :, :, :], pw(Pi, half, gs, sl), bb(Br, half, gs), ALU.mult)
            k.tt('dve', outi[hs, :, :, :], v1[hs, :, :, :], v2[hs, :, :, :], ALU.add)
            if neg_i:
                k.ts('dve', outi[hs, :, :, :], outi[hs, :, :, :], -1.0, ALU.mult)
    S07 = slice(0, 8); S18 = slice(1, 9); R70 = slice(7, None, -1); R81 = slice(8, 0, -1)
    for gb in range(16 // GB):
        gs = slice(gb * GB, (gb + 1) * GB)
        yield
        ctab(Xr, Xi, PPr, PPi, R70, S07, bbr, bbi, gs)
        for gi in range(GB):
            g = gb * GB + gi
            pp = psP[gi % 2]
            k.tr(pp[:, 0, :], Xr[:, gi, :, :], self.ident)
            k.tr(pp[:, 1, :], Xi[:, gi, :, :], self.ident)
            k.cp('act', Wall[:, g, :, :], pp[:, 0:2, :])
        ctab(Xr, Xi, PPr, PPi, S18, R81, cre, cim, gs, neg_i=True)
        k.cp('act', Vall[:, gs, 0, :], Xr[:, :, :, :])
        k.cp('act', Vall[:, gs, 1, :], Xi[:, :, :, :])
        for half, (Pr_, Pi_) in ((0, (PNr, PNi)), (1, (PPr, PPi))):
            pass
        for half, (Pr_, Pi_) in ((0, (PNr, PNi)), (1, (PPr, PPi))):
            hs = slice(64 * half, 64 * half + 64)
            k.tt('dve', v1[hs, :, :, :], pw(Pr_, half, gs, S07), bb(bbr, half, gs), ALU.mult)
            k.tt('dve', v2[hs, :, :, :], pw(Pi_, half, gs, S07), bb(bbi, half, gs), ALU.mult)
            k.tt('dve', Wtr[hs, :, :, :], v1[hs, :, :, :], v2[hs, :, :, :], ALU.subtract)
            k.tt('dve', v1[hs, :, :, :], pw(Pr_, half, gs, S07), bb(bbi, half, gs), ALU.mult)
            k.tt('dve', v2[hs, :, :, :], pw(Pi_, half, gs, S07), bb(bbr, half, gs), ALU.mult)
            k.tt('dve', Wti[hs, :, :, :], v1[hs, :, :, :], v2[hs, :, :, :], ALU.add)
        for half, (Pr_, Pi_) in ((0, (PPr, PPi)), (1, (PNr, PNi))):
            hs = slice(64 * half, 64 * half + 64)
            k.tt('dve', v1[hs, :, :, :], pw(Pr_, half, gs, S07), bb(cre, half, gs), ALU.mult)
            k.tt('dve', v2[hs, :, :, :], pw(Pi_, half, gs, S07), bb(cim, half, gs), ALU.mult)
            k.tt('dve', Vtr[hs, :, :, :], v1[hs, :, :, :], v2[hs, :, :, :], ALU.subtract)
            k.tt('dve', v1[hs, :, :, :], pw(Pr_, half, gs, S07), bb(cim, half, gs), ALU.mult)
            k.tt('dve', v2[hs, :, :, :], pw(Pi_, half, gs, S07), bb(cre, half, gs), ALU.mult)
            k.tt('dve', NVti[hs, :, :, :], v1[hs, :, :, :], v2[hs, :, :, :], ALU.add)
            k.ts('dve', NVti[hs, :, :, :], NVti[hs, :, :, :], -1.0, ALU.mult)
        yield
        for d in range(2):
            k.ts('dve', Wz[d][0][:, :, :, :], Wtr[:, :, :, :], self.C('hm2', d, d + 1), ALU.mult)
            k.ts('pool', Wz[d][1][:, :, :, :], Wti[:, :, :, :], self.C('hm2', d, d + 1), ALU.mult)
        for gi in range(GB):
            g = gb * GB + gi
            pp = psP[gi % 2]
            for d in range(2):
                k.mm(pp[:, 2 + d, :], Wz[d][0][:, gi, :, :], Vtr[:, gi, :, :], start=True, stop=False)
                k.mm(pp[:, 2 + d, :], Wz[d][1][:, gi, :, :], NVti[:, gi, :, :], start=False, stop=True)
            k.tt('dve', tmpT[:, :], pp[:, 2, :], self.C('BMf'), ALU.mult)
            k.tt('dve', Tsum[:, g, :], pp[:, 3, :], self.C('BMr'), ALU.mult)
            k.tt('dve', Tsum[:, g, :], Tsum[:, g, :], tmpT[:, :], ALU.add)
    k.barrier_all()
    es2.close()
    selb = kk.sb([128, 16, 256], BF16, 'selb')
    k.dma('pool', selb[:, :, :], self.seldram.wrap(self.seldram.t[:, :].rearrange("p (m x) -> p m x", x=256)))
    uTb1 = kk.sb([128, T], BF16, 'uTb')
    uTb = [uTb1, uTb1]
    r0 = FMROW[1280]
    def load_u(c):
        for hh in range(4):
            k.dma('pool', uTb1[:, hh * (T // 4):(hh + 1) * (T // 4)],
                  self.projFM[r0 + 128 * c:r0 + 128 * c + 128, hh * (T // 4):(hh + 1) * (T // 4)])
    y8all = kk.sb([128, 16, NCH], BF16, 'y8all')
    u8 = [kk.sb([128, NCH], BF16, 'u8') for _ in range(2)]
    Bs2 = [[kk.sb([128, 2, NCH], F32, 'Bs') for _ in range(2)] for _ in range(2)]
    Xp = [kk.sb([128, 2, NCH], BF16, 'Xp') for _ in range(2)]
    psU = BK[0]; psU2 = BK[1]; psXb = BK[2]; psY = BK[3]
    LAT = slice(NCC, NCH); CTX = slice(0, NCC)
    def gbody(g):
        b = g % 2
        c = g // 8; gl = g % 8
        if gl == 0:
            load_u(c)
        yield
        for j in range(8):
            sel = selb[:, gl, 128 - 16 * j:256 - 16 * j]
            k.mm(psU[:, :], sel, uTb[c][:, CTXL + j:T:8], start=(j == 0), stop=(j == 7))
            k.mm(psU2[:, 0:NCC], sel, uTb[c][:, j:CTXL:8], start=(j == 0), stop=(j == 7))
        k.cp('act', u8[b][:, LAT], psU[:, :])
        k.cp('act', u8[b][:, CTX], psU2[:, 0:NCC])
        B0 = Bs2[b][0]; B1 = Bs2[b][1]
        for ri in range(2):
            k.mm(psU2[:, 64 + 32 * ri:96 + 32 * ri], Wall[:, g, ri, :], u8[b][:, CTX])
        for ri in range(2):
            k.cp('act', B0[0:64, ri, CTX], psU2[0:64, 64 + 32 * ri:96 + 32 * ri])
            k.cp('dve', B0[64:128, ri, CTX], psU2[64:128, 95 + 32 * ri:63 + 32 * ri:-1])
        yield
        for hf in range(2):
            a_, b_ = NCC + 256 * hf, NCC + 256 * (hf + 1)
            for ri in range(2):
                k.mm(psXb[:, 256 * ri:256 * (ri + 1)], Wall[:, g, ri, :], u8[b][:, a_:b_])
            for ri in range(2):
                k.cp('act', B0[0:64, ri, a_:b_], psXb[0:64, 256 * ri:256 * (ri + 1)])
                k.cp('dve', B0[64:128, ri, NCH + NCC - b_:NCH + NCC - a_], psXb[64:128, 256 * (ri + 1) - 1:(256 * ri - 1) if ri > 0 else None:-1])
            yield
        src, dst = B0, B1
        for q in range(10):
            shf = 1 << q
            n = NCH - shf
            mr = MUr[:, g, q:q + 1]; mi = MUi[:, g, q:q + 1]; nmi = NMi[:, g, q:q + 1]
            k.stt('dve', dst[:, 0, shf:], src[:, 0, 0:n], mr, src[:, 0, shf:], ALU.mult, ALU.add)
            k.stt('dve', dst[:, 0, shf:], src[:, 1, 0:n], nmi, dst[:, 0, shf:], ALU.mult, ALU.add)
            k.stt('dve', dst[:, 1, shf:], src[:, 1, 0:n], mr, src[:, 1, shf:], ALU.mult, ALU.add)
            k.stt('dve', dst[:, 1, shf:], src[:, 0, 0:n], mi, dst[:, 1, shf:], ALU.mult, ALU.add)
            k.cp('pool', dst[:, :, 0:shf], src[:, :, 0:shf])
            src, dst = dst, src
            yield
        X = src
        xp = Xp[b]
        k.memset('pool', xp[0:64, :, 0:1], 0.0)
        k.cp('act', xp[0:64, :, 1:NCH], X[0:64, :, 0:NCH - 1])
        k.memset('pool', xp[64:128, :, NCC - 1:NCC], 0.0)
        k.cp('dve', xp[64:128, :, 0:NCC - 1], X[64:128, :, NCC - 2::-1])
        k.cp('dve', xp[64:128, :, NCC:NCH], X[64:128, :, NCH - 2:NCC - 2:-1])
        yield
        k.mm(psY[:, :], Tsum[:, g, :], u8[b][:, LAT], start=True, stop=False)
        k.mm(psY[:, :], Vall[:, g, 0, :], xp[:, 0, LAT], start=False, stop=False)
        k.mm(psY[:, :], Vall[:, g, 1, :], xp[:, 1, LAT], start=False, stop=True)
        k.mm(psU2[:, 256:256 + NCC], Tsum[:, g, :], u8[b][:, CTX], start=True, stop=False)
        k.mm(psU2[:, 256:256 + NCC], Vall[:, g, 0, :], xp[:, 0, CTX], start=False, stop=False)
        k.mm(psU2[:, 256:256 + NCC], Vall[:, g, 1, :], xp[:, 1, CTX], start=False, stop=True)
        k.cp('act', y8all[:, g, LAT], psY[:, :])
        k.cp('act', y8all[:, g, CTX], psU2[:, 256:256 + NCC])
    for g0 in range(0, 16, 2):
        run_interleaved([gbody(g0), gbody(g0 + 1)])
        yield
    Dt = kk.sb([128, 2], F32, 'Dt'); gbT = kk.sb([128, 2], F32, 'gbT')
    with nc.allow_non_contiguous_dma(reason="small param transposes"):
        k.dma('sp', Dt[:, :], W['s5_d'].wrap(W['s5_d'].t[l, :].rearrange("(c p) -> p c", p=128)))
        k.dma('sp', gbT[:, :], W['s5_glu_b'].wrap(W['s5_glu_b'].t[l, :].rearrange("(c p) -> p c", p=128)))
    gw = kk.sb([128, 2, 256], BF16, 'gw')
    for c in range(2):
        k.dma('pool', gw[:, c, :], W['s5_glu_w'][l, 128 * c:128 * c + 128, :])
    BW = 512
    blocks = [(t0_, min(BW, T - t0_)) for t0_ in range(0, T, BW)]
    yv = [kk.sb([128, 2, BW], F32, 'yv') for _ in range(2)]
    w1 = [kk.sb([128, 2, BW], F32, 'w1') for _ in range(2)]
    gl_ = [kk.sb([128, 2, BW], F32, 'gl') for _ in range(2)]
    glb = [kk.sb([128, 2, BW], BF16, 'glb') for _ in range(2)]
    sgm = [kk.sb([128, 2, BW], F32, 'sgm') for _ in range(2)]
    gt = [kk.sb([128, 2, BW], F32, 'gt') for _ in range(2)]
    ut = [kk.sb([128, 2, BW], F32, 'ut') for _ in range(2)]
    ystg = [kk.sb([128, 2, BW], BF16, 'ystg') for _ in range(2)]
    psO = [psXb, psU]
    psG = [psY, psU2]
    rg = FMROW[1536]
    nblk = len(blocks) if ntile is None else max(1, ntile // 4)
    def tbody(bk):
        b = bk % 2
        t0_, w_ = blocks[bk]
        tk = slice(t0_, t0_ + w_)
        ck = slice(t0_ // 8, (t0_ + w_) // 8)
        for c in range(2):
            k.dma('sp', gt[b][:, c, 0:w_], self.projFM[rg + 128 * c:rg + 128 * c + 128, tk])
            k.dma('sp', ut[b][:, c, 0:w_], self.projFM[r0 + 128 * c:r0 + 128 * c + 128, tk])
            po = psO[c]
            for l_ in range(8):
                for gl in range(8):
                    selT = selb[:, 8 + l_, 128 - 16 * gl:256 - 16 * gl]
                    k.mm(po.wrap(po.t[:, l_:w_:8]), selT, y8all[:, 8 * c + gl, ck],
                         start=(gl == 0), stop=(gl == 7))
        for c in range(2):
            k.stt('dve', yv[b][:, c, 0:w_], ut[b][:, c, 0:w_], Dt[:, c:c + 1], psO[c][:, 0:w_], ALU.mult, ALU.add)
        yield
        k.tt('pool', w1[b][:, :, 0:w_], yv[b][:, :, 0:w_], yv[b][:, :, 0:w_], ALU.mult)
        k.ts('dve', w1[b][:, :, 0:w_], w1[b][:, :, 0:w_], 0.044715, ALU.mult, 1.0, ALU.add)
        k.tt('dve', w1[b][:, :, 0:w_], w1[b][:, :, 0:w_], yv[b][:, :, 0:w_], ALU.mult)
        k.act(w1[b][:, :, 0:w_], w1[b][:, :, 0:w_], AF.Sigmoid, scale=1.5957691216057308)
        k.tt('dve', gl_[b][:, :, 0:w_], w1[b][:, :, 0:w_], yv[b][:, :, 0:w_], ALU.mult)
        k.cp('pool', glb[b][:, :, 0:w_], gl_[b][:, :, 0:w_])
        for oc in range(2):
            for c in range(2):
                k.mm(psG[oc][:, 0:w_], gw[:, c, oc * 128:(oc + 1) * 128], glb[b][:, c, 0:w_],
                     start=(c == 0), stop=(c == 1))
        for oc in range(2):
            k.act(sgm[b][:, oc, 0:w_], psG[oc][:, 0:w_], AF.Sigmoid, bias=gbT[:, oc:oc + 1])
        k.tt('dve', gl_[b][:, :, 0:w_], gl_[b][:, :, 0:w_], sgm[b][:, :, 0:w_], ALU.mult)
        k.act(gt[b][:, :, 0:w_], gt[b][:, :, 0:w_], AF.Silu)
        k.tt('dve', ystg[b][:, :, 0:w_], gl_[b][:, :, 0:w_], gt[b][:, :, 0:w_], ALU.mult)
        k.dma('act', self.ycatD.at('s5', (slice(None), slice(2, 4), tk)), ystg[b][:, :, 0:w_])
    for b0 in range(0, nblk, 2):
        run_interleaved([tbody(x_) for x_ in range(b0, min(b0 + 2, nblk))])
        yield
    yield ('DONE', es)


def phase_s5(self, l, ntile=None):
    for r in gen_s5(self, l, ntile):
        if isinstance(r, tuple):
            self.k.barrier_all()
            r[1].close()


Prog.phase_s5 = phase_s5


def phase_out(self, l, ntile=None):
    k, nc = self.k, self.nc
    es = ExitStack()
    kk = KBScope(k, es)
    last = (l == DEPTH - 1)
    wob = kk.sb([128, 8, D], BF16, 'wob')
    for kc in range(8):
        k.dma('pool', wob[:, kc, :], self.w['w_out'][l, kc * 128:(kc + 1) * 128, :])
    NB = 2
    hb = [kk.sb([128, D], F32, 'hb') for _ in range(NB)]
    yt = [kk.sb([128, 8, 128], BF16, 'yt') for _ in range(NB)]
    yn = [kk.sb([128, D], F32, 'yn') for _ in range(NB)]
    ssq = [kk.sb([128, 2], F32, 'ssq') for _ in range(NB)]
    rstd = [kk.sb([128, 1], F32, 'rstd') for _ in range(NB)]
    junk = kk.sb([128, 512], BF16, 'junk')
    epsb = kk.sb([128, 1], F32, 'epsb')
    k.memset('dve', epsb[:, :], EPS)
    py = [[kk.ps([128, 512], F32, 'py') for _ in range(2)] for _ in range(2)]
    tiles = list(range(NT)) if ntile is None else list(range(ntile))
    if last:
        tiles = [t for t in tiles if t >= NTC]
    for it, t in enumerate(tiles):
        b = it % NB
        s = 1 if t < NTC else 0
        tok = slice(t * 128, (t + 1) * 128)
        k.dma('sp', hb[b][:, :], self.h_src(l, t))
        k.dma('sp', yt[b][:, :, :], self.ycatD[:, :, tok])
        for nb in range(2):
            p = py[b][nb]
            for kc in range(8):
                k.mm(p[:, :], yt[b][:, kc, :], wob[:, kc, nb * 512:(nb + 1) * 512], start=(kc == 0), stop=(kc == 7))
            k.act(junk[:, :], p[:, :], AF.Square, accum_out=ssq[b][:, nb:nb + 1])
        k.tt('dve', rstd[b][:, :], ssq[b][:, 0:1], ssq[b][:, 1:2], ALU.add)
        k.act(rstd[b][:, :], rstd[b][:, :], AF.Sqrt, bias=epsb[:, :], scale=1.0 / D)
        k.recip(rstd[b][:, :], rstd[b][:, :])
        for nb in range(2):
            cs = slice(nb * 512, (nb + 1) * 512)
            k.stt('dve', yn[b][:, cs], py[b][nb][:, :], rstd[b][:, 0:1], self.Gbc[s][:, cs], ALU.mult, ALU.mult)
        k.tt('pool', yn[b][:, :], yn[b][:, :], hb[b][:, :], ALU.add)
        if last:
            dst = self.out[(t - NTC) * 128:(t - NTC + 1) * 128, :]
        else:
            dst = self.hs[l % 2][tok, :]
        k.dma('pool', dst, yn[b][:, :])
    k.barrier_all()
    es.close()


Prog.phase_out = phase_out


def gen_rwkv(self, l, ntile=None, NBX=2):
    k, nc = self.k, self.nc
    es = ExitStack()
    kk = KBScope(k, es)
    W = self.w
    ntile = NT if ntile is None else ntile
    def bc(name, idx, n):
        tl = kk.sb([128, n], F32, 'bc_' + name)
        k.dma('sp', tl[:, :], W[name].wrap(idx.partition_broadcast(128)))
        return tl
    mu_bc = bc('rw_mu', W['rw_mu'].t[l, :], 1024)
    w0_bc = [bc('rw_w0', W['rw_w0'].t[l, d, :], 256) for d in range(2)]
    a0_bc = [bc('rw_a0', W['rw_a0'].t[l, d, :], 256) for d in range(2)]
    kk_bc = bc('rw_kk', W['rw_kk'].t[l, :], 256)
    ka_bc = bc('rw_ka', W['rw_ka'].t[l, :], 256)
    rk_bc = bc('rw_rk', W['rw_rk'].t[l].rearrange("a b -> (a b)"), 256)
    lnw_bc = bc('rw_ln_w', W['rw_ln_w'].t[l, :], 256)
    lnb_bc = bc('rw_ln_b', W['rw_ln_b'].t[l, :], 256)
    w2 = kk.sb([64, 2, 256], F32, 'w2'); a2 = kk.sb([64, 2, 256], F32, 'a2')
    for d in range(2):
        k.dma('sp', w2[:, d, :], W['rw_w2'][l, d, :, :])
        k.dma('sp', a2[:, d, :], W['rw_a2'][l, d, :, :])
    yfD = self.yfD
    bsum = kk.sb([128, NT, 4], F32, 'bsum')
    Sf = [kk.sb([128, 128], F32, 'Sf') for _ in range(2)]
    import os
    RDT0 = F32 if os.environ.get('RW_gA', '16') == '32' else BF16
    SBDT = F32 if os.environ.get('RW_gR', '16') == '32' else BF16
    Sb = [kk.sb([128, 128], SBDT, 'Sb') for _ in range(2)]
    epsb = kk.sb([128, 1], F32, 'epsb'); k.memset('dve', epsb[:, :], EPS)
    gneps = kk.sb([128, 1], F32, 'gneps'); k.memset('dve', gneps[:, :], 64e-5)
    import os
    RDT = F32 if os.environ.get('RW_F32', '1') == '1' else BF16
    NB = NBX
    RWDEF = {'gA': '16', 'gD': '32', 'gR': '16'}
    def mk(shape, dt=F32, name='b'):
        if dt == BF16 and name != 'keepbf':
            dt = F32 if os.environ.get('RW_' + name, RWDEF.get(name, '32')) == '32' else BF16
        return [kk.sb(shape, dt, name) for _ in range(NB)]
    def mk1(shape, dt=F32, name='b1'):
        t_ = kk.sb(shape, dt, name)
        return [t_ for _ in range(NB)]
    z = mk([128, 1024]); zs = mk([128, 1024]); zm = mk([128, 1024]); gate = mk([128, 256])
    for b in range(NB):
        k.memset('pool', zs[b][:, :], 0.0)
    lT = mk([64, 2, 128]); logw = mk([128, 256]); av = mk([128, 256]); kkn = mk([128, 256]); t256 = mk([128, 256])
    ssq = mk([128, 4]); kd = mk([128, 256]); bv = mk([128, 256]); Pp = mk([128, 256]); Pinv = mk([128, 256])
    Pex = mk([128, 256]); Dk = mk([128, 256]); pc = mk([128, 2])
    TMb = mk([128, 4, 256], BF16, 'gA')
    Bh = mk([128, 256], BF16, 'gR'); Kh = mk([128, 256], BF16, 'gR'); vbf = mk([128, 256], BF16, 'gR')
    FMt = mk([128, 2, 6, 128], BF16, 'gA')
    Pm = mk([128, 4, 128], BF16, 'gD'); PTm = mk([128, 4, 128], BF16, 'gD'); acc = mk([128, 4, 128], BF16, 'gD')
    Pm2 = mk([128, 4, 128], BF16, 'gD'); PTm2 = mk([128, 4, 128], BF16, 'gD')
    AakT = mk([128, 4, 128], BF16, 'gR'); ArbT = mk([128, 4, 128], BF16, 'gR'); ArkT = mk([128, 4, 128], BF16, 'gR')
    AG = mk([128, 4, 128], BF16, 'gD'); Wm = mk([128, 256], BF16, 'gR'); U0 = mk([128, 256], BF16, 'gR'); QT = mk([128, 4, 128], BF16, 'gR')
    Mf = mk([128, 2, 128], F32)
    ystg = mk([128, 2, 128], BF16, 'keepbf')
    ysb = mk([128, 256], BF16, 'keepbf')
    ysum = mk([128, 256]); st4 = mk([128, 4]); yc = mk([128, 256]); sg = mk([128, 256]); yo = mk([128, 256], BF16, 'keepbf')
    NBK = 3 if NBX <= 2 else 2
    banks = [kk.ps([128, 4, 128], F32, 'bk') for _ in range(NB * NBK)]
    pbT = kk.ps([128, 8, 128], BF16, 'pbT')
    pfT = [kk.ps([128, 4, 128], F32, 'pfT') for _ in range(2)] if RDT0 == F32 else None
    bi = [0] * NBX
    assert RDT0 == BF16
    def bankb(b):
        bi[b] += 1
        return banks[NBK * b + bi[b] % NBK]
    bm2 = self.C('bm2'); ident = self.ident; identb = self.identb[:, :]
    F32R = mybir.dt.float32r
    USE_R = os.environ.get('RW_F32R', '0') == '1'
    def rr(a):
        return A(a.ap.bitcast(F32R), a.tok) if USE_R else a
    def bc4(a, n=64):
        return A(a.ap.unsqueeze(2).to_broadcast([128, 4, n]), a.tok)
    def v4(a):
        return A(a.ap.rearrange("p (h k) -> p h k", h=4), a.tok)
    def bch(m):
        return A(m.ap.unsqueeze(1).to_broadcast([128, 4, 128]), m.tok)
    it = 0
    yield
    for d in range(2):
        order = [t for t in _chain_order(d) if t < ntile]
        TRI = self.C('LE') if d == 0 else self.C('GE')
        STRI = self.C('LT') if d == 0 else self.C('GT')
        STRI_T = self.C('GT') if d == 0 else self.C('LT')
        for c in range(2):
            k.memset('dve', Sf[c][:, :], 0.0)
            k.memset('dve', Sb[c][:, :], 0.0)
        def body(t, b):
            def bank():
                return bankb(b)
            tok = slice(t * 128, (t + 1) * 128)
            isctx = t < NTC
            lo, hi = (0, CTXL) if isctx else (CTXL, T)
            a0_ = t * 128
            k.dma('sp', z[b][:, :], self.projTM[tok, 0:1024])
            if d == 1:
                k.dma('sp', gate[b][:, :], self.projTM[tok, 1024:1280])
                k.dma('sp', ysb[b][:, :], yfD.at(t, (t, slice(None), slice(None))))
            def shl(cols, sh):
                r0_, r1_ = a0_ + sh, a0_ + sh + 128
                p0 = max(0, lo - r0_); p1 = 128 - max(0, r1_ - hi)
                if p0 > 0 or p1 < 128:
                    if abs(sh) == 1:
                        pass
                    if p0 > 0:
                        k.memset('pool', zs[b][0:64 if p0 == 64 else 32, cols], 0.0)
                    if p1 < 128:
                        k.memset('pool', zs[b][64:128, cols] if p1 == 64 else zs[b][96:128, cols], 0.0)
                k.dma('sp', zs[b][p0:p1, cols], self.projTM[r0_ + p0:r0_ + p1, cols])
            if isctx:
                shl(slice(0, 512), -1)
                shl(slice(512, 1024), 1)
                segs = [(slice(0, 512), None), (slice(512, 1024), None)]
            else:
                shl(slice(0, 256), -1)
                shl(slice(256, 512), 1)
                shl(slice(512, 768), -64)
                shl(slice(768, 1024), 64)
                segs = [(slice(0, 256), 0), (slice(256, 512), 1), (slice(512, 768), None), (slice(768, 1024), None)]
            for cols, mi in segs:
                if mi is None:
                    k.tt('dve', zm[b][:, cols], zs[b][:, cols], z[b][:, cols], ALU.subtract)
                else:
                    k.stt('dve', zm[b][:, cols], zs[b][:, cols], self.C('mL', mi, mi + 1), z[b][:, cols], ALU.mult, ALU.subtract)
            k.tt('pool', zm[b][:, :], zm[b][:, :], mu_bc[:, :], ALU.mult)
            k.tt('pool', zm[b][:, :], zm[b][:, :], z[b][:, :], ALU.add)
            r_ = zm[b][:, 0:256]; k_ = zm[b][:, 256:512]; v_ = zm[b][:, 512:768]
            yield
            pl = bank()
            k.tr(pl[0:64, 0, :], zm[b][:, 768 + 64 * d:768 + 64 * d + 64], ident)
            k.tr(pl[0:64, 1, :], zm[b][:, 896 + 64 * d:896 + 64 * d + 64], ident)
            k.act(lT[b][:, 0, :], pl[0:64, 0, :], AF.Tanh)
            k.cp('act', lT[b][:, 1, :], pl[0:64, 1, :])
            pw_ = bank()
            pwv = pw_.wrap(pw_.t[:, :, :].rearrange("p a b -> p (a b)"))
            k.mm(pw_.wrap(pw_.t[:, 0:2, :].rearrange("p a b -> p (a b)")), lT[b][:, 0, :], w2[:, d, :])
            k.mm(pw_.wrap(pw_.t[:, 2:4, :].rearrange("p a b -> p (a b)")), lT[b][:, 1, :], a2[:, d, :])
            k.tt('dve', logw[b][:, :], pw_.wrap(pw_.t[:, 0:2, :].rearrange("p a b -> p (a b)")), w0_bc[d][:, :], ALU.add)
            k.act(logw[b][:, :], logw[b][:, :], AF.Sigmoid)
            k.amul(logw[b][:, :], logw[b][:, :], -0.6065306597126334)
            k.tt('dve', av[b][:, :], pw_.wrap(pw_.t[:, 2:4, :].rearrange("p a b -> p (a b)")), a0_bc[d][:, :], ALU.add)
            k.act(av[b][:, :], av[b][:, :], AF.Sigmoid)
            yield
            k.tt('dve', kkn[b][:, :], k_, kk_bc[:, :], ALU.mult)
            k.tt('pool', t256[b][:, :], kkn[b][:, :], kkn[b][:, :], ALU.mult)
            k.reduce('dve', ssq[b][:, :], v4(t256[b][:, :]), ALU.add)
            k.act(ssq[b][:, :], ssq[b][:, :], AF.Sqrt, bias=epsb[:, :])
            k.recip(ssq[b][:, :], ssq[b][:, :])
            k.tt('dve', v4(kkn[b][:, :]), v4(kkn[b][:, :]), bc4(ssq[b][:, :]), ALU.mult)
            k.stt('dve', kd[b][:, :], av[b][:, :], -1.0, ka_bc[:, :], ALU.add, ALU.mult)
            k.stt('dve', kd[b][:, :], kd[b][:, :], 1.0, k_, ALU.add, ALU.mult)
            k.tt('pool', bv[b][:, :], kkn[b][:, :], av[b][:, :], ALU.mult)
            k.tt('pool', t256[b][:, :], r_, kd[b][:, :], ALU.mult)
            k.tt('pool', t256[b][:, :], t256[b][:, :], rk_bc[:, :], ALU.mult)
            k.reduce('dve', st4[b][:, :], v4(t256[b][:, :]), ALU.add)
            if d == 0:
                k.cp('pool', bsum[:, t, :], st4[b][:, :])
            yield
            pL = bank()
            pLv = pL.wrap(pL.t[:, 0:2, :].rearrange("p a b -> p (a b)"))
            pDv = pL.wrap(pL.t[:, 2:4, :].rearrange("p a b -> p (a b)"))
            k.mm(pLv, TRI, logw[b][:, :])
            k.mm(pDv, STRI_T, logw[b][:, :])
            pc_ps = bank()
            for c in range(2):
                k.mm(pc_ps[:, 0, c:c + 1], logw[b][:, 128 * c:128 * c + 128], self.C('ones', 0, 1))
            k.act(Pp[b][:, :], pLv, AF.Exp)
            k.act(Pinv[b][:, :], pLv, AF.Exp, scale=-1.0)
            k.tt('dve', Pex[b][:, :], pLv, logw[b][:, :], ALU.subtract)
            k.act(Pex[b][:, :], Pex[b][:, :], AF.Exp)
            k.act(Dk[b][:, :], pDv, AF.Exp)
            k.act(pc[b][:, :], pc_ps[:, 0, 0:2], AF.Exp)
            yield
            k.stt('dve', TMb[b][:, 0, :], kkn[b][:, :], -1.0, Pex[b][:, :], ALU.mult, ALU.mult)
            k.tt('dve', TMb[b][:, 1, :], bv[b][:, :], Pinv[b][:, :], ALU.mult)
            k.tt('pool', TMb[b][:, 2, :], kd[b][:, :], Pinv[b][:, :], ALU.mult)
            k.tt('pool', TMb[b][:, 3, :], r_, Pp[b][:, :], ALU.mult)
            k.tt('dve', Bh[b][:, :], bv[b][:, :], Dk[b][:, :], ALU.mult)
            k.tt('pool', Kh[b][:, :], kd[b][:, :], Dk[b][:, :], ALU.mult)
            k.cp('pool', vbf[b][:, :], v_)
            yield
            for c in range(2):
                if RDT0 == F32:
                    src_ = pfT[c]; o_ = 0; idn = ident
                else:
                    src_ = pbT; o_ = 4 * c; idn = identb
                for q in range(4):
                    k.tr(src_[:, o_ + q, :], TMb[b][:, q, 128 * c:128 * c + 128], idn)
                k.cp('act', FMt[b][:, c, 0, :], src_[:, o_ + 0, :])
                k.cp('act', FMt[b][:, c, 1, :], src_[:, o_ + 3, :])
                for hl in range(2):
                    k.ts('dve', FMt[b][:, c, 2 + hl, :], src_[:, o_ + 1, :], self.C('hm2', hl, hl + 1), ALU.mult)
                    k.ts('dve', FMt[b][:, c, 4 + hl, :], src_[:, o_ + 2, :], self.C('hm2', hl, hl + 1), ALU.mult)
            yield
            def amat(lq, rq, lz, rz):
                p = bank()
                for h in range(4):
                    c, hl = h // 2, h % 2
                    lhs = FMt[b][:, c, lq + (hl if lz else 0), :]
                    rhs = FMt[b][:, c, rq + (hl if rz else 0), :]
                    k.mm(p[:, h, :], lhs, rhs)
                return p
            def amat2(lq):
                pbs = [bank(), bank()]
                for h in range(4):
                    c, hl = h // 2, h % 2
                    rhs2 = FMt[b].wrap(FMt[b].t[:, c, 0:2, :].rearrange("p a b -> p (a b)"))
                    out2 = pbs[h // 2].wrap(pbs[h // 2].t[:, 2 * (h % 2):2 * (h % 2) + 2, :].rearrange("p a b -> p (a b)"))
                    k.mm(out2, FMt[b][:, c, lq + hl, :], rhs2)
                return pbs
            def bc2(m):
                return A(m.ap.unsqueeze(1).to_broadcast([128, 2, 128]), m.tok)
            pbs = amat2(2)
            for hb in range(2):
                pv = pbs[hb].wrap(pbs[hb].t[:, :, :].rearrange("p (h q) n -> p h q n", q=2))
                k.tt('dve', Pm[b][:, 2 * hb:2 * hb + 2, :], pv.tok and A(pv.ap[:, :, 0, :], pv.tok), bc2(STRI), ALU.mult)
                k.tt('dve', ArbT[b][:, 2 * hb:2 * hb + 2, :], A(pv.ap[:, :, 1, :], pv.tok), bc2(TRI), ALU.mult)
            k.tt('pool', acc[b][:, :, :], Pm[b][:, :, :], bch(ident), ALU.add)
            yield
            p = amat(0, 2, False, True)
            k.tt('dve', PTm[b][:, :, :], p[:, :, :], bch(STRI_T), ALU.mult)
            yield
            pbs = amat2(4)
            for hb in range(2):
                pv = pbs[hb].wrap(pbs[hb].t[:, :, :].rearrange("p (h q) n -> p h q n", q=2))
                k.tt('dve', AakT[b][:, 2 * hb:2 * hb + 2, :], A(pv.ap[:, :, 0, :], pv.tok), bc2(STRI), ALU.mult)
                k.tt('dve', ArkT[b][:, 2 * hb:2 * hb + 2, :], A(pv.ap[:, :, 1, :], pv.tok), bc2(TRI), ALU.mult)
            P_, PT_, P2_, PT2_ = Pm[b], PTm[b], Pm2[b], PTm2[b]
            for q in range(1, 7):
                pa_ = bank()
                for h in range(4):
                    k.mm(pa_[:, h, :], rr(P_[:, h, :]), rr(PT_[:, h, :]))
                k.cp('act', PT2_[:, :, :], pa_[:, :, :])
                if q < 6:
                    pb_ = bank()
                    for h in range(4):
                        k.mm(pb_[:, h, :], rr(PT_[:, h, :]), rr(P_[:, h, :]))
                    k.cp('act', P2_[:, :, :], pb_[:, :, :])
                pc_ = bank()
                for h in range(4):
                    k.mm(pc_[:, h, :], rr(PT2_[:, h, :]), rr(acc[b][:, h, :]))
                k.tt('dve', acc[b][:, :, :], pc_[:, :, :], acc[b][:, :, :], ALU.add)
                P_, P2_ = P2_, P_
                yield
                PT_, PT2_ = PT2_, PT_
            TT = acc[b]
            yield
            pg = bank()
            for h in range(4):
                k.mm(pg[:, h, 0:64], AakT[b][:, h, :], vbf[b][:, 64 * h:64 * h + 64])
            for h in range(4):
                k.cp('pool', AG[b][:, h, 0:64], TMb[b][:, 0, 64 * h:64 * h + 64])
            k.cp('act', AG[b][:, :, 64:128], pg[:, :, 0:64])
            pwu = bank()
            for h in range(4):
                k.mm(pwu[:, h, :], rr(TT[:, h, :]), rr(AG[b][:, h, :]))
            k.cp('act', v4(Wm[b][:, :]), pwu[:, :, 0:64])
            k.cp('act', v4(U0[b][:, :]), pwu[:, :, 64:128])
            yield
            pq = bank()
            for h in range(4):
                c = h // 2
                k.mm(pq[:, h, :], Wm[b][:, 128 * c:128 * c + 128], ArbT[b][:, h, :])
            for h in range(4):
                c = h // 2
                k.tt('dve', QT[b][:, h, :], pq[:, h, :], FMt[b][:, c, 1, :], ALU.add)
            yield 'STATE'
            py = bank()
            for h in range(4):
                c, hl = h // 2, h % 2
                k.mm(py[:, h, 0:64], QT[b][:, h, :], Sb[c][:, 64 * hl:64 * hl + 64], start=True, stop=False)
                k.mm(py[:, h, 0:64], ArbT[b][:, h, :], U0[b][:, 64 * h:64 * h + 64], start=False, stop=False)
                k.mm(py[:, h, 0:64], ArkT[b][:, h, :], vbf[b][:, 64 * h:64 * h + 64], start=False, stop=True)
            if d == 0:
                k.cp('act', v4(ysb[b][:, :]), py[:, :, 0:64])
                k.dma('act', yfD.at(t, (t, slice(None), slice(None))), ysb[b][:, :])
            else:
                k.tt('dve', v4(ysum[b][:, :]), py[:, :, 0:64], v4(ysb[b][:, :]), ALU.add)
                k.reduce('dve', ssq[b][:, :], v4(ysum[b][:, :]), ALU.add)
                k.ts('dve', ssq[b][:, :], ssq[b][:, :], 1.0 / 64, ALU.mult)
                k.tt('dve', v4(yc[b][:, :]), v4(ysum[b][:, :]), bc4(ssq[b][:, :]), ALU.subtract)
                k.tt('pool', t256[b][:, :], yc[b][:, :], yc[b][:, :], ALU.mult)
                k.reduce('dve', ssq[b][:, :], v4(t256[b][:, :]), ALU.add)
                k.act(ssq[b][:, :], ssq[b][:, :], AF.Sqrt, bias=gneps[:, :], scale=1.0 / 64)
                k.recip(ssq[b][:, :], ssq[b][:, :])
                k.tt('dve', v4(yc[b][:, :]), v4(yc[b][:, :]), bc4(ssq[b][:, :]), ALU.mult)
                k.tt('pool', yc[b][:, :], yc[b][:, :], lnw_bc[:, :], ALU.mult)
                k.tt('pool', yc[b][:, :], yc[b][:, :], lnb_bc[:, :], ALU.add)
                k.tt('dve', st4[b][:, :], st4[b][:, :], bsum[:, t, :], ALU.add)
                k.tt('dve', v4(t256[b][:, :]), v4(v_), bc4(st4[b][:, :]), ALU.mult)
                k.tt('pool', yc[b][:, :], yc[b][:, :], t256[b][:, :], ALU.add)
                k.act(sg[b][:, :], gate[b][:, :], AF.Silu)
                k.tt('dve', yo[b][:, :], yc[b][:, :], sg[b][:, :], ALU.mult)
                for c in range(2):
                    k.tr(pbT[:, c, :], yo[b][:, c * 128:(c + 1) * 128], identb)
                k.cp('act', ystg[b][:, :, :], pbT[:, 0:2, :])
                k.dma('act', self.ycatD.at('rwkv', (slice(None), slice(0, 2), tok)), ystg[b][:, :, :])
            yield
            for c in range(2):
                pm_ = bank()
                k.mm(pm_[:, 0, :], Wm[b][:, 128 * c:128 * c + 128], Bh[b][:, 128 * c:128 * c + 128])
                k.tt('dve', Mf[b][:, c, :], pm_[:, 0, :], bm2, ALU.mult)
                k.stt('dve', Mf[b][:, c, :], ident, pc[b][:, c:c + 1], Mf[b][:, c, :], ALU.mult, ALU.add)
                ps_ = bank()
                k.mm(ps_[:, 0, :], Mf[b][:, c, :], Sf[c][:, :], start=True, stop=False)
                k.mm(ps_[:, 0, :], Bh[b][:, 128 * c:128 * c + 128], U0[b][:, 128 * c:128 * c + 128], start=False, stop=False)
                k.mm(ps_[:, 0, :], Kh[b][:, 128 * c:128 * c + 128], vbf[b][:, 128 * c:128 * c + 128], start=False, stop=True)
                k.tt('dve', Sf[c][:, :], ps_[:, 0, :], bm2, ALU.mult)
                k.cp('act', Sb[c][:, :], Sf[c][:, :])
        if NB == 1:
            for t_ in order:
                for _r in body(t_, 0):
                    yield
        else:
            for i_ in range(0, len(order), NB):
                run_interleaved([body(t_, b_) for b_, t_ in enumerate(order[i_:i_ + NB])])
                yield
    yield ('DONE', es)


def phase_rwkv(self, l, ntile=None):
    for r in gen_rwkv(self, l, ntile):
        if isinstance(r, tuple):
            self.k.barrier_all()
            r[1].close()


def phase_rwkv_s5(self, l, ntile=None, ratio=None):
    ratio = int(os.environ.get('RATIO', '40')) if ratio is None else ratio
    g1 = gen_rwkv(self, l, None, NBX=1)
    next(g1)
    g2 = gen_s5(self, l)
    d1 = d2 = None
    while d1 is None or d2 is None:
        if d1 is None:
            for _ in range(ratio):
                r = next(g1)
                if isinstance(r, tuple):
                    d1 = r[1]
                    break
        if d2 is None:
            for _ in range(int(os.environ.get('S5STEP', '6'))):
                r = next(g2)
                if isinstance(r, tuple):
                    d2 = r[1]
                    break
    self.k.barrier_all()
    d2.close()
    d1.close()


Prog.phase_rwkv_s5 = phase_rwkv_s5
Prog.phase_rwkv = phase_rwkv


def build_program(nc, es, layers=DEPTH):
    P = Prog(nc, es, dbg=set())
    for l in range(layers):
        P.phase_mod(l)
        P.phase_proj(l)
        P.phase_rwkv(l)
        P.phase_ssd(l)
        P.phase_gla_s5(l)
        P.phase_out(l)
    P.k.barrier_all()
    return P


def make_in_maps(inputs, cores):
    maps = []
    for b in cores:
        m = {'x': np.ascontiguousarray(inputs['x'][b], dtype=np.float32),
             'ctx': np.ascontiguousarray(inputs['ctx'][b], dtype=np.float32),
             'c2': np.ascontiguousarray(np.stack([inputs['c'][b], inputs['c_ctx']]), dtype=np.float32),
             'consts': CONSTS, 'selc': SELC}
        for n in WNAMES:
            m[n] = np.ascontiguousarray(inputs[n], dtype=np.float32)
        maps.append(m)
    return maps


def kernel(**inputs):
    inputs = {k_: np.asarray(v_) for k_, v_ in inputs.items()}
    nc = bass.Bass("TRN2", target_bir_lowering=False)
    with ExitStack() as es:
        build_program(nc, es)
    n = 8
    res = run_bass_kernel_spmd(nc, make_in_maps(inputs, list(range(n))), core_ids=list(range(n)))
    out = np.stack([np.asarray(res.results[i]['out'], dtype=np.float32) for i in range(n)], axis=0)
    return out


def phase_rwkv1(self, l, ntile=None):
    for r in gen_rwkv(self, l, ntile, NBX=1):
        if isinstance(r, tuple):
            self.k.barrier_all()
            r[1].close()


Prog.phase_rwkv1 = phase_rwkv1


def phase_s5hi(self, l, ntile=None):
    es = ExitStack()
    kk = KBScope(self.k, es)
    dummy = [kk.ps([128, 512], F32, 'dummy') for _ in range(4)]
    phase_s5(self, l, ntile)
    es.close()


Prog.phase_s5hi = phase_s5hi


def phase_rwkv1hi(self, l, ntile=None):
    es = ExitStack()
    kk = KBScope(self.k, es)
    dummy = [kk.ps([128, 512], F32, 'dummy') for _ in range(4)]
    phase_rwkv1(self, l, ntile)
    es.close()


Prog.phase_rwkv1hi = phase_rwkv1hi
```

```python
import os
import numpy as np
import concourse.bass as bass
import concourse.mybir as mybir

F32, BF16 = mybir.dt.float32, mybir.dt.bfloat16
AF = mybir.ActivationFunctionType
ALU = mybir.AluOpType
AX = mybir.AxisListType
NDS = 24


class Tok:
    __slots__ = ('w', 'r')

    def __init__(self):
        self.w = None
        self.r = {}


class A:
    __slots__ = ('ap', 'tok')

    def __init__(self, ap, tok):
        self.ap = ap
        self.tok = tok


class Buf:
    def __init__(self, t):
        self.t = t
        self.tok = Tok()
        self.sub = {}

    def __getitem__(self, idx):
        return A(self.t[idx], self.tok)

    def at(self, key, idx):
        return A(self.t[idx], self.sub.setdefault(key, Tok()))

    def wrap(self, ap, key=None):
        return A(ap, self.tok if key is None else self.sub.setdefault(key, Tok()))


def _ap(x):
    return x.ap if isinstance(x, A) else x


class KB:
    def __init__(self, nc, es):
        self.nc, self.es = nc, es
        self.E = {'pe': nc.tensor, 'act': nc.scalar, 'dve': nc.vector, 'pool': nc.gpsimd, 'sp': nc.sync}
        self.sems, self.cnt = {}, {}
        self.seen = {e: {} for e in self.E}
        for e in self.E:
            self._mksem(e)
        self.dq = {'hw': [], 'sw': []}
        for i in range(NDS):
            self._mksem('d%d' % i)
            self.dq['hw'].append('d%d' % i)
        for i in range(8):
            self._mksem('q%d' % i)
            self.dq['sw'].append('q%d' % i)
        self.dq['hwa'] = []
        for i in range(12):
            self._mksem('a%d' % i)
            self.dq['hwa'].append('a%d' % i)
        self.dq_i = {'hw': 0, 'sw': 0, 'hwa': 0}
        self.nid = 0
        self.ninst = 0

    def _mksem(self, key):
        self.sems[key] = self.es.enter_context(self.nc.semaphore('s_' + key))
        self.cnt[key] = 0

    def sb(self, shape, dt=F32, name=None):
        self.nid += 1
        t = self.es.enter_context(self.nc.sbuf_tensor('%s_%d' % (name or 'sb', self.nid), list(shape), dt))
        return Buf(t)

    def ps(self, shape, dt=F32, name=None):
        self.nid += 1
        t = self.es.enter_context(self.nc.psum_tensor('%s_%d' % (name or 'ps', self.nid), list(shape), dt))
        return Buf(t)

    def dram(self, name, shape, dt=F32, kind="Internal"):
        t = self.nc.dram_tensor(name, list(shape), dt, kind=kind)
        b = Buf(t.ap())
        return b

    def _wait(self, eng, key, val):
        if self.seen[eng].get(key, 0) >= val:
            return
        self.E[eng].wait_ge(self.sems[key], val)
        self.seen[eng][key] = val
        self.ninst += 1

    def _deps(self, R, W):
        evs = {}
        for t in R:
            if t.w is not None and evs.get(t.w[0], 0) < t.w[1]:
                evs[t.w[0]] = t.w[1]
        for t in W:
            if t.w is not None and evs.get(t.w[0], 0) < t.w[1]:
                evs[t.w[0]] = t.w[1]
            for k, v in t.r.items():
                if evs.get(k, 0) < v:
                    evs[k] = v
        return evs

    def _toks(self, xs):
        out = []
        for x in xs:
            if isinstance(x, A):
                if x.tok not in out:
                    out.append(x.tok)
            elif isinstance(x, Tok):
                if x not in out:
                    out.append(x)
        return out

    def op(self, eng, fn, R=(), W=()):
        R = self._toks(R)
        W = self._toks(W)
        evs = self._deps(R, W)
        for k, v in evs.items():
            if k == 'pe' and eng == 'pe':
                continue
            self._wait(eng, k, v)
        ins = fn(self.E[eng])
        self.cnt[eng] += 1
        ins.then_inc(self.sems[eng], 1)
        self.ninst += 1
        n = self.cnt[eng]
        for t in R:
            t.r[eng] = n
        for t in W:
            t.w = (eng, n)
            t.r = {}
        return ins

    def dma(self, q, out, in_, **kw):
        R = self._toks([in_])
        W = self._toks([out])
        cls = 'sw' if q == 'pool' else ('hwa' if q == 'act' else 'hw')
        key = self.dq[cls][self.dq_i[cls]]
        self.dq_i[cls] = (self.dq_i[cls] + 1) % len(self.dq[cls])
        evs = self._deps(R, W)
        if self.cnt[key] > 0:
            evs[key] = max(evs.get(key, 0), self.cnt[key])
        for k, v in evs.items():
            self._wait(q, k, v)
        ins = self.E[q].dma_start(out=_ap(out), in_=_ap(in_), **kw)
        self.cnt[key] += 16
        ins.then_inc(self.sems[key], 16)
        self.ninst += 1
        n = self.cnt[key]
        for t in R:
            t.r[key] = n
        for t in W:
            t.w = (key, n)
            t.r = {}
        return ins

    def finish(self, toks):
        for t in self._toks(toks):
            if t.w is not None:
                self._wait('sp', t.w[0], t.w[1])

    def barrier_all(self):
        for e in self.E:
            for k, v in self.cnt.items():
                if v > 0 and not (k == e):
                    self._wait(e, k, v)

    def tt(self, eng, out, in0, in1, op):
        return self.op(eng, lambda e: e.tensor_tensor(out=_ap(out), in0=_ap(in0), in1=_ap(in1), op=op),
                       R=[in0, in1], W=[out])

    def ts(self, eng, out, in0, s1, op0, s2=None, op1=None, accum_out=None):
        kw = {}
        if op1 is not None:
            kw['op1'] = op1
        if accum_out is not None:
            kw['accum_out'] = _ap(accum_out)
        return self.op(eng, lambda e: e.tensor_scalar(out=_ap(out), in0=_ap(in0), scalar1=_ap(s1), scalar2=_ap(s2),
                                                      op0=op0, **kw),
                       R=[in0, s1, s2], W=[out, accum_out])

    def stt(self, eng, out, in0, scalar, in1, op0, op1):
        return self.op(eng, lambda e: e.scalar_tensor_tensor(out=_ap(out), in0=_ap(in0), scalar=_ap(scalar),
                                                             in1=_ap(in1), op0=op0, op1=op1),
                       R=[in0, scalar, in1], W=[out])

    def act(self, out, in_, func, bias=None, scale=None, accum_out=None, eng='act'):
        kw = {}
        if bias is not None:
            kw['bias'] = _ap(bias)
        if scale is not None:
            kw['scale'] = _ap(scale)
        if accum_out is not None:
            kw['accum_out'] = _ap(accum_out)
        return self.op(eng, lambda e: e.activation(out=_ap(out), in_=_ap(in_), func=func, **kw),
                       R=[in_, bias, scale], W=[out, accum_out])

    def amul(self, out, in_, scale):
        return self.act(out, in_, AF.Copy, scale=scale)

    def mm(self, out, lhsT, rhs, start=True, stop=True):
        return self.op('pe', lambda e: e.matmul(_ap(out), _ap(lhsT), _ap(rhs), start=start, stop=stop),
                       R=[lhsT, rhs], W=[out])

    def tr(self, out, in_, ident):
        return self.op('pe', lambda e: e.transpose(_ap(out), _ap(in_), _ap(ident)),
                       R=[in_, ident], W=[out])

    def cp(self, eng, out, in_):
        if eng == 'act':
            return self.op(eng, lambda e: e.copy(out=_ap(out), in_=_ap(in_)), R=[in_], W=[out])
        return self.op(eng, lambda e: e.tensor_copy(out=_ap(out), in_=_ap(in_)), R=[in_], W=[out])

    def memset(self, eng, out, val):
        return self.op(eng, lambda e: e.memset(_ap(out), val), R=[], W=[out])

    def recip(self, out, in_, eng='dve'):
        return self.op(eng, lambda e: e.reciprocal(out=_ap(out), in_=_ap(in_)), R=[in_], W=[out])

    def reduce(self, eng, out, in_, op, axis=AX.X):
        return self.op(eng, lambda e: e.tensor_reduce(out=_ap(out), in_=_ap(in_), op=op, axis=axis),
                       R=[in_], W=[out])

    def scan(self, out, d0, d1, initial, op0=ALU.mult, op1=ALU.add):
        return self.op('dve', lambda e: e.tensor_tensor_scan(out=_ap(out), data0=_ap(d0), data1=_ap(d1),
                                                              initial=_ap(initial), op0=op0, op1=op1),
                       R=[d0, d1, initial], W=[out])

from contextlib import ExitStack
from concourse.bass_utils import run_bass_kernel_spmd

D = 1024; SEQ = 4096; CTXL = 256; T = SEQ + CTXL; NT = T // 128; NTC = CTXL // 128
NIN = 3368; DEPTH = 2; EPS = 1e-6
FMBLK = [(1280, 128), (1408, 128), (1536, 128), (1664, 128),
         (1792, 128), (1920, 128), (2048, 128), (2176, 128),
         (2568, 128), (2696, 128), (3080, 32)]
FMROW = {}
_r = 0
for _c, _n in FMBLK:
    FMROW[_c] = _r
    _r += _n
NFM = _r

WNAMES = ['ada_w', 'ada_b', 'norm_pre', 'norm_post', 'w_in', 'w_out', 'rw_mu', 'rw_w0', 'rw_w2', 'rw_a0', 'rw_a2',
          'rw_kk', 'rw_ka', 'rw_rk', 'rw_ln_w', 'rw_ln_b', 's5_a_re', 's5_a_im', 's5_log_dt', 's5_b_re', 's5_b_im',
          's5_c_re', 's5_c_im', 's5_d', 's5_glu_w', 's5_glu_b', 'ssd_conv_w', 'ssd_conv_b', 'ssd_dt_bias',
          'ssd_a_log', 'ssd_d', 'ssd_norm', 'gla_g2', 'gla_gb', 'gla_norm']
WSHAPES = {'ada_w': (2, 1024, 3072), 'ada_b': (2, 3072), 'norm_pre': (2, 1024), 'norm_post': (2, 1024),
           'w_in': (2, 1024, 3368), 'w_out': (2, 1024, 1024), 'rw_mu': (2, 1024), 'rw_w0': (2, 2, 256),
           'rw_w2': (2, 2, 64, 256), 'rw_a0': (2, 2, 256), 'rw_a2': (2, 2, 64, 256), 'rw_kk': (2, 256),
           'rw_ka': (2, 256), 'rw_rk': (2, 4, 64), 'rw_ln_w': (2, 256), 'rw_ln_b': (2, 256),
           's5_a_re': (2, 2, 16, 64), 's5_a_im': (2, 2, 16, 64), 's5_log_dt': (2, 2, 16),
           's5_b_re': (2, 2, 16, 64, 16), 's5_b_im': (2, 2, 16, 64, 16), 's5_c_re': (2, 2, 16, 16, 64),
           's5_c_im': (2, 2, 16, 16, 64), 's5_d': (2, 256), 's5_glu_w': (2, 256, 256), 's5_glu_b': (2, 256),
           'ssd_conv_w': (2, 5, 512), 'ssd_conv_b': (2, 512), 'ssd_dt_bias': (2, 2, 4), 'ssd_a_log': (2, 2, 4),
           'ssd_d': (2, 4), 'ssd_norm': (2, 256), 'gla_g2': (2, 2, 16, 128), 'gla_gb': (2, 2, 128),
           'gla_norm': (2, 256)}


def make_consts():
    p = np.arange(128)[:, None]
    f = np.arange(128)[None, :]
    c = {}
    c['ident'] = np.eye(128, dtype=np.float32)
    c['LE'] = (p <= f).astype(np.float32)
    c['GE'] = (p >= f).astype(np.float32)
    c['LT'] = (p < f).astype(np.float32)
    c['GT'] = (p > f).astype(np.float32)
    c['hm4'] = (p // 32 == np.arange(4)[None, :]).astype(np.float32)
    c['hm2'] = (p // 64 == np.arange(2)[None, :]).astype(np.float32)
    c['bm_gla'] = (p // 32 == np.arange(256)[None, :] // 64).astype(np.float32)
    c['ones'] = np.ones((128, 128), np.float32)
    c['bm2'] = (p // 64 == f // 64).astype(np.float32)
    c['mL'] = np.concatenate([(p % 64 != 0), (p % 64 != 63)], axis=1).astype(np.float32)
    c['BMf'] = (f // 16 >= p // 16).astype(np.float32)
    c['BMr'] = (p // 16 >= f // 16).astype(np.float32)
    c['NLE'] = -(p <= f).astype(np.float32)
    c['NGE'] = -(p >= f).astype(np.float32)
    c['MLE'] = np.where(p <= f, 0.0, -30000.0).astype(np.float32)
    c['MGE'] = np.where(p >= f, 0.0, -30000.0).astype(np.float32)
    c['GM0'] = ((f // 64 == 0) * np.ones((128, 1))).astype(np.float32)
    c['GM1'] = ((f // 64 == 1) * np.ones((128, 1))).astype(np.float32)
    off = {}
    cols = []
    o = 0
    for k_, v_ in c.items():
        off[k_] = (o, v_.shape[1])
        o += v_.shape[1]
        cols.append(v_)
    return np.concatenate(cols, axis=1), off


def make_sel():
    p = np.arange(128)[:, None]
    x = np.arange(256)[None, :]
    mats = []
    for g in range(8):
        mats.append(((p // 16 == g) & (x == 128 + (p - 16 * g))).astype(np.float32))
    for l in range(8):
        mats.append(((p // 16 == l) & (x == 128 + (p % 16))).astype(np.float32))
    return np.concatenate(mats, axis=1)


SELC = make_sel()
CONSTS, COFF = make_consts()
NCONST = CONSTS.shape[1]


class Prog:
    def __init__(self, nc, es, dbg=()):
        self.nc, self.es = nc, es
        k = self.k = KB(nc, es)
        self.dbg = set(dbg)
        self.x = k.dram('x', [SEQ, D], kind="ExternalInput")
        self.ctx = k.dram('ctx', [CTXL, D], kind="ExternalInput")
        self.c2 = k.dram('c2', [2, D], kind="ExternalInput")
        self.w = {n: k.dram(n, list(WSHAPES[n]), kind="ExternalInput") for n in WNAMES}
        self.cdram = k.dram('consts', [128, NCONST], kind="ExternalInput")
        self.seldram = k.dram('selc', [128, 16 * 256], kind="ExternalInput")
        self.out = k.dram('out', [SEQ, D], kind="ExternalOutput")
        def scr(name, shape):
            return k.dram(name, shape, kind="ExternalOutput" if name in self.dbg else "Internal")
        self.hs = [scr('h_a', [T, D]), scr('h_b', [T, D])]
        self.projTM = scr('projTM', [T, NIN])
        self.projFM = scr('projFM', [NFM, T])
        self.yfD = k.dram('yfD', [NT, 128, 256], BF16)
        self.csb = k.sb([128, NCONST], F32, 'consts')
        k.dma('sp', self.csb[:, :], self.cdram[:, :])
        self.ident = self.C('ident')
        self.identb = k.sb([128, 128], BF16, 'identb')
        k.cp('dve', self.identb[:, :], self.ident)
        self.ycatD = k.dram('ycatD', [128, 8, T], BF16, kind="ExternalOutput" if 'ycatD' in self.dbg else "Internal")
        self.g1T = k.sb([128, 8, 2], F32, 'g1T')
        self.shT = k.sb([128, 8, 2], F32, 'shT')
        self.Gbc = [k.sb([128, D], F32, 'Gbc') for _ in range(2)]

    def C(self, name, c0=0, c1=None):
        o, n = COFF[name]
        c1 = n if c1 is None else c1
        return self.csb[:, o + c0:o + c1]

    def h_src(self, l, t):
        if l == 0:
            if t < NTC:
                return self.ctx[t * 128:(t + 1) * 128, :]
            return self.x[(t - NTC) * 128:(t - NTC + 1) * 128, :]
        return self.hs[(l - 1) % 2][t * 128:(t + 1) * 128, :]

    def phase_mod(self, l):
        k, nc = self.k, self.nc
        es = ExitStack()
        kk = KBScope(k, es)
        cT = kk.sb([128, 8, 2], F32, 'cT')
        with nc.allow_non_contiguous_dma(reason="small param transposes"):
            for kc in range(8):
                k.dma('sp', cT[:, kc, :], self.c2.wrap(self.c2.t[:, kc * 128:(kc + 1) * 128].rearrange("s p -> p s")))
        scT = kk.sb([128, 8, 2], F32, 'scT')
        k.act(scT[:, :, :], cT[:, :, :], AF.Silu)
        abT = kk.sb([128, 24], F32, 'abT')
        npT = kk.sb([128, 8], F32, 'npT')
        with nc.allow_non_contiguous_dma(reason="small param transposes"):
            k.dma('sp', abT[:, :], self.w['ada_b'].wrap(self.w['ada_b'].t[l, :].rearrange("(j p) -> p j", p=128)))
            k.dma('sp', npT[:, :], self.w['norm_pre'].wrap(self.w['norm_pre'].t[l, :].rearrange("(j p) -> p j", p=128)))
        modT = kk.sb([128, 24, 2], F32, 'modT')
        aw = [kk.sb([128, 8, 128], F32, 'aw') for _ in range(2)]
        pm = [kk.ps([128, 512], F32, 'pm') for _ in range(2)]
        for j in range(24):
            a = aw[j % 2]
            k.dma('sp', a[:, :, :], self.w['ada_w'].wrap(
                self.w['ada_w'].t[l, :, j * 128:(j + 1) * 128].rearrange("(k p) n -> p k n", p=128)))
            p = pm[j % 2]
            for kc in range(8):
                k.mm(p[:, 0:2], a[:, kc, :], scT[:, kc, :], start=(kc == 0), stop=(kc == 7))
            k.ts('dve', modT[:, j, :], p[:, 0:2], abT[:, j:j + 1], ALU.add)
        for s in range(2):
            for kc in range(8):
                k.ts('dve', self.g1T[:, kc, s:s + 1], modT[:, 8 + kc, s:s + 1], 1.0, ALU.add, npT[:, kc:kc + 1], ALU.mult)
        k.cp('dve', self.shT[:, :, :], modT[:, 0:8, :])
        rep = kk.sb([128, 8, 2, 128], F32, 'rep')
        ones = kk.sb([128, 128], F32, 'ones')
        k.memset('dve', ones[:, :], 1.0)
        for kc in range(8):
            for s in range(2):
                k.ts('dve', rep[:, kc, s, :], ones[:, :], scT[:, kc, s:s + 1], ALU.mult)
        awg = kk.sb([128, 8, D], F32, 'awg')
        for kc in range(8):
            k.dma('sp', awg[:, kc, :], self.w['ada_w'][l, kc * 128:(kc + 1) * 128, 2048:3072])
        bbc = kk.sb([128, D], F32, 'bbc')
        npbc = kk.sb([128, D], F32, 'npbc')
        k.dma('sp', bbc[:, :], self.w['ada_b'].wrap(self.w['ada_b'].t[l, 2048:3072].partition_broadcast(128)))
        k.dma('sp', npbc[:, :], self.w['norm_post'].wrap(self.w['norm_post'].t[l, :].partition_broadcast(128)))
        pg = [kk.ps([128, 512], F32, 'pg') for _ in range(2)]
        i = 0
        for s in range(2):
            for nb in range(2):
                p = pg[i % 2]; i += 1
                for kc in range(8):
                    k.mm(p[:, :], rep[:, kc, s, :], awg[:, kc, nb * 512:(nb + 1) * 512], start=(kc == 0), stop=(kc == 7))
                g = self.Gbc[s]
                k.tt('dve', g[:, nb * 512:(nb + 1) * 512], p[:, :], bbc[:, nb * 512:(nb + 1) * 512], ALU.add)
                k.tt('dve', g[:, nb * 512:(nb + 1) * 512], g[:, nb * 512:(nb + 1) * 512], npbc[:, nb * 512:(nb + 1) * 512], ALU.mult)
        k.barrier_all()
        es.close()

    def phase_proj(self, l, tiles=None):
        k, nc = self.k, self.nc
        es = ExitStack()
        kk = KBScope(k, es)
        tiles = list(range(NT)) if tiles is None else tiles
        wbf = kk.sb([128, 8, NIN], BF16, 'wbf')
        half = NIN // 2
        for kc in range(8):
            for hh in range(2):
                k.dma('pool', wbf[:, kc, hh * half:(hh + 1) * half],
                      self.w['w_in'][l, kc * 128:(kc + 1) * 128, hh * half:(hh + 1) * half])
        NB = 2
        hb = [kk.sb([128, D], F32, 'hb') for _ in range(NB)]
        junk = kk.sb([128, D], BF16, 'junk')
        ssq = [kk.sb([128, 1], F32, 'ssq') for _ in range(NB)]
        rstd = [kk.sb([128, 1], F32, 'rstd') for _ in range(NB)]
        hsb = [kk.sb([128, D], BF16, 'hsb') for _ in range(NB)]
        hnT = [kk.sb([128, 8, 512], BF16, 'hnT') for _ in range(2)]
        pjs = [kk.sb([128, NIN], F32, 'pjs') for _ in range(NB)]
        fms = [kk.sb([128, len(FMBLK), 512], F32, 'fms') for _ in range(2)]
        tp = [kk.ps([128, 8, 128], BF16, 'tp') for _ in range(2)]
        pj = [kk.ps([128, 512], F32, 'pj') for _ in range(3)]
        pf = [kk.ps([128, 512], F32, 'pf') for _ in range(2)]
        epsb = kk.sb([128, 1], F32, 'epsb')
        k.memset('dve', epsb[:, :], EPS)
        nblk = [(0, 512), (512, 512), (1024, 256), (2304, 264), (2696, 384), (3112, 256)]
        groups = []
        ct = [t for t in tiles if t < NTC]
        lt = [t for t in tiles if t >= NTC]
        if ct:
            groups.append(ct)
        for i in range(0, len(lt), 4):
            groups.append(lt[i:i + 4])
        ipj = 0; ipf = 0; it = 0
        for gi, grp in enumerate(groups):
            hg = hnT[gi % 2]
            fg = fms[gi % 2]
            for ti, t in enumerate(grp):
                b = it % NB
                s = 1 if t < NTC else 0
                k.dma('sp', hb[b][:, :], self.h_src(l, t))
                k.act(junk[:, :], hb[b][:, :], AF.Square, accum_out=ssq[b][:, :])
                k.act(rstd[b][:, :], ssq[b][:, :], AF.Sqrt, bias=epsb[:, :], scale=1.0 / D)
                k.recip(rstd[b][:, :], rstd[b][:, :])
                k.act(hsb[b][:, :], hb[b][:, :], AF.Copy, scale=rstd[b][:, :])
                tpp = tp[it % 2]
                it += 1
                for kc in range(8):
                    k.tr(tpp[:, kc, :], hsb[b][:, kc * 128:(kc + 1) * 128], self.identb[:, :])
                for kc in range(8):
                    k.ts('dve', hg[:, kc, ti * 128:(ti + 1) * 128], tpp[:, kc, :], self.g1T[:, kc, s:s + 1], ALU.mult,
                         self.shT[:, kc, s:s + 1], ALU.add)
                for bi, (n0, nn) in enumerate(nblk):
                    p = pj[ipj % 3]; ipj += 1
                    for kc in range(8):
                        k.mm(p[:, 0:nn], hg[:, kc, ti * 128:(ti + 1) * 128], wbf[:, kc, n0:n0 + nn], start=(kc == 0), stop=(kc == 7))
                    if bi % 2 == 0:
                        k.cp('act', pjs[b][:, n0:n0 + nn], p[:, 0:nn])
                    else:
                        k.cp('dve', pjs[b][:, n0:n0 + nn], p[:, 0:nn])
                k.dma('pool', self.projTM[t * 128:(t + 1) * 128, 0:1280], pjs[b][:, 0:1280])
                k.dma('pool', self.projTM[t * 128:(t + 1) * 128, 2304:2568], pjs[b][:, 2304:2568])
                k.dma('pool', self.projTM[t * 128:(t + 1) * 128, 2696:3080], pjs[b][:, 2696:3080])
                k.dma('pool', self.projTM[t * 128:(t + 1) * 128, 3112:3368], pjs[b][:, 3112:3368])
            ng = len(grp) * 128
            t0_ = grp[0] * 128
            for fi, (c0, ncol) in enumerate(FMBLK):
                p = pf[ipf % 2]; ipf += 1
                for kc in range(8):
                    k.mm(p[0:ncol, 0:ng], wbf[:, kc, c0:c0 + ncol], hg[:, kc, 0:ng], start=(kc == 0), stop=(kc == 7))
                if fi % 2 == 0:
                    k.cp('act', fg[0:ncol, fi, 0:ng], p[0:ncol, 0:ng])
                else:
                    k.cp('dve', fg[0:ncol, fi, 0:ng], p[0:ncol, 0:ng])
            k.dma('pool', self.projFM.wrap(self.projFM.t[0:1280, t0_:t0_ + ng].rearrange("(f p) n -> p f n", p=128)),
                  fg[:, 0:10, 0:ng])
            k.dma('pool', self.projFM[1280:1312, t0_:t0_ + ng], fg[0:32, 10, 0:ng])
        k.barrier_all()
        es.close()


class KBScope:
    def __init__(self, k, es):
        self.k, self.es = k, es

    def sb(self, shape, dt=F32, name=None):
        self.k.nid += 1
        t = self.es.enter_context(self.k.nc.sbuf_tensor('%s_%d' % (name or 'sb', self.k.nid), list(shape), dt))
        try:
            self.k.min_rem = min(getattr(self.k, 'min_rem', 1 << 30), self.k.nc.sbuf_bytes_remaining)
        except Exception:
            pass
        return Buf(t)

    def ps(self, shape, dt=F32, name=None):
        self.k.nid += 1
        nb = int(np.prod(shape[1:])) * (4 if dt == F32 else 2)
        assert nb == 2048, ("psum tiles must be exactly one bank", shape, dt)
        t = self.es.enter_context(self.k.nc.psum_tensor('%s_%d' % (name or 'ps', self.k.nid), list(shape), dt))
        return Buf(t)


def run_interleaved(gens):
    live = list(gens)
    waiting = [False] * len(live)
    alive = [True] * len(live)
    while any(a and not w for a, w in zip(alive, waiting)):
        for i, g in enumerate(live):
            if not alive[i] or waiting[i]:
                continue
            try:
                r = next(g)
                if r == 'STATE':
                    waiting[i] = True
            except StopIteration:
                alive[i] = False
    for i, g in enumerate(live):
        if alive[i]:
            for _ in g:
                pass


def _chain_order(d):
    if d == 0:
        return list(range(NT))
    return list(range(NTC - 1, -1, -1)) + list(range(NT - 1, NTC - 1, -1))


def gen_gla(self, l, ntile=None, NBX=2):
    k, nc = self.k, self.nc
    es = ExitStack()
    kk = KBScope(k, es)
    W = self.w
    g2aug = [kk.sb([17, 128], F32, 'g2aug') for _ in range(2)]
    for d in range(2):
        k.dma('sp', g2aug[d][0:16, :], W['gla_g2'][l, d, :, :])
        k.dma('sp', g2aug[d][16:17, :], W['gla_gb'][l, d:d + 1, :])
    gn_bc = kk.sb([128, 256], F32, 'gn_bc')
    k.dma('sp', gn_bc[:, :], W['gla_norm'].wrap(W['gla_norm'].t[l, :].partition_broadcast(128)))
    yfwd = kk.sb([128, NT, 256], F32, 'yfwd')
    S = kk.sb([128, 256], F32, 'S')
    Sb = kk.sb([128, 256], BF16, 'Sb')
    bm = self.C('bm_gla')
    NB = NBX
    def mk(shape, dt=F32, name='b'):
        return [kk.sb(shape, dt, name) for _ in range(NB)]
    glT = mk([17, 128]); qT = mk([128, 128]); kT = mk([128, 128]); kv = mk([128, 384]); gate = mk([128, 256])
    e1 = mk([128, 128]); la16 = mk([128, 128]); eq = mk([128, 128]); ek = mk([128, 128])
    qtT = mk([128, 128], BF16); ktT = mk([128, 128], F32); kth = mk([128, 4, 128], BF16)
    kdec = mk([128, 128]); khat = mk([128, 128], BF16); vbf = mk([128, 256], BF16)
    sc = mk([128, 4, 128], BF16); ysum = mk([128, 256]); ssq = mk([128, 4]); rstd = mk([128, 4])
    yn = mk([128, 256]); sg = mk([128, 256]); yo = mk([128, 256], BF16); tmpc = mk([128, 256])
    jk = kk.sb([128, 64], F32, 'jk')
    epsb = kk.sb([128, 1], F32, 'epsb')
    k.memset('dve', epsb[:, :], EPS)
    psA = [kk.ps([128, 4, 128], F32, 'psA') for _ in range(NB)]
    psS = [kk.ps([128, 4, 128], F32, 'psS') for _ in range(NB)]
    psY = [kk.ps([128, 2, 256], F32, 'psY') for _ in range(NB)]
    psT = kk.ps([128, 8, 128], BF16, 'psT')
    ystg = mk([128, 2, 128], BF16, 'ystg')
    for b in range(NB):
        k.memset('dve', glT[b][:, :], 1.0)
    rq = FMROW[2568]; rk = FMROW[2696]; rg = FMROW[3080]
    qscale = 32 ** -0.5
    it = 0
    yield
    for d in range(2):
        order = _chain_order(d)
        if ntile is not None:
            order = [t for t in order if t < ntile]
        TRI = self.C('LE') if d == 0 else self.C('GE')
        STR = self.C('GT') if d == 0 else self.C('LT')
        last = 127 if d == 0 else 0
        k.memset('dve', S[:, :], 0.0)
        k.memset('dve', Sb[:, :], 0.0)
        def body(t, b):
            tok = slice(t * 128, (t + 1) * 128)
            pa = psA[b]; pS = psS[b]; pY = psY[b]
            k.dma('sp', glT[b][0:16, :], self.projFM[rg + 16 * d:rg + 16 * d + 16, tok])
            k.dma('sp', qT[b][:, :], self.projFM[rq:rq + 128, tok])
            k.dma('sp', kT[b][:, :], self.projFM[rk:rk + 128, tok])
            k.dma('sp', kv[b][:, :], self.projTM[tok, 2696:3080])
            if d == 1:
                k.dma('sp', gate[b][:, :], self.projTM[tok, 3112:3368])
            k.mm(pa[:, 0, :], glT[b][:, :], g2aug[d][:, :])
            k.act(e1[b][:, :], pa[:, 0, :], AF.Exp, scale=-1.0)
            k.act(e1[b][:, :], e1[b][:, :], AF.Ln, bias=1.0)
            k.ts('dve', la16[b][:, :], e1[b][:, :], -1.0 / 16.0, ALU.mult)
            yield
            k.mm(pa[:, 1, :], la16[b][:, :], TRI)
            k.act(eq[b][:, :], pa[:, 1, :], AF.Exp)
            k.act(ek[b][:, :], pa[:, 1, :], AF.Exp, scale=-1.0)
            k.stt('dve', qtT[b][:, :], qT[b][:, :], qscale, eq[b][:, :], ALU.mult, ALU.mult)
            k.tt('dve', ktT[b][:, :], kT[b][:, :], ek[b][:, :], ALU.mult)
            k.mm(pa[:, 2, :], STR, la16[b][:, :])
            k.act(kdec[b][:, :], pa[:, 2, :], AF.Exp)
            k.tt('dve', khat[b][:, :], kv[b][:, 0:128], kdec[b][:, :], ALU.mult)
            k.cp('pool', vbf[b][:, :], kv[b][:, 128:384])
            yield
            for h in range(4):
                k.amul(kth[b][:, h, :], ktT[b][:, :], self.C('hm4', h, h + 1)) if h % 2 else k.ts('dve', kth[b][:, h, :], ktT[b][:, :], self.C('hm4', h, h + 1), ALU.mult)
            for h in range(4):
                k.mm(pS[:, h, :], kth[b][:, h, :], qtT[b][:, :])
            for h in range(4):
                k.tt('dve', sc[b][:, h, :], pS[:, h, :], TRI, ALU.mult)
            yield 'STATE'
            for h in range(4):
                hs = slice(h * 64, (h + 1) * 64)
                k.mm(pY[:, 0, hs], qtT[b][:, :], Sb[:, hs], start=True, stop=False)
                k.mm(pY[:, 0, hs], sc[b][:, h, :], vbf[b][:, hs], start=False, stop=True)
            if d == 0:
                k.cp('act', yfwd[:, t, :], pY[:, 0, :])
            else:
                k.tt('dve', ysum[b][:, :], pY[:, 0, :], yfwd[:, t, :], ALU.add)
                for h in range(4):
                    hs = slice(h * 64, (h + 1) * 64)
                    k.act(jk[:, :], ysum[b][:, hs], AF.Square, accum_out=ssq[b][:, h:h + 1])
                k.act(rstd[b][:, :], ssq[b][:, :], AF.Sqrt, bias=epsb[:, :], scale=1.0 / 64)
                k.recip(rstd[b][:, :], rstd[b][:, :])
                for h in range(4):
                    hs = slice(h * 64, (h + 1) * 64)
                    k.stt('dve', yn[b][:, hs], ysum[b][:, hs], rstd[b][:, h:h + 1], gn_bc[:, hs], ALU.mult, ALU.mult)
                k.act(sg[b][:, :], gate[b][:, :], AF.Silu)
                k.tt('dve', yo[b][:, :], yn[b][:, :], sg[b][:, :], ALU.mult)
                for c in range(2):
                    k.tr(psT[:, c, :], yo[b][:, c * 128:(c + 1) * 128], self.identb[:, :])
                k.cp('act', ystg[b][:, :, :], psT[:, 0:2, :])
                k.dma('act', self.ycatD.at('gla', (slice(None), slice(6, 8), tok)), ystg[b][:, :, :])
            yield
            k.mm(pY[:, 1, :], khat[b][:, :], vbf[b][:, :])
            k.tt('dve', tmpc[b][:, :], pY[:, 1, :], bm, ALU.mult)
            k.stt('dve', S[:, :], S[:, :], eq[b][:, last:last + 1], tmpc[b][:, :], ALU.mult, ALU.add)
            k.cp('act', Sb[:, :], S[:, :])
        if NB == 1:
            for t_ in order:
                for _r in body(t_, 0):
                    yield
        else:
            for i_ in range(0, len(order), 2):
                run_interleaved([body(t_, b_) for b_, t_ in enumerate(order[i_:i_ + 2])])
                yield
    yield ('DONE', es)


def phase_gla(self, l, ntile=None):
    for r in gen_gla(self, l, ntile):
        if isinstance(r, tuple):
            self.k.barrier_all()
            r[1].close()


def phase_gla_s5(self, l, ntile=None):
    ra = int(os.environ.get('RA', '6'))
    rb = int(os.environ.get('RB', '6'))
    g1 = gen_gla(self, l, None, NBX=1)
    next(g1)
    g2 = gen_s5(self, l)
    d1 = d2 = None
    while d1 is None or d2 is None:
        if d1 is None:
            for _ in range(ra):
                r = next(g1)
                if isinstance(r, tuple):
                    d1 = r[1]
                    break
        if d2 is None:
            for _ in range(rb):
                r = next(g2)
                if isinstance(r, tuple):
                    d2 = r[1]
                    break
    self.k.barrier_all()
    d2.close()
    d1.close()


Prog.phase_gla_s5 = phase_gla_s5
Prog.phase_gla = phase_gla


def phase_ssd(self, l, ntile=None):
    k, nc = self.k, self.nc
    es = ExitStack()
    kk = KBScope(k, es)
    W = self.w
    ntile = NT if ntile is None else ntile
    r0 = FMROW[1792]
    cw = kk.sb([128, 4, 5], F32, 'cw')
    cb = kk.sb([128, 4], F32, 'cb')
    with nc.allow_non_contiguous_dma(reason="small param transposes"):
        for kt in range(5):
            k.dma('sp', cw[:, :, kt], W['ssd_conv_w'].wrap(W['ssd_conv_w'].t[l, kt, :].rearrange("(c p) -> p c", p=128)))
        k.dma('sp', cb[:, :], W['ssd_conv_b'].wrap(W['ssd_conv_b'].t[l, :].rearrange("(c p) -> p c", p=128)))
    dtb = kk.sb([128, 8], F32, 'dtb')
    aneg = kk.sb([128, 8], F32, 'aneg')
    Dbc = kk.sb([128, 4], F32, 'Dbc')
    nbc = kk.sb([128, 256], F32, 'nbc')
    k.dma('sp', dtb[:, :], W['ssd_dt_bias'].wrap(W['ssd_dt_bias'].t[l].rearrange("a b -> (a b)").partition_broadcast(128)))
    k.dma('sp', aneg[:, :], W['ssd_a_log'].wrap(W['ssd_a_log'].t[l].rearrange("a b -> (a b)").partition_broadcast(128)))
    k.dma('sp', Dbc[:, :], W['ssd_d'].wrap(W['ssd_d'].t[l, :].partition_broadcast(128)))
    k.dma('sp', nbc[:, :], W['ssd_norm'].wrap(W['ssd_norm'].t[l, :].partition_broadcast(128)))
    k.act(aneg[:, :], aneg[:, :], AF.Exp)
    k.ts('dve', aneg[:, :], aneg[:, :], -1.0, ALU.mult)
    xc = kk.sb([128, 4, T], BF16, 'xc')
    xin = [kk.sb([128, 516], F32, 'xin') for _ in range(2)]
    acc = [kk.sb([128, 512], F32, 'acc') for _ in range(2)]
    pieces = [(0, min(CTXL, ntile * 128), 0, CTXL)]
    for p0 in range(CTXL, min(T, ntile * 128), 512):
        pieces.append((p0, min(512, ntile * 128 - p0), CTXL, T))
    i = 0
    for (p0, n, lo, hi) in pieces:
        for blk in range(4):
            b = i % 2
            i += 1
            a0 = max(lo, p0 - 2)
            a1 = min(hi, p0 + n + 2, ntile * 128)
            if a0 > p0 - 2 or a1 < p0 + n + 2:
                k.memset('pool', xin[b][:, :], 0.0)
            k.dma('sp', xin[b][:, a0 - (p0 - 2):a1 - (p0 - 2)], self.projFM[r0 + blk * 128:r0 + (blk + 1) * 128, a0:a1])
            k.ts('dve', acc[b][:, 0:n], xin[b][:, 0:n], cw[:, blk, 0:1], ALU.mult, cb[:, blk:blk + 1], ALU.add)
            for kt in range(1, 5):
                k.stt('dve', acc[b][:, 0:n], xin[b][:, kt:kt + n], cw[:, blk, kt:kt + 1], acc[b][:, 0:n], ALU.mult, ALU.add)
            k.act(xc[:, blk, p0:p0 + n], acc[b][:, 0:n], AF.Silu)
    yfwd = kk.sb([128, NT, 256], F32, 'yfwd')
    S = kk.sb([128, 2, 64], F32, 'S')
    Sb = kk.sb([128, 2, 2, 64], BF16, 'Sb')
    NB = 2
    def mk(shape, dt=F32, name='b'):
        return [kk.sb(shape, dt, name) for _ in range(NB)]
    dtr = mk([128, 8]); zt = mk([128, 256]); dt_ = mk([128, 4]); la = mk([128, 4]); ecs = mk([128, 4]); dec = mk([128, 4])
    tot = mk([128, 4]); sdec = mk([128, 2]); rep = mk([128, 4, 128]); seg = mk([128, 4, 128], F32)
    scm = mk([128, 4, 128], BF16); xtm = mk([128, 256], BF16); btm = mk([128, 128], BF16); vv = mk([128, 256], BF16)
    bz = mk([128, 4, 128], BF16); btg = mk([128, 2, 128], BF16); yst = mk([128, 256]); ysum = mk([128, 256]); gz = mk([128, 256]); yo = mk([128, 256], BF16)
    ssq = mk([128, 1]); rstd = mk([128, 1]); sz = mk([128, 256]); ystg = mk([128, 2, 128], BF16, 'ystg')
    jk = kk.sb([128, 256], F32, 'jk')
    epsb = kk.sb([128, 1], F32, 'epsb')
    k.memset('dve', epsb[:, :], EPS)
    psA = kk.ps([128, 4, 128], F32, 'psA')
    psG = [kk.ps([128, 4, 128], F32, 'psG') for _ in range(2)]
    psC = [kk.ps([128, 4, 128], F32, 'psC') for _ in range(2)]
    psY = [kk.ps([128, 2, 256], F32, 'psY') for _ in range(2)]
    psT = kk.ps([128, 8, 128], BF16, 'psT')
    ones = self.C('ones')
    MSK4 = [kk.sb([128, 4, 128], F32, 'MSK4') for _ in range(2)]
    for d_ in range(2):
        for h_ in range(4):
            k.cp('pool', MSK4[d_][:, h_, :], self.C('MLE') if d_ == 0 else self.C('MGE'))
    it = 0
    for d in range(2):
        order = [t for t in _chain_order(d) if t < ntile]
        TRI = self.C('LE') if d == 0 else self.C('GE')
        NTRI = self.C('NLE') if d == 0 else self.C('NGE')
        MSK = self.C('MLE') if d == 0 else self.C('MGE')
        STR = self.C('GT') if d == 0 else self.C('LT')
        k.memset('dve', S[:, :, :], 0.0)
        k.memset('dve', Sb[:, :, :, :], 0.0)
        def body(t, b):
            tok = slice(t * 128, (t + 1) * 128)
            pG = psG[b]; pC = psC[b]; pY = psY[b]
            k.dma('sp', dtr[b][:, :], self.projTM[tok, 2304:2312])
            if d == 1:
                k.dma('sp', zt[b][:, :], self.projTM[tok, 2312:2568])
            k.tt('dve', dt_[b][:, :], dtr[b][:, 4 * d:4 * d + 4], dtb[:, 4 * d:4 * d + 4], ALU.add)
            k.act(dt_[b][:, :], dt_[b][:, :], AF.Exp)
            k.act(dt_[b][:, :], dt_[b][:, :], AF.Ln, bias=1.0)
            k.tt('dve', la[b][:, :], dt_[b][:, :], aneg[:, 4 * d:4 * d + 4], ALU.mult)
            k.mm(psA[:, 0, 0:4], TRI, la[b][:, :])
            k.mm(psA[:, 1, 0:4], STR, la[b][:, :])
            k.mm(psA[:, 2, 0:4], ones, la[b][:, :])
            k.act(ecs[b][:, :], psA[:, 0, 0:4], AF.Exp)
            k.act(dec[b][:, :], psA[:, 1, 0:4], AF.Exp)
            k.act(tot[b][:, :], psA[:, 2, 0:4], AF.Exp)
            k.ts('dve', sdec[b][:, :], tot[b][:, 0:2], self.C('hm2', 0, 1), ALU.mult)
            k.stt('dve', sdec[b][:, :], tot[b][:, 2:4], self.C('hm2', 1, 2), sdec[b][:, :], ALU.mult, ALU.add)
            yield
            for h in range(4):
                k.amul(rep[b][:, h, :], ones, la[b][:, h:h + 1]) if h % 2 else k.ts('dve', rep[b][:, h, :], ones, la[b][:, h:h + 1], ALU.mult)
            pG4 = pG.wrap(pG.t[:, :, :].rearrange("p a b -> p (a b)"))
            k.mm(pG4, NTRI, rep[b].wrap(rep[b].t[:, :, :].rearrange("p a b -> p (a b)")), start=True, stop=False)
            for h in range(4):
                k.mm(pG[:, h, :], rep[b][:, h, :], TRI, start=False, stop=False)
            k.mm(pG4, self.ident, MSK4[d].wrap(MSK4[d].t[:, :, :].rearrange("p a b -> p (a b)")), start=False, stop=True)
            k.act(seg[b][:, :, :], pG[:, :, :], AF.Exp)
            yield
            for g in range(2):
                k.amul(btg[b][:, g, :], xc[:, 2, tok], self.C('hm2', g, g + 1))
                k.mm(pC[:, g, :], btg[b][:, g, :], xc[:, 3, tok])
            for h in range(4):
                k.tt('dve', scm[b][:, h, :], pC[:, h // 2, :], seg[b][:, h, :], ALU.mult)
            yield
            for c in range(2):
                k.tr(psT[:, c, :], xc[:, c, tok], self.identb[:, :])
            k.tr(psT[:, 2, :], xc[:, 2, tok], self.identb[:, :])
            k.cp('act', xtm[b][:, :], psT[:, 0:2, :])
            k.cp('act', btm[b][:, :], psT[:, 2, :])
            for h in range(4):
                hs = slice(h * 64, (h + 1) * 64)
                k.act(vv[b][:, hs], xtm[b][:, hs], AF.Copy, scale=dt_[b][:, h:h + 1])
            yield 'STATE'
            for h in range(4):
                hs = slice(h * 64, (h + 1) * 64)
                k.mm(pY[:, 0, hs], scm[b][:, h, :], vv[b][:, hs])
                k.mm(pY[:, 1, hs], xc[:, 3, tok], Sb[:, h // 2, h % 2, :])
            for h in range(4):
                hs = slice(h * 64, (h + 1) * 64)
                k.act(yst[b][:, hs], pY[:, 1, hs], AF.Copy, scale=ecs[b][:, h:h + 1])
            if d == 0:
                k.tt('dve', yfwd[:, t, :], pY[:, 0, :], yst[b][:, :], ALU.add)
            else:
                k.tt('dve', ysum[b][:, :], pY[:, 0, :], yst[b][:, :], ALU.add)
                k.tt('pool', ysum[b][:, :], ysum[b][:, :], yfwd[:, t, :], ALU.add)
                for h in range(4):
                    hs = slice(h * 64, (h + 1) * 64)
                    k.stt('dve', ysum[b][:, hs], xtm[b][:, hs], Dbc[:, h:h + 1], ysum[b][:, hs], ALU.mult, ALU.add)
                k.act(sz[b][:, :], zt[b][:, :], AF.Silu)
                k.tt('dve', gz[b][:, :], ysum[b][:, :], sz[b][:, :], ALU.mult)
                k.act(jk[:, :], gz[b][:, :], AF.Square, accum_out=ssq[b][:, :])
                k.act(rstd[b][:, :], ssq[b][:, :], AF.Sqrt, bias=epsb[:, :], scale=1.0 / 256)
                k.recip(rstd[b][:, :], rstd[b][:, :])
                k.stt('dve', yo[b][:, :], gz[b][:, :], rstd[b][:, 0:1], nbc[:, :], ALU.mult, ALU.mult)
                for c in range(2):
                    k.tr(psT[:, 4 + c, :], yo[b][:, c * 128:(c + 1) * 128], self.identb[:, :])
                k.cp('act', ystg[b][:, :, :], psT[:, 4:6, :])
                k.dma('act', self.ycatD.at('ssd', (slice(None), slice(4, 6), tok)), ystg[b][:, :, :])
            yield
            for h in range(4):
                k.stt('dve', bz[b][:, h, :], btm[b][:, :], dec[b][:, h:h + 1],
                      self.C('GM%d' % (h // 2)),
                      ALU.mult, ALU.mult)
            for hh in range(2):
                k.mm(psA[:, 3, hh * 64:(hh + 1) * 64], bz[b][:, hh, :], vv[b][:, hh * 64:(hh + 1) * 64], start=True, stop=False)
                k.mm(psA[:, 3, hh * 64:(hh + 1) * 64], bz[b][:, 2 + hh, :], vv[b][:, (2 + hh) * 64:(3 + hh) * 64], start=False, stop=True)
            for hh in range(2):
                k.stt('dve', S[:, hh, :], S[:, hh, :], sdec[b][:, hh:hh + 1], psA[:, 3, hh * 64:(hh + 1) * 64], ALU.mult, ALU.add)
            for g in range(2):
                k.amul(Sb[:, g, :, :], S[:, :, :], self.C('hm2', g, g + 1))
        for i_ in range(0, len(order), 2):
            run_interleaved([body(t_, b_) for b_, t_ in enumerate(order[i_:i_ + 2])])
    k.barrier_all()
    es.close()


Prog.phase_ssd = phase_ssd


def gen_s5(self, l, ntile=None):
    k, nc = self.k, self.nc
    es = ExitStack()
    kk = KBScope(k, es)
    W = self.w
    I32 = mybir.dt.int32
    NCH = T // 8
    NCC = CTXL // 8
    TWO_PI = 6.283185307179586
    Tsum = kk.sb([128, 16, 128], BF16, 'Tsum')
    Wall = kk.sb([128, 16, 2, 128], BF16, 'Wall')
    Vall = kk.sb([128, 16, 2, 128], BF16, 'Vall')
    MUr = kk.sb([128, 16, 10], F32, 'MUr'); MUi = kk.sb([128, 16, 10], F32, 'MUi'); NMi = kk.sb([128, 16, 10], F32, 'NMi')
    BK = [kk.ps([128, 512], F32, 'bk5') for _ in range(4)]
    psP = [Buf(BK[i].t[:, :].rearrange("p (a b) -> p a b", a=4)) for i in range(2)]
    for i in range(2):
        psP[i].tok = BK[i].tok
    es2 = ExitStack()
    kp = KBScope(k, es2)
    def sm(shape=(128, 16), dt=F32, name='sm'):
        return kp.sb(list(shape), dt, name)
    are = sm(); aim = sm(); ldt = sm()
    with nc.allow_non_contiguous_dma(reason="small param transposes"):
        for d in range(2):
            k.dma('sp', are[64 * d:64 * d + 64, :], W['s5_a_re'].wrap(W['s5_a_re'].t[l, d].rearrange("g p -> p g")))
            k.dma('sp', aim[64 * d:64 * d + 64, :], W['s5_a_im'].wrap(W['s5_a_im'].t[l, d].rearrange("g p -> p g")))
            k.dma('sp', ldt[64 * d:64 * d + 64, :], W['s5_log_dt'].wrap(W['s5_log_dt'].t[l, d, :].partition_broadcast(64)))
    bre = kp.sb([128, 16, 16], F32, 'bre'); bim = kp.sb([128, 16, 16], F32, 'bim')
    for d in range(2):
        k.dma('sp', bre[64 * d:64 * d + 64, :, :], W['s5_b_re'].wrap(W['s5_b_re'].t[l, d].rearrange("g p c -> p g c")))
        k.dma('sp', bim[64 * d:64 * d + 64, :, :], W['s5_b_im'].wrap(W['s5_b_im'].t[l, d].rearrange("g p c -> p g c")))
    cre = kp.sb([128, 16, 16], F32, 'cre'); cim = kp.sb([128, 16, 16], F32, 'cim')
    craw = kp.sb([128, 2, 2, 2, 128], F32, 'craw')
    k.memset('pool', craw[:, :, :, :, :], 0.0)
    for ri, nm in enumerate(['s5_c_re', 's5_c_im']):
        for d in range(2):
            k.dma('sp', craw[:, ri, d, :, 64 * d:64 * d + 64],
                  W[nm].wrap(W[nm].t[l, d].rearrange("g c p -> (g c) p").rearrange("(q r) p -> r q p", r=128)))
    for ri, dst in enumerate([cre, cim]):
        pp = psP[ri]
        for q in range(2):
            for d in range(2):
                k.mm(pp[:, q, :], craw[:, ri, d, q, :], self.ident, start=(d == 0), stop=(d == 1))
        k.cp('dve', dst[:, :, :], pp[:, 0:2, :])
    dt_ = sm(); ar = sm(); ai = sm(); mag = sm(); tt_ = sm(); ki = sm(dt=I32); kf = sm(); rr = sm(); half = sm()
    sh = sm(); ah = sm(); ch = sm(); cc = sm(); ss = sm(); lr = sm(); li = sm(); t1 = sm(); t2 = sm()
    k.act(dt_[:, :], ldt[:, :], AF.Exp)
    k.tt('dve', ar[:, :], are[:, :], dt_[:, :], ALU.mult)
    k.tt('dve', ai[:, :], aim[:, :], dt_[:, :], ALU.mult)
    k.act(mag[:, :], ar[:, :], AF.Exp)
    k.ts('dve', tt_[:, :], ai[:, :], 1.0 / TWO_PI, ALU.mult)
    k.cp('dve', ki[:, :], tt_[:, :])
    k.cp('dve', kf[:, :], ki[:, :])
    k.stt('dve', rr[:, :], kf[:, :], -TWO_PI, ai[:, :], ALU.mult, ALU.add)
    k.ts('dve', half[:, :], rr[:, :], 0.5, ALU.mult)
    k.act(sh[:, :], half[:, :], AF.Sin)
    k.act(ah[:, :], half[:, :], AF.Abs)
    k.act(ch[:, :], ah[:, :], AF.Sin, scale=-1.0, bias=1.5707963267948966)
    k.tt('dve', ss[:, :], sh[:, :], ch[:, :], ALU.mult)
    k.ts('dve', ss[:, :], ss[:, :], 2.0, ALU.mult)
    k.tt('dve', t1[:, :], ch[:, :], ch[:, :], ALU.mult)
    k.tt('dve', t2[:, :], sh[:, :], sh[:, :], ALU.mult)
    k.tt('dve', cc[:, :], t1[:, :], t2[:, :], ALU.subtract)
    k.tt('dve', lr[:, :], mag[:, :], cc[:, :], ALU.mult)
    k.tt('dve', li[:, :], mag[:, :], ss[:, :], ALU.mult)
    yield
    PPr = kp.sb([128, 16, 9], F32, 'PPr'); PPi = kp.sb([128, 16, 9], F32, 'PPi')
    PNr = kp.sb([128, 16, 9], F32, 'PNr'); PNi = kp.sb([128, 16, 9], F32, 'PNi')
    def cmul(or_, oi_, ar_, ai_, br_, bi_):
        k.tt('dve', t1[:, :], ar_, br_, ALU.mult)
        k.tt('dve', t2[:, :], ai_, bi_, ALU.mult)
        k.tt('dve', or_, t1[:, :], t2[:, :], ALU.subtract)
        k.tt('dve', t1[:, :], ar_, bi_, ALU.mult)
        k.tt('dve', t2[:, :], ai_, br_, ALU.mult)
        k.tt('dve', oi_, t1[:, :], t2[:, :], ALU.add)
    im2 = sm(); nr = sm(); ni = sm()
    k.act(im2[:, :], ar[:, :], AF.Exp, scale=-2.0)
    k.tt('dve', nr[:, :], lr[:, :], im2[:, :], ALU.mult)
    k.tt('dve', ni[:, :], li[:, :], im2[:, :], ALU.mult)
    k.ts('dve', ni[:, :], ni[:, :], -1.0, ALU.mult)
    for (Pr, Pi, br_, bi_, n) in ((PPr, PPi, lr, li, 9), (PNr, PNi, nr, ni, 8)):
        k.memset('dve', Pr[:, :, 0], 1.0)
        k.memset('dve', Pi[:, :, 0], 0.0)
        for s_ in range(1, n):
            cmul(Pr[:, :, s_], Pi[:, :, s_], Pr[:, :, s_ - 1], Pi[:, :, s_ - 1], br_[:, :], bi_[:, :])
    k.cp('dve', MUr[:, :, 0], PPr[:, :, 8])
    k.cp('dve', MUi[:, :, 0], PPi[:, :, 8])
    for q in range(1, 10):
        cmul(MUr[:, :, q], MUi[:, :, q], MUr[:, :, q - 1], MUi[:, :, q - 1], MUr[:, :, q - 1], MUi[:, :, q - 1])
    k.ts('dve', NMi[:, :, :], MUi[:, :, :], -1.0, ALU.mult)
    yield
    den = sm(); cfr = sm(); cfi = sm(); n1 = sm()
    k.tt('dve', t1[:, :], are[:, :], are[:, :], ALU.mult)
    k.tt('dve', t2[:, :], aim[:, :], aim[:, :], ALU.mult)
    k.tt('dve', den[:, :], t1[:, :], t2[:, :], ALU.add)
    k.recip(den[:, :], den[:, :])
    k.ts('dve', n1[:, :], lr[:, :], -1.0, ALU.add)
    k.tt('dve', t1[:, :], n1[:, :], are[:, :], ALU.mult)
    k.tt('dve', t2[:, :], li[:, :], aim[:, :], ALU.mult)
    k.tt('dve', cfr[:, :], t1[:, :], t2[:, :], ALU.add)
    k.tt('dve', cfr[:, :], cfr[:, :], den[:, :], ALU.mult)
    k.tt('dve', t1[:, :], li[:, :], are[:, :], ALU.mult)
    k.tt('dve', t2[:, :], n1[:, :], aim[:, :], ALU.mult)
    k.tt('dve', cfi[:, :], t1[:, :], t2[:, :], ALU.subtract)
    k.tt('dve', cfi[:, :], cfi[:, :], den[:, :], ALU.mult)
    bbr = kp.sb([128, 16, 16], F32, 'bbr'); bbi = kp.sb([128, 16, 16], F32, 'bbi')
    u1 = kp.sb([128, 16, 16], F32, 'u1'); u2 = kp.sb([128, 16, 16], F32, 'u2')
    def bc3(a):
        return A(a.ap.unsqueeze(2).to_broadcast([128, 16, 16]), a.tok)
    k.tt('dve', u1[:, :, :], bre[:, :, :], bc3(cfr[:, :]), ALU.mult)
    k.tt('dve', u2[:, :, :], bim[:, :, :], bc3(cfi[:, :]), ALU.mult)
    k.tt('dve', bbr[:, :, :], u1[:, :, :], u2[:, :, :], ALU.subtract)
    k.tt('dve', u1[:, :, :], bim[:, :, :], bc3(cfr[:, :]), ALU.mult)
    k.tt('dve', u2[:, :, :], bre[:, :, :], bc3(cfi[:, :]), ALU.mult)
    k.tt('dve', bbi[:, :, :], u1[:, :, :], u2[:, :, :], ALU.add)
    GB = 4
    def tb(name):
        return kp.sb([128, GB, 8, 16], F32, name)
    Xr = tb('Xr'); Xi = tb('Xi'); v1 = tb('v1'); v2 = tb('v2')
    Wtr = tb('Wtr'); Wti = tb('Wti'); Vtr = tb('Vtr'); NVti = tb('NVti')
    Wz = [[tb('Wz') for _ in range(2)] for _ in range(2)]
    tmpT = kp.sb([128, 128], F32, 'tmpT')
    def pw(P, half, gs, sl):
        a = P.t[64 * half:64 * half + 64, gs, sl]
        return A(a.unsqueeze(3).to_broadcast([64, GB, 8, 16]), P.tok)
    def bb(Bt, half, gs):
        a = Bt.t[64 * half:64 * half + 64, gs, :]
        return A(a.unsqueeze(2).to_broadcast([64, GB, 8, 16]), Bt.tok)
    def ctab(outr, outi, Pr, Pi, slf, slr, Br, Bi, gs, neg_i=False):
        for half, sl in ((0, slf), (1, slr)):
            hs = slice(64 * half, 64 * half + 64)
            k.tt('dve', v1[hs, :, :, :], pw(Pr, half, gs, sl), bb(Br, half, gs), ALU.mult)
            k.tt('dve', v2[hs, :, :, :], pw(Pi, half, gs, sl), bb(Bi, half, gs), ALU.mult)
            k.tt('dve', outr[hs, :, :, :], v1[hs, :, :, :], v2[hs, :, :, :], ALU.subtract)
            k.tt('dve', v1[hs, :, :, :], pw(Pr, half, gs, sl), bb(Bi, half, gs), ALU.mult)
            k.tt('dve', v2[hs, :, :, :], pw(Pi, half, gs, sl), bb(Br, half, gs), ALU.mult)
            k.tt('dve', outi[hs, :, :, :], v1[hs, :, :, :], v2[hs, :, :, :], ALU.add)
            if neg_i:
                k.ts('dve', outi[hs, :, :, :], outi[hs, :, :, :], -1.0, ALU.mult)
    S07 = slice(0, 8); S18 = slice(1, 9); R70 = slice(7, None, -1); R81 = slice(8, 0, -1)
    for gb in range(16 // GB):
        gs = slice(gb * GB, (gb + 1) * GB)
        yield
        ctab(Xr, Xi, PPr, PPi, R70, S07, bbr, bbi, gs)
        for gi in range(GB):
            g = gb * GB + gi
            pp = psP[gi % 2]
            k.tr(pp[:, 0, :], Xr[:, gi, :, :], self.ident)
            k.tr(pp[:, 1, :], Xi[:, gi, :, :], self.ident)
            k.cp('act', Wall[:, g, :, :], pp[:, 0:2, :])
        ctab(Xr, Xi, PPr, PPi, S18, R81, cre, cim, gs, neg_i=True)
        k.cp('act', Vall[:, gs, 0, :], Xr[:, :, :, :])
        k.cp('act', Vall[:, gs, 1, :], Xi[:, :, :, :])
        for half, (Pr_, Pi_) in ((0, (PNr, PNi)), (1, (PPr, PPi))):
            pass
        for half, (Pr_, Pi_) in ((0, (PNr, PNi)), (1, (PPr, PPi))):
            hs = slice(64 * half, 64 * half + 64)
            k.tt('dve', v1[hs, :, :, :], pw(Pr_, half, gs, S07), bb(bbr, half, gs), ALU.mult)
            k.tt('dve', v2[hs, :, :, :], pw(Pi_, half, gs, S07), bb(bbi, half, gs), ALU.mult)
            k.tt('dve', Wtr[hs, :, :, :], v1[hs, :, :, :], v2[hs, :, :, :], ALU.subtract)
            k.tt('dve', v1[hs, :, :, :], pw(Pr_, half, gs, S07), bb(bbi, half, gs), ALU.mult)
            k.tt('dve', v2[hs, :, :, :], pw(Pi_, half, gs, S07), bb(bbr, half, gs), ALU.mult)
            k.tt('dve', Wti[hs, :, :, :], v1[hs, :, :, :], v2[hs, :, :, :], ALU.add)
        for half, (Pr_, Pi_) in ((0, (PPr, PPi)), (1, (PNr, PNi))):
            hs = slice(64 * half, 64 * half + 64)
            k.tt('dve', v1[hs, :, :, :], pw(Pr_, half, gs, S07), bb(cre, half, gs), ALU.mult)
            k.tt('dve', v2[hs, :, :, :], pw(Pi_, half, gs, S07), bb(cim, half, gs), ALU.mult)
            k.tt('dve', Vtr[hs, :, :, :], v1[hs, :, :, :], v2[hs, :, :, :], ALU.subtract)
            k.tt('dve', v1[hs, :, :, :], pw(Pr_, half, gs, S07), bb(cim, half, gs), ALU.mult)
            k.tt('dve', v2[hs, :, :, :], pw(Pi_, half, gs, S07), bb(cre, half, gs), ALU.mult)
            k.tt('dve', NVti[hs, :, :, :], v1[hs, :, :, :], v2[hs, :, :, :], ALU.add)
            k.ts('dve', NVti[hs, :, :, :], NVti[hs, :, :, :], -1.0, ALU.mult)
        yield
        for d in range(2):
            k.ts('dve', Wz[d][0][:, :, :, :], Wtr[:, :, :, :], self.C('hm2', d, d + 1), ALU.mult)
            k.ts('pool', Wz[d][1][:, :, :, :], Wti[:, :, :, :], self.C('hm2', d, d + 1), ALU.mult)
        for gi in range(GB):
            g = gb * GB + gi
            pp = psP[gi % 2]
            for d in range(2):
                k.mm(pp[:, 2 + d, :], Wz[d][0][:, gi, :, :], Vtr[:, gi, :, :], start=True, stop=False)
                k.mm(pp[:, 2 + d, :], Wz[d][1][:, gi, :, :], NVti[:, gi, :, :], start=False, stop=True)
            k.tt('dve', tmpT[:, :], pp[:, 2, :], self.C('BMf'), ALU.mult)
            k.tt('dve', Tsum[:, g, :], pp[:, 3, :], self.C('BMr'), ALU.mult)
            k.tt('dve', Tsum[:, g, :], Tsum[:, g, :], tmpT[:, :], ALU.add)
    k.barrier_all()
    es2.close()
    selb = kk.sb([128, 16, 256], BF16, 'selb')
    k.dma('pool', selb[:, :, :], self.seldram.wrap(self.seldram.t[:, :].rearrange("p (m x) -> p m x", x=256)))
    uTb1 = kk.sb([128, T], BF16, 'uTb')
    uTb = [uTb1, uTb1]
    r0 = FMROW[1280]
    def load_u(c):
        for hh in range(4):
            k.dma('pool', uTb1[:, hh * (T // 4):(hh + 1) * (T // 4)],
                  self.projFM[r0 + 128 * c:r0 + 128 * c + 128, hh * (T // 4):(hh + 1) * (T // 4)])
    y8all = kk.sb([128, 16, NCH], BF16, 'y8all')
    u8 = [kk.sb([128, NCH], BF16, 'u8') for _ in range(2)]
    Bs2 = [[kk.sb([128, 2, NCH], F32, 'Bs') for _ in range(2)] for _ in range(2)]
    Xp = [kk.sb([128, 2, NCH], BF16, 'Xp') for _ in range(2)]
    psU = BK[0]; psU2 = BK[1]; psXb = BK[2]; psY = BK[3]
    LAT = slice(NCC, NCH); CTX = slice(0, NCC)
    def gbody(g):
        b = g % 2
        c = g // 8; gl = g % 8
        if gl == 0:
            load_u(c)
        yield
        for j in range(8):
            sel = selb[:, gl, 128 - 16 * j:256 - 16 * j]
            k.mm(psU[:, :], sel, uTb[c][:, CTXL + j:T:8], start=(j == 0), stop=(j == 7))
            k.mm(psU2[:, 0:NCC], sel, uTb[c][:, j:CTXL:8], start=(j == 0), stop=(j == 7))
        k.cp('act', u8[b][:, LAT], psU[:, :])
        k.cp('act', u8[b][:, CTX], psU2[:, 0:NCC])
        B0 = Bs2[b][0]; B1 = Bs2[b][1]
        for ri in range(2):
            k.mm(psU2[:, 64 + 32 * ri:96 + 32 * ri], Wall[:, g, ri, :], u8[b][:, CTX])
        for ri in range(2):
            k.cp('act', B0[0:64, ri, CTX], psU2[0:64, 64 + 32 * ri:96 + 32 * ri])
            k.cp('dve', B0[64:128, ri, CTX], psU2[64:128, 95 + 32 * ri:63 + 32 * ri:-1])
        yield
        for hf in range(2):
            a_, b_ = NCC + 256 * hf, NCC + 256 * (hf + 1)
            for ri in range(2):
                k.mm(psXb[:, 256 * ri:256 * (ri + 1)], Wall[:, g, ri, :], u8[b][:, a_:b_])
            for ri in range(2):
                k.cp('act', B0[0:64, ri, a_:b_], psXb[0:64, 256 * ri:256 * (ri + 1)])
                k.cp('dve', B0[64:128, ri, NCH + NCC - b_:NCH + NCC - a_], psXb[64:128, 256 * (ri + 1) - 1:(256 * ri - 1) if ri > 0 else None:-1])
            yield
        src, dst = B0, B1
        for q in range(10):
            shf = 1 << q
            n = NCH - shf
            mr = MUr[:, g, q:q + 1]; mi = MUi[:, g, q:q + 1]; nmi = NMi[:, g, q:q + 1]
            k.stt('dve', dst[:, 0, shf:], src[:, 0, 0:n], mr, src[:, 0, shf:], ALU.mult, ALU.add)
            k.stt('dve', dst[:, 0, shf:], src[:, 1, 0:n], nmi, dst[:, 0, shf:], ALU.mult, ALU.add)
            k.stt('dve', dst[:, 1, shf:], src[:, 1, 0:n], mr, src[:, 1, shf:], ALU.mult, ALU.add)
            k.stt('dve', dst[:, 1, shf:], src[:, 0, 0:n], mi, dst[:, 1, shf:], ALU.mult, ALU.add)
            k.cp('pool', dst[:, :, 0:shf], src[:, :, 0:shf])
            src, dst = dst, src
            yield
        X = src
        xp = Xp[b]
        k.memset('pool', xp[0:64, :, 0:1], 0.0)
        k.cp('act', xp[0:64, :, 1:NCH], X[0:64, :, 0:NCH - 1])
        k.memset('pool', xp[64:128, :, NCC - 1:NCC], 0.0)
        k.cp('dve', xp[64:128, :, 0:NCC - 1], X[64:128, :, NCC - 2::-1])
        k.cp('dve', xp[64:128, :, NCC:NCH], X[64:128, :, NCH - 2:NCC - 2:-1])
        yield
        k.mm(psY[:, :], Tsum[:, g, :], u8[b][:, LAT], start=True, stop=False)
        k.mm(psY[:, :], Vall[:, g, 0, :], xp[:, 0, LAT], start=False, stop=False)
        k.mm(psY[:, :], Vall[:, g, 1, :], xp[:, 1, LAT], start=False, stop=True)
        k.mm(psU2[:, 256:256 + NCC], Tsum[:, g, :], u8[b][:, CTX], start=True, stop=False)
        k.mm(psU2[:, 256:256 + NCC], Vall[:, g, 0, :], xp[:, 0, CTX], start=False, stop=False)
        k.mm(psU2[:, 256:256 + NCC], Vall[:, g, 1, :], xp[:, 1, CTX], start=False, stop=True)
        k.cp('act', y8all[:, g, LAT], psY[:, :])
        k.cp('act', y8all[:, g, CTX], psU2[:, 256:256 + NCC])
    for g0 in range(0, 16, 2):
        run_interleaved([gbody(g0), gbody(g0 + 1)])
        yield
    Dt = kk.sb([128, 2], F32, 'Dt'); gbT = kk.sb([128, 2], F32, 'gbT')
    with nc.allow_non_contiguous_dma(reason="small param transposes"):
        k.dma('sp', Dt[:, :], W['s5_d'].wrap(W['s5_d'].t[l, :].rearrange("(c p) -> p c", p=128)))
        k.dma('sp', gbT[:, :], W['s5_glu_b'].wrap(W['s5_glu_b'].t[l, :].rearrange("(c p) -> p c", p=128)))
    gw = kk.sb([128, 2, 256], BF16, 'gw')
    for c in range(2):
        k.dma('pool', gw[:, c, :], W['s5_glu_w'][l, 128 * c:128 * c + 128, :])
    BW = 512
    blocks = [(t0_, min(BW, T - t0_)) for t0_ in range(0, T, BW)]
    yv = [kk.sb([128, 2, BW], F32, 'yv') for _ in range(2)]
    w1 = [kk.sb([128, 2, BW], F32, 'w1') for _ in range(2)]
    gl_ = [kk.sb([128, 2, BW], F32, 'gl') for _ in range(2)]
    glb = [kk.sb([128, 2, BW], BF16, 'glb') for _ in range(2)]
    sgm = [kk.sb([128, 2, BW], F32, 'sgm') for _ in range(2)]
    gt = [kk.sb([128, 2, BW], F32, 'gt') for _ in range(2)]
    ut = [kk.sb([128, 2, BW], F32, 'ut') for _ in range(2)]
    ystg = [kk.sb([128, 2, BW], BF16, 'ystg') for _ in range(2)]
    psO = [psXb, psU]
    psG = [psY, psU2]
    rg = FMROW[1536]
    nblk = len(blocks) if ntile is None else max(1, ntile // 4)
    def tbody(bk):
        b = bk % 2
        t0_, w_ = blocks[bk]
        tk = slice(t0_, t0_ + w_)
        ck = slice(t0_ // 8, (t0_ + w_) // 8)
        for c in range(2):
            k.dma('sp', gt[b][:, c, 0:w_], self.projFM[rg + 128 * c:rg + 128 * c + 128, tk])
            k.dma('sp', ut[b][:, c, 0:w_], self.projFM[r0 + 128 * c:r0 + 128 * c + 128, tk])
            po = psO[c]
            for l_ in range(8):
                for gl in range(8):
                    selT = selb[:, 8 + l_, 128 - 16 * gl:256 - 16 * gl]
                    k.mm(po.wrap(po.t[:, l_:w_:8]), selT, y8all[:, 8 * c + gl, ck],
                         start=(gl == 0), stop=(gl == 7))
        for c in range(2):
            k.stt('dve', yv[b][:, c, 0:w_], ut[b][:, c, 0:w_], Dt[:, c:c + 1], psO[c][:, 0:w_], ALU.mult, ALU.add)
        yield
        k.tt('pool', w1[b][:, :, 0:w_], yv[b][:, :, 0:w_], yv[b][:, :, 0:w_], ALU.mult)
        k.ts('dve', w1[b][:, :, 0:w_], w1[b][:, :, 0:w_], 0.044715, ALU.mult, 1.0, ALU.add)
        k.tt('dve', w1[b][:, :, 0:w_], w1[b][:, :, 0:w_], yv[b][:, :, 0:w_], ALU.mult)
        k.act(w1[b][:, :, 0:w_], w1[b][:, :, 0:w_], AF.Sigmoid, scale=1.5957691216057308)
        k.tt('dve', gl_[b][:, :, 0:w_], w1[b][:, :, 0:w_], yv[b][:, :, 0:w_], ALU.mult)
        k.cp('pool', glb[b][:, :, 0:w_], gl_[b][:, :, 0:w_])
        for oc in range(2):
            for c in range(2):
                k.mm(psG[oc][:, 0:w_], gw[:, c, oc * 128:(oc + 1) * 128], glb[b][:, c, 0:w_],
                     start=(c == 0), stop=(c == 1))
        for oc in range(2):
            k.act(sgm[b][:, oc, 0:w_], psG[oc][:, 0:w_], AF.Sigmoid, bias=gbT[:, oc:oc + 1])
        k.tt('dve', gl_[b][:, :, 0:w_], gl_[b][:, :, 0:w_], sgm[b][:, :, 0:w_], ALU.mult)
        k.act(gt[b][:, :, 0:w_], gt[b][:, :, 0:w_], AF.Silu)
        k.tt('dve', ystg[b][:, :, 0:w_], gl_[b][:, :, 0:w_], gt[b][:, :, 0:w_], ALU.mult)
        k.dma('act', self.ycatD.at('s5', (slice(None), slice(2, 4), tk)), ystg[b][:, :, 0:w_])
    for b0 in range(0, nblk, 2):
        run_interleaved([tbody(x_) for x_ in range(b0, min(b0 + 2, nblk))])
        yield
    yield ('DONE', es)


def phase_s5(self, l, ntile=None):
    for r in gen_s5(self, l, ntile):
        if isinstance(r, tuple):
            self.k.barrier_all()
            r[1].close()


Prog.phase_s5 = phase_s5


def phase_out(self, l, ntile=None):
    k, nc = self.k, self.nc
    es = ExitStack()
    kk = KBScope(k, es)
    last = (l == DEPTH - 1)
    wob = kk.sb([128, 8, D], BF16, 'wob')
    for kc in range(8):
        k.dma('pool', wob[:, kc, :], self.w['w_out'][l, kc * 128:(kc + 1) * 128, :])
    NB = 2
    hb = [kk.sb([128, D], F32, 'hb') for _ in range(NB)]
    yt = [kk.sb([128, 8, 128], BF16, 'yt') for _ in range(NB)]
    yn = [kk.sb([128, D], F32, 'yn') for _ in range(NB)]
    ssq = [kk.sb([128, 2], F32, 'ssq') for _ in range(NB)]
    rstd = [kk.sb([128, 1], F32, 'rstd') for _ in range(NB)]
    junk = kk.sb([128, 512], BF16, 'junk')
    epsb = kk.sb([128, 1], F32, 'epsb')
    k.memset('dve', epsb[:, :], EPS)
    py = [[kk.ps([128, 512], F32, 'py') for _ in range(2)] for _ in range(2)]
    tiles = list(range(NT)) if ntile is None else list(range(ntile))
    if last:
        tiles = [t for t in tiles if t >= NTC]
    for it, t in enumerate(tiles):
        b = it % NB
        s = 1 if t < NTC else 0
        tok = slice(t * 128, (t + 1) * 128)
        k.dma('sp', hb[b][:, :], self.h_src(l, t))
        k.dma('sp', yt[b][:, :, :], self.ycatD[:, :, tok])
        for nb in range(2):
            p = py[b][nb]
            for kc in range(8):
                k.mm(p[:, :], yt[b][:, kc, :], wob[:, kc, nb * 512:(nb + 1) * 512], start=(kc == 0), stop=(kc == 7))
            k.act(junk[:, :], p[:, :], AF.Square, accum_out=ssq[b][:, nb:nb + 1])
        k.tt('dve', rstd[b][:, :], ssq[b][:, 0:1], ssq[b][:, 1:2], ALU.add)
        k.act(rstd[b][:, :], rstd[b][:, :], AF.Sqrt, bias=epsb[:, :], scale=1.0 / D)
        k.recip(rstd[b][:, :], rstd[b][:, :])
        for nb in range(2):
            cs = slice(nb * 512, (nb + 1) * 512)
            k.stt('dve', yn[b][:, cs], py[b][nb][:, :], rstd[b][:, 0:1], self.Gbc[s][:, cs], ALU.mult, ALU.mult)
        k.tt('pool', yn[b][:, :], yn[b][:, :], hb[b][:, :], ALU.add)
        if last:
            dst = self.out[(t - NTC) * 128:(t - NTC + 1) * 128, :]
        else:
            dst = self.hs[l % 2][tok, :]
        k.dma('pool', dst, yn[b][:, :])
    k.barrier_all()
    es.close()


Prog.phase_out = phase_out


def gen_rwkv(self, l, ntile=None, NBX=2):
    k, nc = self.k, self.nc
    es = ExitStack()
    kk = KBScope(k, es)
    W = self.w
    ntile = NT if ntile is None else ntile
    def bc(name, idx, n):
        tl = kk.sb([128, n], F32, 'bc_' + name)
        k.dma('sp', tl[:, :], W[name].wrap(idx.partition_broadcast(128)))
        return tl
    mu_bc = bc('rw_mu', W['rw_mu'].t[l, :], 1024)
    w0_bc = [bc('rw_w0', W['rw_w0'].t[l, d, :], 256) for d in range(2)]
    a0_bc = [bc('rw_a0', W['rw_a0'].t[l, d, :], 256) for d in range(2)]
    kk_bc = bc('rw_kk', W['rw_kk'].t[l, :], 256)
    ka_bc = bc('rw_ka', W['rw_ka'].t[l, :], 256)
    rk_bc = bc('rw_rk', W['rw_rk'].t[l].rearrange("a b -> (a b)"), 256)
    lnw_bc = bc('rw_ln_w', W['rw_ln_w'].t[l, :], 256)
    lnb_bc = bc('rw_ln_b', W['rw_ln_b'].t[l, :], 256)
    w2 = kk.sb([64, 2, 256], F32, 'w2'); a2 = kk.sb([64, 2, 256], F32, 'a2')
    for d in range(2):
        k.dma('sp', w2[:, d, :], W['rw_w2'][l, d, :, :])
        k.dma('sp', a2[:, d, :], W['rw_a2'][l, d, :, :])
    yfD = self.yfD
    bsum = kk.sb([128, NT, 4], F32, 'bsum')
    Sf = [kk.sb([128, 128], F32, 'Sf') for _ in range(2)]
    import os
    RDT0 = F32 if os.environ.get('RW_gA', '16') == '32' else BF16
    SBDT = F32 if os.environ.get('RW_gR', '16') == '32' else BF16
    Sb = [kk.sb([128, 128], SBDT, 'Sb') for _ in range(2)]
    epsb = kk.sb([128, 1], F32, 'epsb'); k.memset('dve', epsb[:, :], EPS)
    gneps = kk.sb([128, 1], F32, 'gneps'); k.memset('dve', gneps[:, :], 64e-5)
    import os
    RDT = F32 if os.environ.get('RW_F32', '1') == '1' else BF16
    NB = NBX
    RWDEF = {'gA': '16', 'gD': '32', 'gR': '16'}
    def mk(shape, dt=F32, name='b'):
        if dt == BF16 and name != 'keepbf':
            dt = F32 if os.environ.get('RW_' + name, RWDEF.get(name, '32')) == '32' else BF16
        return [kk.sb(shape, dt, name) for _ in range(NB)]
    def mk1(shape, dt=F32, name='b1'):
        t_ = kk.sb(shape, dt, name)
        return [t_ for _ in range(NB)]
    z = mk([128, 1024]); zs = mk([128, 1024]); zm = mk([128, 1024]); gate = mk([128, 256])
    for b in range(NB):
        k.memset('pool', zs[b][:, :], 0.0)
    lT = mk([64, 2, 128]); logw = mk([128, 256]); av = mk([128, 256]); kkn = mk([128, 256]); t256 = mk([128, 256])
    ssq = mk([128, 4]); kd = mk([128, 256]); bv = mk([128, 256]); Pp = mk([128, 256]); Pinv = mk([128, 256])
    Pex = mk([128, 256]); Dk = mk([128, 256]); pc = mk([128, 2])
    TMb = mk([128, 4, 256], BF16, 'gA')
    Bh = mk([128, 256], BF16, 'gR'); Kh = mk([128, 256], BF16, 'gR'); vbf = mk([128, 256], BF16, 'gR')
    FMt = mk([128, 2, 6, 128], BF16, 'gA')
    Pm = mk([128, 4, 128], BF16, 'gD'); PTm = mk([128, 4, 128], BF16, 'gD'); acc = mk([128, 4, 128], BF16, 'gD')
    Pm2 = mk([128, 4, 128], BF16, 'gD'); PTm2 = mk([128, 4, 128], BF16, 'gD')
    AakT = mk([128, 4, 128], BF16, 'gR'); ArbT = mk([128, 4, 128], BF16, 'gR'); ArkT = mk([128, 4, 128], BF16, 'gR')
    AG = mk([128, 4, 128], BF16, 'gD'); Wm = mk([128, 256], BF16, 'gR'); U0 = mk([128, 256], BF16, 'gR'); QT = mk([128, 4, 128], BF16, 'gR')
    Mf = mk([128, 2, 128], F32)
    ystg = mk([128, 2, 128], BF16, 'keepbf')
    ysb = mk([128, 256], BF16, 'keepbf')
    ysum = mk([128, 256]); st4 = mk([128, 4]); yc = mk([128, 256]); sg = mk([128, 256]); yo = mk([128, 256], BF16, 'keepbf')
    NBK = 3 if NBX <= 2 else 2
    banks = [kk.ps([128, 4, 128], F32, 'bk') for _ in range(NB * NBK)]
    pbT = kk.ps([128, 8, 128], BF16, 'pbT')
    pfT = [kk.ps([128, 4, 128], F32, 'pfT') for _ in range(2)] if RDT0 == F32 else None
    bi = [0] * NBX
    assert RDT0 == BF16
    def bankb(b):
        bi[b] += 1
        return banks[NBK * b + bi[b] % NBK]
    bm2 = self.C('bm2'); ident = self.ident; identb = self.identb[:, :]
    F32R = mybir.dt.float32r
    USE_R = os.environ.get('RW_F32R', '0') == '1'
    def rr(a):
        return A(a.ap.bitcast(F32R), a.tok) if USE_R else a
    def bc4(a, n=64):
        return A(a.ap.unsqueeze(2).to_broadcast([128, 4, n]), a.tok)
    def v4(a):
        return A(a.ap.rearrange("p (h k) -> p h k", h=4), a.tok)
    def bch(m):
        return A(m.ap.unsqueeze(1).to_broadcast([128, 4, 128]), m.tok)
    it = 0
    yield
    for d in range(2):
        order = [t for t in _chain_order(d) if t < ntile]
        TRI = self.C('LE') if d == 0 else self.C('GE')
        STRI = self.C('LT') if d == 0 else self.C('GT')
        STRI_T = self.C('GT') if d == 0 else self.C('LT')
        for c in range(2):
            k.memset('dve', Sf[c][:, :], 0.0)
            k.memset('dve', Sb[c][:, :], 0.0)
        def body(t, b):
            def bank():
                return bankb(b)
            tok = slice(t * 128, (t + 1) * 128)
            isctx = t < NTC
            lo, hi = (0, CTXL) if isctx else (CTXL, T)
            a0_ = t * 128
            k.dma('sp', z[b][:, :], self.projTM[tok, 0:1024])
            if d == 1:
                k.dma('sp', gate[b][:, :], self.projTM[tok, 1024:1280])
                k.dma('sp', ysb[b][:, :], yfD.at(t, (t, slice(None), slice(None))))
            def shl(cols, sh):
                r0_, r1_ = a0_ + sh, a0_ + sh + 128
                p0 = max(0, lo - r0_); p1 = 128 - max(0, r1_ - hi)
                if p0 > 0 or p1 < 128:
                    if abs(sh) == 1:
                        pass
                    if p0 > 0:
                        k.memset('pool', zs[b][0:64 if p0 == 64 else 32, cols], 0.0)
                    if p1 < 128:
                        k.memset('pool', zs[b][64:128, cols] if p1 == 64 else zs[b][96:128, cols], 0.0)
                k.dma('sp', zs[b][p0:p1, cols], self.projTM[r0_ + p0:r0_ + p1, cols])
            if isctx:
                shl(slice(0, 512), -1)
                shl(slice(512, 1024), 1)
                segs = [(slice(0, 512), None), (slice(512, 1024), None)]
            else:
                shl(slice(0, 256), -1)
                shl(slice(256, 512), 1)
                shl(slice(512, 768), -64)
                shl(slice(768, 1024), 64)
                segs = [(slice(0, 256), 0), (slice(256, 512), 1), (slice(512, 768), None), (slice(768, 1024), None)]
            for cols, mi in segs:
                if mi is None:
                    k.tt('dve', zm[b][:, cols], zs[b][:, cols], z[b][:, cols], ALU.subtract)
                else:
                    k.stt('dve', zm[b][:, cols], zs[b][:, cols], self.C('mL', mi, mi + 1), z[b][:, cols], ALU.mult, ALU.subtract)
            k.tt('pool', zm[b][:, :], zm[b][:, :], mu_bc[:, :], ALU.mult)
            k.tt('pool', zm[b][:, :], zm[b][:, :], z[b][:, :], ALU.add)
            r_ = zm[b][:, 0:256]; k_ = zm[b][:, 256:512]; v_ = zm[b][:, 512:768]
            yield
            pl = bank()
            k.tr(pl[0:64, 0, :], zm[b][:, 768 + 64 * d:768 + 64 * d + 64], ident)
            k.tr(pl[0:64, 1, :], zm[b][:, 896 + 64 * d:896 + 64 * d + 64], ident)
            k.act(lT[b][:, 0, :], pl[0:64, 0, :], AF.Tanh)
            k.cp('act', lT[b][:, 1, :], pl[0:64, 1, :])
            pw_ = bank()
            pwv = pw_.wrap(pw_.t[:, :, :].rearrange("p a b -> p (a b)"))
            k.mm(pw_.wrap(pw_.t[:, 0:2, :].rearrange("p a b -> p (a b)")), lT[b][:, 0, :], w2[:, d, :])
            k.mm(pw_.wrap(pw_.t[:, 2:4, :].rearrange("p a b -> p (a b)")), lT[b][:, 1, :], a2[:, d, :])
            k.tt('dve', logw[b][:, :], pw_.wrap(pw_.t[:, 0:2, :].rearrange("p a b -> p (a b)")), w0_bc[d][:, :], ALU.add)
            k.act(logw[b][:, :], logw[b][:, :], AF.Sigmoid)
            k.amul(logw[b][:, :], logw[b][:, :], -0.6065306597126334)
            k.tt('dve', av[b][:, :], pw_.wrap(pw_.t[:, 2:4, :].rearrange("p a b -> p (a b)")), a0_bc[d][:, :], ALU.add)
            k.act(av[b][:, :], av[b][:, :], AF.Sigmoid)
            yield
            k.tt('dve', kkn[b][:, :], k_, kk_bc[:, :], ALU.mult)
            k.tt('pool', t256[b][:, :], kkn[b][:, :], kkn[b][:, :], ALU.mult)
            k.reduce('dve', ssq[b][:, :], v4(t256[b][:, :]), ALU.add)
            k.act(ssq[b][:, :], ssq[b][:, :], AF.Sqrt, bias=epsb[:, :])
            k.recip(ssq[b][:, :], ssq[b][:, :])
            k.tt('dve', v4(kkn[b][:, :]), v4(kkn[b][:, :]), bc4(ssq[b][:, :]), ALU.mult)
            k.stt('dve', kd[b][:, :], av[b][:, :], -1.0, ka_bc[:, :], ALU.add, ALU.mult)
            k.stt('dve', kd[b][:, :], kd[b][:, :], 1.0, k_, ALU.add, ALU.mult)
            k.tt('pool', bv[b][:, :], kkn[b][:, :], av[b][:, :], ALU.mult)
            k.tt('pool', t256[b][:, :], r_, kd[b][:, :], ALU.mult)
            k.tt('pool', t256[b][:, :], t256[b][:, :], rk_bc[:, :], ALU.mult)
            k.reduce('dve', st4[b][:, :], v4(t256[b][:, :]), ALU.add)
            if d == 0:
                k.cp('pool', bsum[:, t, :], st4[b][:, :])
            yield
            pL = bank()
            pLv = pL.wrap(pL.t[:, 0:2, :].rearrange("p a b -> p (a b)"))
            pDv = pL.wrap(pL.t[:, 2:4, :].rearrange("p a b -> p (a b)"))
            k.mm(pLv, TRI, logw[b][:, :])
            k.mm(pDv, STRI_T, logw[b][:, :])
            pc_ps = bank()
            for c in range(2):
                k.mm(pc_ps[:, 0, c:c + 1], logw[b][:, 128 * c:128 * c + 128], self.C('ones', 0, 1))
            k.act(Pp[b][:, :], pLv, AF.Exp)
            k.act(Pinv[b][:, :], pLv, AF.Exp, scale=-1.0)
            k.tt('dve', Pex[b][:, :], pLv, logw[b][:, :], ALU.subtract)
            k.act(Pex[b][:, :], Pex[b][:, :], AF.Exp)
            k.act(Dk[b][:, :], pDv, AF.Exp)
            k.act(pc[b][:, :], pc_ps[:, 0, 0:2], AF.Exp)
            yield
            k.stt('dve', TMb[b][:, 0, :], kkn[b][:, :], -1.0, Pex[b][:, :], ALU.mult, ALU.mult)
            k.tt('dve', TMb[b][:, 1, :], bv[b][:, :], Pinv[b][:, :], ALU.mult)
            k.tt('pool', TMb[b][:, 2, :], kd[b][:, :], Pinv[b][:, :], ALU.mult)
            k.tt('pool', TMb[b][:, 3, :], r_, Pp[b][:, :], ALU.mult)
            k.tt('dve', Bh[b][:, :], bv[b][:, :], Dk[b][:, :], ALU.mult)
            k.tt('pool', Kh[b][:, :], kd[b][:, :], Dk[b][:, :], ALU.mult)
            k.cp('pool', vbf[b][:, :], v_)
            yield
            for c in range(2):
                if RDT0 == F32:
                    src_ = pfT[c]; o_ = 0; idn = ident
                else:
                    src_ = pbT; o_ = 4 * c; idn = identb
                for q in range(4):
                    k.tr(src_[:, o_ + q, :], TMb[b][:, q, 128 * c:128 * c + 128], idn)
                k.cp('act', FMt[b][:, c, 0, :], src_[:, o_ + 0, :])
                k.cp('act', FMt[b][:, c, 1, :], src_[:, o_ + 3, :])
                for hl in range(2):
                    k.ts('dve', FMt[b][:, c, 2 + hl, :], src_[:, o_ + 1, :], self.C('hm2', hl, hl + 1), ALU.mult)
                    k.ts('dve', FMt[b][:, c, 4 + hl, :], src_[:, o_ + 2, :], self.C('hm2', hl, hl + 1), ALU.mult)
            yield
            def amat(lq, rq, lz, rz):
                p = bank()
                for h in range(4):
                    c, hl = h // 2, h % 2
                    lhs = FMt[b][:, c, lq + (hl if lz else 0), :]
                    rhs = FMt[b][:, c, rq + (hl if rz else 0), :]
                    k.mm(p[:, h, :], lhs, rhs)
                return p
            def amat2(lq):
                pbs = [bank(), bank()]
                for h in range(4):
                    c, hl = h // 2, h % 2
                    rhs2 = FMt[b].wrap(FMt[b].t[:, c, 0:2, :].rearrange("p a b -> p (a b)"))
                    out2 = pbs[h // 2].wrap(pbs[h // 2].t[:, 2 * (h % 2):2 * (h % 2) + 2, :].rearrange("p a b -> p (a b)"))
                    k.mm(out2, FMt[b][:, c, lq + hl, :], rhs2)
                return pbs
            def bc2(m):
                return A(m.ap.unsqueeze(1).to_broadcast([128, 2, 128]), m.tok)
            pbs = amat2(2)
            for hb in range(2):
                pv = pbs[hb].wrap(pbs[hb].t[:, :, :].rearrange("p (h q) n -> p h q n", q=2))
                k.tt('dve', Pm[b][:, 2 * hb:2 * hb + 2, :], pv.tok and A(pv.ap[:, :, 0, :], pv.tok), bc2(STRI), ALU.mult)
                k.tt('dve', ArbT[b][:, 2 * hb:2 * hb + 2, :], A(pv.ap[:, :, 1, :], pv.tok), bc2(TRI), ALU.mult)
            k.tt('pool', acc[b][:, :, :], Pm[b][:, :, :], bch(ident), ALU.add)
            yield
            p = amat(0, 2, False, True)
            k.tt('dve', PTm[b][:, :, :], p[:, :, :], bch(STRI_T), ALU.mult)
            yield
            pbs = amat2(4)
            for hb in range(2):
                pv = pbs[hb].wrap(pbs[hb].t[:, :, :].rearrange("p (h q) n -> p h q n", q=2))
                k.tt('dve', AakT[b][:, 2 * hb:2 * hb + 2, :], A(pv.ap[:, :, 0, :], pv.tok), bc2(STRI), ALU.mult)
                k.tt('dve', ArkT[b][:, 2 * hb:2 * hb + 2, :], A(pv.ap[:, :, 1, :], pv.tok), bc2(TRI), ALU.mult)
            P_, PT_, P2_, PT2_ = Pm[b], PTm[b], Pm2[b], PTm2[b]
            for q in range(1, 7):
                pa_ = bank()
                for h in range(4):
                    k.mm(pa_[:, h, :], rr(P_[:, h, :]), rr(PT_[:, h, :]))
                k.cp('act', PT2_[:, :, :], pa_[:, :, :])
                if q < 6:
                    pb_ = bank()
                    for h in range(4):
                        k.mm(pb_[:, h, :], rr(PT_[:, h, :]), rr(P_[:, h, :]))
                    k.cp('act', P2_[:, :, :], pb_[:, :, :])
                pc_ = bank()
                for h in range(4):
                    k.mm(pc_[:, h, :], rr(PT2_[:, h, :]), rr(acc[b][:, h, :]))
                k.tt('dve', acc[b][:, :, :], pc_[:, :, :], acc[b][:, :, :], ALU.add)
                P_, P2_ = P2_, P_
                yield
                PT_, PT2_ = PT2_, PT_
            TT = acc[b]
            yield
            pg = bank()
            for h in range(4):
                k.mm(pg[:, h, 0:64], AakT[b][:, h, :], vbf[b][:, 64 * h:64 * h + 64])
            for h in range(4):
                k.cp('pool', AG[b][:, h, 0:64], TMb[b][:, 0, 64 * h:64 * h + 64])
            k.cp('act', AG[b][:, :, 64:128], pg[:, :, 0:64])
            pwu = bank()
            for h in range(4):
                k.mm(pwu[:, h, :], rr(TT[:, h, :]), rr(AG[b][:, h, :]))
            k.cp('act', v4(Wm[b][:, :]), pwu[:, :, 0:64])
            k.cp('act', v4(U0[b][:, :]), pwu[:, :, 64:128])
            yield
            pq = bank()
            for h in range(4):
                c = h // 2
                k.mm(pq[:, h, :], Wm[b][:, 128 * c:128 * c + 128], ArbT[b][:, h, :])
            for h in range(4):
                c = h // 2
                k.tt('dve', QT[b][:, h, :], pq[:, h, :], FMt[b][:, c, 1, :], ALU.add)
            yield 'STATE'
            py = bank()
            for h in range(4):
                c, hl = h // 2, h % 2
                k.mm(py[:, h, 0:64], QT[b][:, h, :], Sb[c][:, 64 * hl:64 * hl + 64], start=True, stop=False)
                k.mm(py[:, h, 0:64], ArbT[b][:, h, :], U0[b][:, 64 * h:64 * h + 64], start=False, stop=False)
                k.mm(py[:, h, 0:64], ArkT[b][:, h, :], vbf[b][:, 64 * h:64 * h + 64], start=False, stop=True)
            if d == 0:
                k.cp('act', v4(ysb[b][:, :]), py[:, :, 0:64])
                k.dma('act', yfD.at(t, (t, slice(None), slice(None))), ysb[b][:, :])
            else:
                k.tt('dve', v4(ysum[b][:, :]), py[:, :, 0:64], v4(ysb[b][:, :]), ALU.add)
                k.reduce('dve', ssq[b][:, :], v4(ysum[b][:, :]), ALU.add)
                k.ts('dve', ssq[b][:, :], ssq[b][:, :], 1.0 / 64, ALU.mult)
                k.tt('dve', v4(yc[b][:, :]), v4(ysum[b][:, :]), bc4(ssq[b][:, :]), ALU.subtract)
                k.tt('pool', t256[b][:, :], yc[b][:, :], yc[b][:, :], ALU.mult)
                k.reduce('dve', ssq[b][:, :], v4(t256[b][:, :]), ALU.add)
                k.act(ssq[b][:, :], ssq[b][:, :], AF.Sqrt, bias=gneps[:, :], scale=1.0 / 64)
                k.recip(ssq[b][:, :], ssq[b][:, :])
                k.tt('dve', v4(yc[b][:, :]), v4(yc[b][:, :]), bc4(ssq[b][:, :]), ALU.mult)
                k.tt('pool', yc[b][:, :], yc[b][:, :], lnw_bc[:, :], ALU.mult)
                k.tt('pool', yc[b][:, :], yc[b][:, :], lnb_bc[:, :], ALU.add)
                k.tt('dve', st4[b][:, :], st4[b][:, :], bsum[:, t, :], ALU.add)
                k.tt('dve', v4(t256[b][:, :]), v4(v_), bc4(st4[b][:, :]), ALU.mult)
                k.tt('pool', yc[b][:, :], yc[b][:, :], t256[b][:, :], ALU.add)
                k.act(sg[b][:, :], gate[b][:, :], AF.Silu)
                k.tt('dve', yo[b][:, :], yc[b][:, :], sg[b][:, :], ALU.mult)
                for c in range(2):
                    k.tr(pbT[:, c, :], yo[b][:, c * 128:(c + 1) * 128], identb)
                k.cp('act', ystg[b][:, :, :], pbT[:, 0:2, :])
                k.dma('act', self.ycatD.at('rwkv', (slice(None), slice(0, 2), tok)), ystg[b][:, :, :])
            yield
            for c in range(2):
                pm_ = bank()
                k.mm(pm_[:, 0, :], Wm[b][:, 128 * c:128 * c + 128], Bh[b][:, 128 * c:128 * c + 128])
                k.tt('dve', Mf[b][:, c, :], pm_[:, 0, :], bm2, ALU.mult)
                k.stt('dve', Mf[b][:, c, :], ident, pc[b][:, c:c + 1], Mf[b][:, c, :], ALU.mult, ALU.add)
                ps_ = bank()
                k.mm(ps_[:, 0, :], Mf[b][:, c, :], Sf[c][:, :], start=True, stop=False)
                k.mm(ps_[:, 0, :], Bh[b][:, 128 * c:128 * c + 128], U0[b][:, 128 * c:128 * c + 128], start=False, stop=False)
                k.mm(ps_[:, 0, :], Kh[b][:, 128 * c:128 * c + 128], vbf[b][:, 128 * c:128 * c + 128], start=False, stop=True)
                k.tt('dve', Sf[c][:, :], ps_[:, 0, :], bm2, ALU.mult)
                k.cp('act', Sb[c][:, :], Sf[c][:, :])
        if NB == 1:
            for t_ in order:
                for _r in body(t_, 0):
                    yield
        else:
            for i_ in range(0, len(order), NB):
                run_interleaved([body(t_, b_) for b_, t_ in enumerate(order[i_:i_ + NB])])
                yield
    yield ('DONE', es)


def phase_rwkv(self, l, ntile=None):
    for r in gen_rwkv(self, l, ntile):
        if isinstance(r, tuple):
            self.k.barrier_all()
            r[1].close()


def phase_rwkv_s5(self, l, ntile=None, ratio=None):
    ratio = int(os.environ.get('RATIO', '40')) if ratio is None else ratio
    g1 = gen_rwkv(self, l, None, NBX=1)
    next(g1)
    g2 = gen_s5(self, l)
    d1 = d2 = None
    while d1 is None or d2 is None:
        if d1 is None:
            for _ in range(ratio):
                r = next(g1)
                if isinstance(r, tuple):
                    d1 = r[1]
                    break
        if d2 is None:
            for _ in range(int(os.environ.get('S5STEP', '6'))):
                r = next(g2)
                if isinstance(r, tuple):
                    d2 = r[1]
                    break
    self.k.barrier_all()
    d2.close()
    d1.close()


Prog.phase_rwkv_s5 = phase_rwkv_s5
Prog.phase_rwkv = phase_rwkv


def build_program(nc, es, layers=DEPTH):
    P = Prog(nc, es, dbg=set())
    for l in range(layers):
        P.phase_mod(l)
        P.phase_proj(l)
        P.phase_rwkv(l)
        P.phase_s5(l)
        P.phase_ssd(l)
        P.phase_gla(l)
        P.phase_out(l)
    P.k.barrier_all()
    return P


def make_in_maps(inputs, cores):
    maps = []
    for b in cores:
        m = {'x': np.ascontiguousarray(inputs['x'][b], dtype=np.float32),
             'ctx': np.ascontiguousarray(inputs['ctx'][b], dtype=np.float32),
             'c2': np.ascontiguousarray(np.stack([inputs['c'][b], inputs['c_ctx']]), dtype=np.float32),
             'consts': CONSTS, 'selc': SELC}
        for n in WNAMES:
            m[n] = np.ascontiguousarray(inputs[n], dtype=np.float32)
        maps.append(m)
    return maps


def kernel(**inputs):
    inputs = {k_: np.asarray(v_) for k_, v_ in inputs.items()}
    nc = bass.Bass("TRN2", target_bir_lowering=False)
    with ExitStack() as es:
        build_program(nc, es)
    n = 8
    res = run_bass_kernel_spmd(nc, make_in_maps(inputs, list(range(n))), core_ids=list(range(n)))
    out = np.stack([np.asarray(res.results[i]['out'], dtype=np.float32) for i in range(n)], axis=0)
    return out


def phase_rwkv1(self, l, ntile=None):
    for r in gen_rwkv(self, l, ntile, NBX=1):
        if isinstance(r, tuple):
            self.k.barrier_all()
            r[1].close()


Prog.phase_rwkv1 = phase_rwkv1


def phase_s5hi(self, l, ntile=None):
    es = ExitStack()
    kk = KBScope(self.k, es)
    dummy = [kk.ps([128, 512], F32, 'dummy') for _ in range(4)]
    phase_s5(self, l, ntile)
    es.close()


Prog.phase_s5hi = phase_s5hi


def phase_rwkv1hi(self, l, ntile=None):
    es = ExitStack()
    kk = KBScope(self.k, es)
    dummy = [kk.ps([128, 512], F32, 'dummy') for _ in range(4)]
    phase_rwkv1(self, l, ntile)
    es.close()


Prog.phase_rwkv1hi = phase_rwkv1hi
```
